# Optimizing a Trainium2 kernel written in Bass

```python
import math
import jax, jax.numpy as jnp
from jax import lax
import numpy as np

D_MODEL = 1024
BATCH = 16
SEQ = 2048
DEPTH = 1

N_MEM = 256
FOX_WIDTH = D_MODEL // 2
HEAD_DIM = 64
N_FOX_HEADS = FOX_WIDTH // HEAD_DIM
Q_BLOCK = 128
S5_WIDTH = D_MODEL - FOX_WIDTH
S5_GROUP_CH = 16
S5_GROUPS = S5_WIDTH // S5_GROUP_CH
S5_STATE = 64
N_X_HEADS = 4
X_HEAD_DIM = D_MODEL // N_X_HEADS
D_FF = 128 * ((8 * D_MODEL // 3 + 127) // 128)
CONV_W = 3
IN_COLS = 3 * FOX_WIDTH + N_FOX_HEADS + S5_WIDTH
EPS = 1e-6

kernel_name = "fox_s5_parallel_hybrid_layer"


def _rms_norm(x, g):
    xf = x.astype(jnp.float32)
    y = xf * lax.rsqrt(jnp.mean(xf * xf, axis=-1, keepdims=True) + EPS)
    return (y * g.astype(jnp.float32)).astype(x.dtype)


def _fox_attention(q, k, v, log_f):
    L = q.shape[2]
    c = jnp.cumsum(log_f, axis=-1)
    scale = HEAD_DIM ** -0.5
    outs = []
    for blk in range(L // Q_BLOCK):
        qs = blk * Q_BLOCK
        ke = qs + Q_BLOCK
        s = jnp.einsum('bhqd,bhkd->bhqk', q[:, :, qs:ke], k[:, :, :ke]).astype(jnp.float32) * scale
        s = s + c[:, :, qs:ke, None] - c[:, :, None, :ke]
        mask = jnp.arange(ke)[None, :] <= (qs + jnp.arange(Q_BLOCK))[:, None]
        s = jnp.where(mask, s, -jnp.inf)
        p = jax.nn.softmax(s, axis=-1).astype(v.dtype)
        outs.append(jnp.einsum('bhqk,bhkd->bhqd', p, v[:, :, :ke]))
    return jnp.concatenate(outs, axis=2)


def _cdiag_combine(left, right):
    a1r, a1i, b1r, b1i = left
    a2r, a2i, b2r, b2i = right
    ar = a2r * a1r - a2i * a1i
    ai = a2r * a1i + a2i * a1r
    br = a2r * b1r - a2i * b1i + b2r
    bi = a2r * b1i + a2i * b1r + b2i
    return (ar, ai, br, bi)


def _s5(u, a_re, a_im, log_dt, b_re, b_im, c_re, c_im, d):
    Bsz, L, _ = u.shape
    uf = u.astype(jnp.float32).reshape(Bsz, L, S5_GROUPS, S5_GROUP_CH)
    ar = a_re.astype(jnp.float32)
    ai = a_im.astype(jnp.float32)
    dt = jnp.exp(log_dt.astype(jnp.float32))[:, None]
    mag = jnp.exp(ar * dt)
    lb_r = mag * jnp.cos(ai * dt)
    lb_i = mag * jnp.sin(ai * dt)
    den = ar * ar + ai * ai
    nr = lb_r - 1.0
    coef_r = (nr * ar + lb_i * ai) / den
    coef_i = (lb_i * ar - nr * ai) / den
    br = b_re.astype(jnp.float32)
    bi = b_im.astype(jnp.float32)
    bb_r = coef_r[:, :, None] * br - coef_i[:, :, None] * bi
    bb_i = coef_r[:, :, None] * bi + coef_i[:, :, None] * br
    bu_r = jnp.einsum('blgc,gpc->lbgp', uf, bb_r)
    bu_i = jnp.einsum('blgc,gpc->lbgp', uf, bb_i)
    a_r = jnp.broadcast_to(lb_r[None, None], (L, 1, S5_GROUPS, S5_STATE))
    a_i = jnp.broadcast_to(lb_i[None, None], (L, 1, S5_GROUPS, S5_STATE))
    _, _, xr, xi = lax.associative_scan(_cdiag_combine, (a_r, a_i, bu_r, bu_i), axis=0)
    y = (jnp.einsum('lbgp,gcp->blgc', xr, c_re.astype(jnp.float32))
         - jnp.einsum('lbgp,gcp->blgc', xi, c_im.astype(jnp.float32))
         + d.astype(jnp.float32) * uf)
    return y.reshape(Bsz, L, S5_WIDTH)


def _causal_dwconv(a, w, b):
    L = a.shape[1]
    ap = jnp.pad(a, ((0, 0), (CONV_W - 1, 0), (0, 0)))
    out = b
    for i in range(CONV_W):
        out = out + w[i] * ap[:, i:i + L]
    return out


def setup_inputs(seed: int = 0) -> dict:
    key = jax.random.key(seed)
    ks = jax.random.split(key, 40)
    f32 = jnp.float32

    def nrm(k, shape, scale):
        return jax.random.normal(k, shape, f32) * scale

    def gain(k, shape):
        return 1.0 + 0.02 * jax.random.normal(k, shape, f32)

    Ld = DEPTH
    n_idx = jnp.arange(S5_STATE, dtype=f32)
    inp = {
        "x": jax.random.normal(ks[0], (BATCH, SEQ, D_MODEL), f32),
        "mem": jax.random.normal(ks[1], (BATCH, N_MEM, D_MODEL), f32),
        "norm_mix": gain(ks[2], (Ld, D_MODEL)),
        "w_in": nrm(ks[3], (Ld, D_MODEL, IN_COLS), D_MODEL ** -0.5),
        "fox_q_norm": gain(ks[4], (Ld, HEAD_DIM)),
        "fox_k_norm": gain(ks[5], (Ld, HEAD_DIM)),
        "fox_f_bias": 3.0 + 0.5 * jax.random.normal(ks[6], (Ld, N_FOX_HEADS), f32),
        "s5_a_re": -0.5 + 0.01 * jax.random.normal(ks[7], (Ld, S5_GROUPS, S5_STATE), f32),
        "s5_a_im": math.pi * n_idx[None, None, :] + 0.01 * jax.random.normal(ks[8], (Ld, S5_GROUPS, S5_STATE), f32),
        "s5_log_dt": jax.random.uniform(ks[9], (Ld, S5_GROUPS), f32, math.log(1e-3), math.log(1e-1)),
        "s5_b_re": nrm(ks[10], (Ld, S5_GROUPS, S5_STATE, S5_GROUP_CH), (2 * S5_GROUP_CH) ** -0.5),
        "s5_b_im": nrm(ks[11], (Ld, S5_GROUPS, S5_STATE, S5_GROUP_CH), (2 * S5_GROUP_CH) ** -0.5),
        "s5_c_re": nrm(ks[12], (Ld, S5_GROUPS, S5_GROUP_CH, S5_STATE), (2 * S5_STATE) ** -0.5),
        "s5_c_im": nrm(ks[13], (Ld, S5_GROUPS, S5_GROUP_CH, S5_STATE), (2 * S5_STATE) ** -0.5),
        "s5_d": nrm(ks[14], (Ld, S5_GROUPS, S5_GROUP_CH), 1.0),
        "s5_w_glu": nrm(ks[15], (Ld, S5_WIDTH, S5_WIDTH), S5_WIDTH ** -0.5),
        "s5_b_glu": nrm(ks[16], (Ld, S5_WIDTH), 0.02),
        "out_norm_fox": gain(ks[17], (Ld, FOX_WIDTH)),
        "out_norm_s5": gain(ks[18], (Ld, S5_WIDTH)),
        "w_out": nrm(ks[19], (Ld, D_MODEL, D_MODEL), D_MODEL ** -0.5),
        "norm_cross": gain(ks[20], (Ld, D_MODEL)),
        "norm_mem": gain(ks[21], (Ld, D_MODEL)),
        "w_xq": nrm(ks[22], (Ld, D_MODEL, D_MODEL), D_MODEL ** -0.5),
        "w_xkv": nrm(ks[23], (Ld, D_MODEL, 2 * D_MODEL), D_MODEL ** -0.5),
        "xq_norm": gain(ks[24], (Ld, X_HEAD_DIM)),
        "xk_norm": gain(ks[25], (Ld, X_HEAD_DIM)),
        "w_xo": nrm(ks[26], (Ld, D_MODEL, D_MODEL), D_MODEL ** -0.5),
        "norm_ffn": gain(ks[27], (Ld, D_MODEL)),
        "w_ffn_up": nrm(ks[28], (Ld, D_MODEL, 2 * D_FF), D_MODEL ** -0.5),
        "ffn_conv_w": nrm(ks[29], (Ld, CONV_W, D_FF), CONV_W ** -0.5),
        "ffn_conv_b": nrm(ks[30], (Ld, D_FF), 0.02),
        "w_ffn_down": nrm(ks[31], (Ld, D_FF, D_MODEL), D_FF ** -0.5),
    }
    return inp


def reference(x, mem, norm_mix, w_in, fox_q_norm, fox_k_norm, fox_f_bias,
              s5_a_re, s5_a_im, s5_log_dt, s5_b_re, s5_b_im, s5_c_re, s5_c_im,
              s5_d, s5_w_glu, s5_b_glu, out_norm_fox, out_norm_s5, w_out,
              norm_cross, norm_mem, w_xq, w_xkv, xq_norm, xk_norm, w_xo,
              norm_ffn, w_ffn_up, ffn_conv_w, ffn_conv_b, w_ffn_down):
    Bsz, L, _ = x.shape
    h = x
    for l in range(DEPTH):
        hn = _rms_norm(h, norm_mix[l])
        proj = hn @ w_in[l]
        q, k, v, f_logit, u = jnp.split(
            proj, [FOX_WIDTH, 2 * FOX_WIDTH, 3 * FOX_WIDTH, 3 * FOX_WIDTH + N_FOX_HEADS], axis=-1)
        q = _rms_norm(q.reshape(Bsz, L, N_FOX_HEADS, HEAD_DIM), fox_q_norm[l]).transpose(0, 2, 1, 3)
        k = _rms_norm(k.reshape(Bsz, L, N_FOX_HEADS, HEAD_DIM), fox_k_norm[l]).transpose(0, 2, 1, 3)
        v = v.reshape(Bsz, L, N_FOX_HEADS, HEAD_DIM).transpose(0, 2, 1, 3)
        log_f = jax.nn.log_sigmoid(f_logit.astype(jnp.float32) + fox_f_bias[l].astype(jnp.float32))
        fox = _fox_attention(q, k, v, log_f.transpose(0, 2, 1))
        fox = fox.transpose(0, 2, 1, 3).reshape(Bsz, L, FOX_WIDTH)

        y = _s5(u, s5_a_re[l], s5_a_im[l], s5_log_dt[l], s5_b_re[l], s5_b_im[l],
                s5_c_re[l], s5_c_im[l], s5_d[l])
        y = jax.nn.gelu(y)
        y = y * jax.nn.sigmoid(y @ s5_w_glu[l].astype(jnp.float32) + s5_b_glu[l].astype(jnp.float32))
        y = y.astype(h.dtype)

        mixed = jnp.concatenate([_rms_norm(fox, out_norm_fox[l]), _rms_norm(y, out_norm_s5[l])], axis=-1)
        h = h + mixed @ w_out[l]

        hn = _rms_norm(h, norm_cross[l])
        mn = _rms_norm(mem, norm_mem[l])
        xq = _rms_norm((hn @ w_xq[l]).reshape(Bsz, L, N_X_HEADS, X_HEAD_DIM), xq_norm[l])
        xk, xv = jnp.split(mn @ w_xkv[l], 2, axis=-1)
        xk = _rms_norm(xk.reshape(Bsz, N_MEM, N_X_HEADS, X_HEAD_DIM), xk_norm[l])
        xv = xv.reshape(Bsz, N_MEM, N_X_HEADS, X_HEAD_DIM)
        s = jnp.einsum('bqhd,bmhd->bhqm', xq, xk).astype(jnp.float32) * (X_HEAD_DIM ** -0.5)
        p = jax.nn.softmax(s, axis=-1).astype(xv.dtype)
        xo = jnp.einsum('bhqm,bmhd->bqhd', p, xv).reshape(Bsz, L, D_MODEL)
        h = h + xo @ w_xo[l]

        hn = _rms_norm(h, norm_ffn[l])
        gate, up = jnp.split(hn @ w_ffn_up[l], 2, axis=-1)
        gate = _causal_dwconv(gate, ffn_conv_w[l], ffn_conv_b[l])
        h = h + (jax.nn.silu(gate) * up) @ w_ffn_down[l]
    return h
```

```python
import math
from contextlib import ExitStack
import numpy as np
import concourse.bass as bass
import concourse.mybir as mybir
from concourse.bass_utils import run_bass_kernel_spmd

F32 = mybir.dt.float32
BF16 = mybir.dt.bfloat16
AF = mybir.ActivationFunctionType
ALU = mybir.AluOpType
AX = mybir.AxisListType

D = 1024; L = 2048; NT = 16; NB = 2; NMEM = 256; DFF = 2816; NC_FF = 22
EPS = 1e-6
ENGS = ['sp', 'pe', 'act', 'dve', 'pool']
NDS = 16
DEBUG = None


class Prog:
    def __init__(self, nc):
        self.nc = nc; self.ops = []; self.lastw = {}; self.rd = {}
        self.bar = set(); self.last_eng = {}; self.ndma = 0; self.last_slot = {}; self.gq_since = []

    def barrier(self):
        self.bar = set(self.last_eng.values()) | set(self.last_slot.values()) | set(self.gq_since)
        self.gq_since = []

    def add(self, eng, fn, r=(), w=()):
        i = len(self.ops); deps = set(self.bar)
        for k in list(r) + list(w):
            if k in self.lastw: deps.add(self.lastw[k])
        for k in w:
            rdk = self.rd.get(k)
            if rdk:
                deps.update(rdk[0].values()); deps.update(rdk[1])
        self.ops.append(dict(eng=eng, fn=fn, deps=deps, sig=False))
        for k in w:
            self.lastw[k] = i; self.rd[k] = ({}, [])
        for k in r:
            rdk = self.rd.setdefault(k, ({}, []))
            if eng in ('sp', 'gq', 'aq'): rdk[1].append(i)
            else: rdk[0][eng] = i
        if eng in ('sp', 'aq'):
            self.last_slot[self.ndma % NDS] = i; self.ndma += 1
        elif eng == 'gq':
            self.gq_since.append(i)
        else:
            self.last_eng[eng] = i
        return i

    def emit(self, es):
        nc = self.nc; ops = self.ops
        for op in ops:
            for d in op['deps']:
                if ops[d]['eng'] == 'pe' and op['eng'] == 'pe': continue
                ops[d]['sig'] = True
        cnt = {e: 0 for e in ENGS}; di = 0; qi = 0
        for op in ops:
            e = op['eng']
            if e in ('sp', 'aq'):
                op['dsem'] = di % NDS; op['dval'] = 16 * (di // NDS + 1); di += 1
            elif e == 'gq':
                op['dsem'] = NDS + qi; op['dval'] = 16; qi += 1
            elif op['sig']:
                cnt[e] += 1; op['ord'] = cnt[e]
        esem = {e: es.enter_context(nc.semaphore("s_" + e)) for e in ENGS if e != 'sp'}
        dsem = [es.enter_context(nc.semaphore("d_%d" % i)) for i in range(NDS + qi)]
        dfinal = [0] * (NDS + qi)
        for op in ops:
            if op['eng'] in ('sp', 'gq', 'aq'): dfinal[op['dsem']] = op['dval']

        def run(e, eng):
            waited = {}

            def wait(key, sem, val):
                if waited.get(key, 0) >= val: return
                eng.wait_ge(sem, val); waited[key] = val
            for op in ops:
                oe = op['eng']
                if {'gq': 'pool', 'aq': 'act'}.get(oe, oe) != e: continue
                for d in sorted(op['deps']):
                    dop = ops[d]
                    if dop['eng'] == 'pe' and oe == 'pe': continue
                    if dop['eng'] in ('sp', 'gq', 'aq'): wait(('d', dop['dsem']), dsem[dop['dsem']], dop['dval'])
                    else: wait(dop['eng'], esem[dop['eng']], dop['ord'])
                if oe in ('sp', 'gq', 'aq'):
                    if op['dval'] > 16: wait(('d', op['dsem']), dsem[op['dsem']], op['dval'] - 16)
                    op['fn']().then_inc(dsem[op['dsem']], 16)
                else:
                    ins = op['fn']()
                    if op['sig']: ins.then_inc(esem[e], 1)
            if e == 'sp':
                for i in range(len(dfinal)):
                    if dfinal[i]: wait(('d', i), dsem[i], dfinal[i])
        block = es.enter_context(nc.Block())

        @block.sync
        def _(eng): run('sp', eng)

        @block.tensor
        def _(eng): run('pe', eng)

        @block.scalar
        def _(eng): run('act', eng)

        @block.vector
        def _(eng): run('dve', eng)

        @block.gpsimd
        def _(eng): run('pool', eng)
        print("ops:", len(ops), {e: sum(1 for o in ops if o['eng'] == e) for e in ENGS + ['gq']}, "signals:", cnt)


def build():
    nc = bass.Bass("TRN2", target_bir_lowering=False)
    es = ExitStack()
    P = Prog(nc)

    def din(name, shape): return nc.dram_tensor(name, shape, F32, kind="ExternalInput").ap()
    x = din("x", [NB, L, D]); mem = din("mem", [NB, NMEM, D])
    norm_mix = din("norm_mix", [D]); w_in = din("w_in", [D, 2056])
    fox_q_norm = din("fox_q_norm", [64]); fox_k_norm = din("fox_k_norm", [64]); fox_f_bias = din("fox_f_bias", [8])
    s5_a_re = din("s5_a_re", [32, 64]); s5_a_im = din("s5_a_im", [32, 64]); s5_log_dt = din("s5_log_dt", [32])
    s5_b_re = din("s5_b_re", [32, 64, 16]); s5_b_im = din("s5_b_im", [32, 64, 16])
    s5_c_re = din("s5_c_re", [32, 16, 64]); s5_c_im = din("s5_c_im", [32, 16, 64]); s5_d = din("s5_d", [32, 16])
    s5_w_glu = din("s5_w_glu", [512, 512]); s5_b_glu = din("s5_b_glu", [512])
    out_norm_fox = din("out_norm_fox", [512]); out_norm_s5 = din("out_norm_s5", [512]); w_out = din("w_out", [D, D])
    norm_cross = din("norm_cross", [D]); norm_mem = din("norm_mem", [D]); w_xq = din("w_xq", [D, D])
    w_xkv = din("w_xkv", [D, 2 * D]); xq_norm = din("xq_norm", [256]); xk_norm = din("xk_norm", [256])
    w_xo = din("w_xo", [D, D]); norm_ffn = din("norm_ffn", [D]); w_ffn_up = din("w_ffn_up", [D, 2 * DFF])
    ffn_conv_w = din("ffn_conv_w", [3, DFF]); ffn_conv_b = din("ffn_conv_b", [DFF]); w_ffn_down = din("w_ffn_down", [DFF, D])
    out = nc.dram_tensor("out", [NB, L, D], F32, kind="ExternalOutput").ap()
    h1_d = nc.dram_tensor("h1_d", [NB, L, D], F32, kind="Internal").ap()
    h2_d = nc.dram_tensor("h2_d", [NB, L, D], F32, kind="Internal").ap()
    hid_d = nc.dram_tensor("hid_d", [NB, 8, 128, NC_FF, 256], BF16, kind="Internal").ap()
    mix_d = nc.dram_tensor("mix_d", [NB, NT, 128, 8, 128], BF16, kind="Internal").ap()

    def sb(name, shape, dt=F32): return es.enter_context(nc.sbuf_tensor(name, shape, dt))[:]

    AW = 15580
    arena = es.enter_context(nc.sbuf_tensor("arena", [128, AW], F32))
    aoff = [0]

    def areset():
        aoff[0] = 0; P.barrier()

    def al(shape, dt=F32):
        n = 1
        for v in shape[1:]: n *= v
        nw = (n + 1) // 2 if dt == BF16 else n
        nw = (nw + 7) // 8 * 8
        assert aoff[0] + nw <= AW, ("arena overflow", aoff[0], nw)
        v = arena[:, aoff[0]:aoff[0] + nw]; aoff[0] += nw
        if dt == BF16: v = v.bitcast(BF16)
        v = v[0:shape[0], 0:n]
        if len(shape) > 2:
            names = " ".join("a%d" % i for i in range(len(shape) - 1))
            v = v.rearrange("p (%s) -> p %s" % (names, names), **{"a%d" % i: shape[i + 1] for i in range(len(shape) - 1)})
        return v

    def dma(o, i, r, w, q='sp'):
        e = nc.sync if q == 'sp' else nc.scalar
        P.add(q, lambda: e.dma_start(out=o, in_=i, allow_slow_non_contiguous=True), r, w)

    def mm(o, lhsT, rhs, start, stop, r, w, skip=False, tp=None):
        if tp is None:
            P.add('pe', lambda: nc.tensor.matmul(o, lhsT=lhsT, rhs=rhs, start=start, stop=stop, skip_group_check=skip), r, w)
        else:
            P.add('pe', lambda: nc.tensor.matmul(o, lhsT=lhsT, rhs=rhs, start=start, stop=stop, skip_group_check=skip, tile_position=tp), r, w)

    def tr(o, i, ident, r, w):
        P.add('pe', lambda: nc.tensor.transpose(o, i, ident), r, w)

    def act(o, i, func, r, w, scale=None, bias=None, accum=None):
        kw = {}
        if scale is not None: kw['scale'] = scale
        if bias is not None: kw['bias'] = bias
        if accum is not None: kw['accum_out'] = accum
        P.add('act', lambda: nc.scalar.activation(out=o, in_=i, func=func, **kw), r, w)

    DEF = ['dve']

    def tt(o, a, b, op, r, w, eng=None):
        eng = eng or DEF[0]
        e = nc.vector if eng == 'dve' else nc.gpsimd
        P.add(eng, lambda: e.tensor_tensor(out=o, in0=a, in1=b, op=op), r, w)

    def ts(o, a, s1, s2, op0, op1, r, w, eng=None):
        eng = eng or DEF[0]
        e = nc.vector if eng == 'dve' else nc.gpsimd
        if op1 is None:
            P.add(eng, lambda: e.tensor_scalar(out=o, in0=a, scalar1=s1, scalar2=None, op0=op0), r, w)
        else:
            P.add(eng, lambda: e.tensor_scalar(out=o, in0=a, scalar1=s1, scalar2=s2, op0=op0, op1=op1), r, w)

    def stt(o, a, s, b, op0, op1, r, w):
        P.add('dve', lambda: nc.vector.scalar_tensor_tensor(out=o, in0=a, scalar=s, in1=b, op0=op0, op1=op1), r, w)

    def cp(o, i, r, w, eng=None):
        eng = eng or DEF[0]
        e = nc.vector if eng == 'dve' else nc.gpsimd
        P.add(eng, lambda: e.tensor_copy(out=o, in_=i), r, w)

    def recip(o, i, r, w):
        P.add('dve', lambda: nc.vector.reciprocal(out=o, in_=i), r, w)

    def red(o, i, r, w):
        P.add('dve', lambda: nc.vector.tensor_reduce(out=o, in_=i, axis=AX.X, op=ALU.add), r, w)

    def mset(o, v, w, eng=None):
        eng = eng or DEF[0]
        e = nc.vector if eng == 'dve' else nc.gpsimd
        P.add(eng, lambda: e.memset(o, v), (), w)

    def rsqrt_chain(o, ss, n, keyr, keyw, tmp):
        ts(tmp, ss, 1.0 / n, EPS, ALU.mult, ALU.add, keyr, ['rs_tmp'])
        act(tmp, tmp, AF.Ln, ['rs_tmp'], ['rs_tmp'])
        act(o, tmp, AF.Exp, ['rs_tmp'], keyw, scale=-0.5)

    ident_f = sb("ident_f", [128, 128]); ident_b = sb("ident_b", [128, 128], BF16)
    ones_f = al([128, 128]); ones_b = sb("ones_b", [128, 128], BF16)
    tri_f = al([128, 128]); tri_b = sb("tri_b", [128, 128], BF16)
    mset(ones_f, 1.0, ['ones_f'], 'pool')
    cp(ones_b, ones_f, ['ones_f'], ['ones_b'], 'pool')
    P.add('pool', lambda: nc.gpsimd.affine_select(out=ident_f, in_=ones_f, pattern=[[1, 128]], compare_op=ALU.is_equal,
                                                  fill=0.0, base=0, channel_multiplier=-1), ['ones_f'], ['ident_f'])
    P.add('pool', lambda: nc.gpsimd.affine_select(out=tri_f, in_=ones_f, pattern=[[1, 128]], compare_op=ALU.is_ge,
                                                  fill=0.0, base=0, channel_multiplier=-1), ['ones_f'], ['tri_f'])
    cp(ident_b, ident_f, ['ident_f'], ['ident_b'], 'pool')
    cp(tri_b, tri_f, ['tri_f'], ['tri_b'], 'pool')

    PS = es.enter_context(nc.psum_tensor("PS", [128, 4096], F32))[:]
    pb = [PS[:, 512 * i:512 * (i + 1)] for i in range(8)]
    def pk(i): return 'pb%d' % i

    def colload(name, src, n):
        t = sb(name, [128, n])
        dma(t, src.rearrange("(c p) -> p c", p=128), (), [name])
        return t

    def bcload(name, src, n):
        t = sb(name, [128, n])
        dma(t, src.partition_broadcast(128), (), [name])
        return t
    g_mix = colload("g_mix", norm_mix, 8); g_cross = colload("g_cross", norm_cross, 8)
    g_mem = colload("g_mem", norm_mem, 8); g_ffn = colload("g_ffn", norm_ffn, 8)
    g_s5 = colload("g_s5", out_norm_s5, 4); b_glu = colload("b_glu", s5_b_glu, 4)
    cb = colload("cb", ffn_conv_b, NC_FF)
    cw = sb("cw", [128, 3, NC_FF])
    for k in range(3):
        dma(cw[:, k, :], ffn_conv_w[k].rearrange("(c p) -> p c", p=128), (), ['cw'])
    gq_bc = bcload("gq_bc", fox_q_norm, 64); gk_bc = bcload("gk_bc", fox_k_norm, 64)
    fb_bc = bcload("fb_bc", fox_f_bias, 8); gfox_bc = bcload("gfox_bc", out_norm_fox, 512)
    gxq_bc = bcload("gxq_bc", xq_norm, 256); gxk_bc = bcload("gxk_bc", xk_norm, 256)

    def load_w(dst, dkey, src, kc, ncols, gain=None):
        for k0 in range(0, kc, 22):
            kn = min(22, kc - k0)
            d_ = dst[:, k0:k0 + kn, :]; s_ = src[k0 * 128:(k0 + kn) * 128, :].rearrange("(c p) n -> p c n", p=128)
            P.add('gq', lambda d_=d_, s_=s_: nc.gpsimd.dma_start(out=d_, in_=s_), (), [dkey])

    aT = sb("aT", [128, 8, L], BF16)
    xsc = sb("xsc", [128, D]); xsc2p = sb("xsc2p", [128, D])
    xt = [sb("xt%d" % i, [128, D]) for i in range(2)]
    sm = sb("sm", [128, 64])
    wblk = sb("wblk", [128, 8, 1024], BF16)
    wblk2 = sb("wblk2", [128, 8, 1024], BF16)
    wglu = sb("wglu", [128, 4, 512], BF16)

    def aK(i): return 'aT:%d' % i

    stat_bufs = [xsc, xsc2p]

    def stats_chain(src_tile, skey, par=0):
        xs_ = stat_bufs[par]; c0_ = 0 if par == 0 else 16
        kss = 'ss%d' % par; kr = 'rstd%d' % par; kx = 'xsc%d' % par
        mset(sm[:, c0_:c0_ + 1], 0.0, [kss])
        act(xs_, src_tile, AF.Square, [skey], [kx, kss], accum=sm[:, c0_:c0_ + 1])
        rsqrt_chain(sm[:, c0_ + 2:c0_ + 3], sm[:, c0_:c0_ + 1], D, [kss], [kr], sm[:, c0_ + 1:c0_ + 2])
        act(xs_, src_tile, AF.Copy, [skey, kr], [kx], scale=sm[:, c0_ + 2:c0_ + 3])

    def stats_T(dstT, dkey, gcol, gkey, par=0):
        xs_ = stat_bufs[par]; kx = 'xsc%d' % par
        for hf in range(2):
            bank = 6 + hf
            for j in range(4):
                kc = hf * 4 + j
                tr(pb[bank][:, j * 128:(j + 1) * 128], xs_[:, kc * 128:(kc + 1) * 128], ident_f, [kx, 'ident_f'], [pk(bank)])
            tt(dstT[:, 4 * hf:4 * hf + 4, :], pb[bank].rearrange("p (c t) -> p c t", c=4),
               gcol[:, 4 * hf:4 * hf + 4].unsqueeze(2).to_broadcast([128, 4, 128]), ALU.mult, [pk(bank), gkey], [dkey])

    def tile_stats_T(src_tile, skey, dstT, dkey, gcol, gkey, par=0):
        stats_chain(src_tile, skey, par)
        stats_T(dstT, dkey, gcol, gkey, par)

    p1_pending = []

    def p1_tile(b, i):
        s = i % 2
        dma(xt[s], x[b, i * 128:(i + 1) * 128, :], (), ['xt%d' % s])
        stats_chain(xt[s], 'xt%d' % s, par=s)
        p1_flush()
        p1_pending.append(i)

    def p1_flush():
        while p1_pending:
            j = p1_pending.pop()
            stats_T(aT[:, :, j * 128:(j + 1) * 128], aK(j), g_mix, 'g_mix', par=j % 2)

    def p1_tiles(b):
        for i in range(NT):
            p1_tile(b, i)
        p1_flush()
    DEF[0] = 'pool'
    T2 = 128
    Mlag = sb("Mlag", [128, 4, 8, 128], BF16)
    Ast32 = sb("Ast32", [128, 4, 8, 2, 128], BF16)
    Qst32 = sb("Qst32", [128, 16, 8, 2, 32], BF16)
    E8r = sb("E8r", [128, 16, T2 + 1]); E8i = sb("E8i", [128, 16, T2 + 1])
    r8 = sb("r8", [128, 16]); dcol = sb("dcol", [128, 4]); carry = sb("carry", [128, 16, 2])
    dma(dcol, s5_d.rearrange("(o g) c -> (g c) o", o=4), (), ['dcol'])
    load_w(wglu, 'wglu', s5_w_glu, 4, 512)

    are = al([128, 16]); aim = al([128, 16]); ldt = al([128, 16])
    for h in range(2):
        dma(are[h * 64:(h + 1) * 64, :], s5_a_re.rearrange("(q h) p -> h p q", h=2)[h], (), ['are'])
        dma(aim[h * 64:(h + 1) * 64, :], s5_a_im.rearrange("(q h) p -> h p q", h=2)[h], (), ['aim'])
        dma(ldt[h * 64:(h + 1) * 64, :], s5_log_dt.rearrange("(q h) -> h q", h=2)[h].partition_broadcast(64), (), ['ldt'])
    bre = al([128, 16, 16]); bim = al([128, 16, 16])
    for h in range(2):
        dma(bre[h * 64:(h + 1) * 64], s5_b_re.rearrange("(q h) p c -> h p q c", h=2)[h], (), ['bre'])
        dma(bim[h * 64:(h + 1) * 64], s5_b_im.rearrange("(q h) p c -> h p q c", h=2)[h], (), ['bim'])
    craw = [al([128, 4, 2, 64]) for k in range(2)]
    for k, src in enumerate((s5_c_re, s5_c_im)):
        for dup in range(2):
            dma(craw[k][:, :, dup, :], src.rearrange("(o g) c p -> (g c) o p", o=4), (), ['craw%d' % k])
    cre = al([128, 16, 16]); cim = al([128, 16, 16])
    for k, dst in enumerate((cre, cim)):
        for o in range(4):
            tr(pb[0][:, 0:128], craw[k][:, o].rearrange("p a b -> p (a b)"), ident_f, ['craw%d' % k, 'ident_f'], [pk(0)])
            v = pb[0][:, 0:128].rearrange("p (q h c) -> p q h c", q=4, h=2)
            for h in range(2):
                cp(dst[h * 64:(h + 1) * 64, o * 4:(o + 1) * 4, :], v[h * 64:(h + 1) * 64, :, h, :], [pk(0)], ['cre' if k == 0 else 'cim'], 'dve')
    S = al([128, 24, 16])
    def pl(i): return S[:, i, :]
    K5 = ['s5s']
    dt_ = pl(0); rho = pl(1); th = pl(2); mag = pl(3); cs = pl(4); sn = pl(5); lbr = pl(6); lbi = pl(7)
    den = pl(8); nr = pl(9); cfr = pl(10); cfi = pl(11); t1 = pl(12); t2 = pl(13); inv8 = pl(14)
    act(dt_, ldt, AF.Exp, ['ldt'], K5)
    tt(rho, are, dt_, ALU.mult, K5 + ['are'], K5)
    tt(th, aim, dt_, ALU.mult, K5 + ['aim'], K5)
    act(mag, rho, AF.Exp, K5, K5)
    act(r8, rho, AF.Exp, K5, ['r8'], scale=8.0)
    recip(inv8, r8, ['r8'], K5)
    MAGIC = 12582912.0
    TWO_PI = 2.0 * math.pi

    def sincos(o_s, o_c, ang, tmp1, tmp2, keys):
        for (o, off) in ((o_s, 0.0), (o_c, math.pi / 2)):
            ts(tmp1, ang, off, 1.0 / TWO_PI, ALU.add, ALU.mult, keys, keys)
            ts(tmp2, tmp1, MAGIC, None, ALU.add, None, keys, keys)
            ts(tmp2, tmp2, MAGIC, None, ALU.subtract, None, keys, keys)
            tt(tmp1, tmp1, tmp2, ALU.subtract, keys, keys)
            ts(tmp1, tmp1, TWO_PI, None, ALU.mult, None, keys, keys)
            ts(tmp1, tmp1, math.pi, -math.pi, ALU.min, ALU.max, keys, keys)
            act(o, tmp1, AF.Sin, keys, keys)
    sincos(sn, cs, th, t1, t2, K5)
    tt(lbr, mag, cs, ALU.mult, K5, K5); tt(lbi, mag, sn, ALU.mult, K5, K5)
    tt(den, are, are, ALU.mult, ['are'] + K5, K5); tt(t1, aim, aim, ALU.mult, ['aim'] + K5, K5)
    tt(den, den, t1, ALU.add, K5, K5); recip(den, den, K5, K5)
    ts(nr, lbr, -1.0, None, ALU.add, None, K5, K5)
    tt(t1, nr, are, ALU.mult, K5, K5); tt(t2, lbi, aim, ALU.mult, K5, K5); tt(cfr, t1, t2, ALU.add, K5, K5)
    tt(cfr, cfr, den, ALU.mult, K5, K5)
    tt(t1, lbi, are, ALU.mult, K5, K5); tt(t2, nr, aim, ALU.mult, K5, K5); tt(cfi, t1, t2, ALU.subtract, K5, K5)
    tt(cfi, cfi, den, ALU.mult, K5, K5)
    bbr = al([128, 16, 16]); bbi = al([128, 16, 16]); tb1 = al([128, 16, 16]); nbbi = al([128, 16, 16])
    KB = ['bb']
    cfr_b = cfr.unsqueeze(2).to_broadcast([128, 16, 16]); cfi_b = cfi.unsqueeze(2).to_broadcast([128, 16, 16])
    tt(bbr, bre, cfr_b, ALU.mult, K5 + ['bre'], KB); tt(tb1, bim, cfi_b, ALU.mult, K5 + ['bim'], KB)
    tt(bbr, bbr, tb1, ALU.subtract, KB, KB)
    tt(bbi, bim, cfr_b, ALU.mult, K5 + ['bim'], KB); tt(tb1, bre, cfi_b, ALU.mult, K5 + ['bre'], KB)
    tt(bbi, bbi, tb1, ALU.add, KB, KB)
    ts(nbbi, bbi, -1.0, None, ALU.mult, None, KB, ['nbbi'])

    def cdouble(Tr, Ti, nmax, tA, tB, key, tkey):
        n = 1
        while n < nmax:
            m = min(n, nmax - n)
            ar = Tr[:, :, 1:1 + m]; ai = Ti[:, :, 1:1 + m]
            br_ = Tr[:, :, n:n + 1].to_broadcast([128, 16, m]); bi_ = Ti[:, :, n:n + 1].to_broadcast([128, 16, m])
            tt(tA[:, :, 0:m], ar, br_, ALU.mult, [key], [tkey]); tt(tB[:, :, 0:m], ai, bi_, ALU.mult, [key], [tkey])
            tt(Tr[:, :, n + 1:n + 1 + m], tA[:, :, 0:m], tB[:, :, 0:m], ALU.subtract, [tkey], [key])
            tt(tA[:, :, 0:m], ar, bi_, ALU.mult, [key], [tkey]); tt(tB[:, :, 0:m], ai, br_, ALU.mult, [key], [tkey])
            tt(Ti[:, :, n + 1:n + 1 + m], tA[:, :, 0:m], tB[:, :, 0:m], ALU.add, [tkey], [key])
            n += m
    Et1 = al([128, 16, 64]); Et2 = al([128, 16, 64])
    Lr = al([128, 16, 9]); Li = al([128, 16, 9])
    mset(Lr[:, :, 0:1], 1.0, ['L']); mset(Li[:, :, 0:1], 0.0, ['L'])
    cp(Lr[:, :, 1:2], lbr.unsqueeze(2), K5, ['L']); cp(Li[:, :, 1:2], lbi.unsqueeze(2), K5, ['L'])
    cdouble(Lr, Li, 8, Et1, Et2, 'L', 'Et')
    KE = ['E8']
    mset(E8r[:, :, 0:1], 1.0, KE); mset(E8i[:, :, 0:1], 0.0, KE)
    tt(E8r[:, :, 1:2], Lr[:, :, 8:9], inv8.unsqueeze(2), ALU.mult, ['L'] + K5, KE)
    tt(E8i[:, :, 1:2], Li[:, :, 8:9], inv8.unsqueeze(2), ALU.mult, ['L'] + K5, KE)
    cdouble(E8r, E8i, T2, Et1, Et2, 'E8', 'Et')
    Gi = al([8, 128]); bdmask = al([128, 128]); mtmp = al([128, 128])
    mset(Gi, 1.0, ['Gi'], 'pool')
    P.add('pool', lambda: nc.gpsimd.affine_select(out=Gi, in_=Gi, pattern=[[1, 128]], compare_op=ALU.is_ge, fill=0.0, base=0,
                                                  channel_multiplier=-16), ['Gi'], ['Gi'])
    P.add('pool', lambda: nc.gpsimd.affine_select(out=Gi, in_=Gi, pattern=[[-1, 128]], compare_op=ALU.is_ge, fill=0.0, base=15,
                                                  channel_multiplier=16), ['Gi'], ['Gi'])
    mm(pb[1][:, 0:128], Gi, Gi, True, True, ['Gi'], [pk(1)])
    cp(bdmask, pb[1][:, 0:128], [pk(1)], ['bdmask'], 'dve')
    SOB = al([128, 4, 2, 128]); SOC = al([128, 4, 2, 128])

    def fill_SO(dst, dkey, srcs, keys):
        v = dst.rearrange("p o r (q h c) -> p o r q h c", q=4, h=2)
        for ri, src in enumerate(srcs):
            sv = src.rearrange("p (o q) c -> p o q c", q=4)
            for h in range(2):
                cp(v[h * 64:(h + 1) * 64, :, ri, :, h, :], sv[h * 64:(h + 1) * 64], keys, [dkey])
    mset(SOB, 0.0, ['SOB']); mset(SOC, 0.0, ['SOC']); mset(Qst32, 0.0, ['Qst32'])
    fill_SO(SOB, 'SOB', (bbr, nbbi), KB + ['nbbi'])
    DEF[0] = 'dve'
    SOA = al([128, 4, 2, 128]); LBr = al([128, 16, 16]); LBi = al([128, 16, 16]); LBt = al([128, 16, 16])
    mset(SOA, 0.0, ['SOA'])
    for s_ in range(8):
        k_ = 7 - s_
        lr_b = Lr[:, :, k_:k_ + 1].to_broadcast([128, 16, 16]); li_b = Li[:, :, k_:k_ + 1].to_broadcast([128, 16, 16])
        KC = ['LB']
        tt(LBr, bbr, lr_b, ALU.mult, KB + ['L'], KC); tt(LBt, bbi, li_b, ALU.mult, KB + ['L'], KC); tt(LBr, LBr, LBt, ALU.subtract, KC, KC)
        tt(LBi, bbi, lr_b, ALU.mult, KB + ['L'], KC); tt(LBt, bbr, li_b, ALU.mult, KB + ['L'], KC); tt(LBi, LBi, LBt, ALU.add, KC, KC)
        fill_SO(SOA, 'SOA', (LBr, LBi), KC)
        for o in range(4):
            for ri in range(2):
                tr(pb[1][:, 0:128], SOA[:, o, ri, :], ident_f, ['SOA', 'ident_f'], [pk(1)])
                cp(Ast32[:, o, s_, ri, :], pb[1][:, 0:128], [pk(1)], ['Ast32'], 'dve')

    DEF[0] = 'pool'
    CLr = al([128, 16, 16]); CLi = al([128, 16, 16]); CLt = al([128, 16, 16])
    for tau in range(9):
        if tau < 8:
            DEF[0] = 'dve'
            p1_tile(0, 2 * tau); p1_tile(0, 2 * tau + 1)
            if tau == 7: p1_flush()
            DEF[0] = 'pool'
        lr_b = Lr[:, :, tau:tau + 1].to_broadcast([128, 16, 16]); li_b = Li[:, :, tau:tau + 1].to_broadcast([128, 16, 16])
        KC = ['CL']
        tt(CLr, cre, lr_b, ALU.mult, ['cre', 'L'], KC); tt(CLt, cim, li_b, ALU.mult, ['cim', 'L'], KC); tt(CLr, CLr, CLt, ALU.subtract, KC, KC)
        tt(CLi, cre, li_b, ALU.mult, ['cre', 'L'], KC); tt(CLt, cim, lr_b, ALU.mult, ['cim', 'L'], KC); tt(CLi, CLi, CLt, ALU.add, KC, KC)
        if tau >= 1:
            for h in range(2):
                cp(Qst32[h * 64:(h + 1) * 64, :, tau - 1, 0, 16 * h:16 * h + 16], CLr[h * 64:(h + 1) * 64], KC, ['Qst32'])
                ts(Qst32[h * 64:(h + 1) * 64, :, tau - 1, 1, 16 * h:16 * h + 16], CLi[h * 64:(h + 1) * 64], -1.0, None, ALU.mult, None, KC, ['Qst32'])
        if tau <= 7:
            fill_SO(SOC, 'SOC', (CLr, CLi), KC)
            for o in range(4):
                bk = (0, 2, 3, 4, 5)[(tau * 4 + o) % 5]
                for ri in range(2):
                    mm(pb[bk][:, 0:128], SOB[:, o, ri, :], SOC[:, o, ri, :], ri == 0, ri == 1, ['SOB', 'SOC'], [pk(bk)])
                if tau == 0:
                    tt(mtmp, pb[bk][:, 0:128], bdmask, ALU.mult, [pk(bk), 'bdmask'], ['mtmp'], 'dve')
                    stt(Mlag[:, o, tau, :], ident_f, dcol[:, o:o + 1], mtmp, ALU.mult, ALU.add, ['ident_f', 'dcol', 'mtmp'], ['Mlag'])
                else:
                    tt(Mlag[:, o, tau, :], pb[bk][:, 0:128], bdmask, ALU.mult, [pk(bk), 'bdmask'], ['Mlag'], 'dve')
    DEF[0] = 'dve'
    HG = 2
    gcount = [0]; itc = [0]
    for b in range(NB):
        areset()
        if b > 0:
            p1_tiles(b)

        uT = al([128, 4, L], BF16); gyT = al([128, 4, L], BF16)
        y2T = al([128, 4, 512], BF16); sqT = al([128, 4, 512], BF16)
        Xbufs = [al([128, 4, 2, 257], BF16) for _ in range(2)]
        vre = al([128, 4, T2]); vim = al([128, 4, T2]); wre = al([128, 4, T2]); wim = al([128, 4, T2])
        c1 = al([128, 4, 1]); c2 = al([128, 4, 1]); c3 = al([128, 4, 1]); c4 = al([128, 4, 1])
        pA = sqT[:, 0, :].rearrange("p (q j) -> p q j", q=4); pB = sqT[:, 1, :].rearrange("p (q j) -> p q j", q=4)
        ytmp = al([128, 512]); ytmp2 = al([128, 512])
        for o in range(4):
            wb = wblk if o % 2 == 0 else wblk2; wkk = 'wblk' if o % 2 == 0 else 'wblk2'
            load_w(wb[:, :, 0:128], wkk, w_in[:, 1544 + o * 128:1544 + (o + 1) * 128], 8, 128)
            for tb in range(4):
                bank = tb % 2
                for kc in range(8):
                    mm(pb[bank], wb[:, kc, 0:128], aT[:, kc, tb * 512:(tb + 1) * 512], kc == 0, kc == 7,
                       [wkk] + [aK(4 * tb + j) for j in range(4)], [pk(bank)])
                act(uT[:, o, tb * 512:(tb + 1) * 512], pb[bank], AF.Copy, [pk(bank)], ['uT:%d' % o])

        load_w(wblk[:, :, 0:HG * 64], 'wblk', w_in[:, 0:HG * 64], 8, HG * 64)
        load_w(wblk[:, :, HG * 64:2 * HG * 64], 'wblk', w_in[:, 512:512 + HG * 64], 8, HG * 64)
        load_w(wblk[:, :, 2 * HG * 64:3 * HG * 64], 'wblk', w_in[:, 1024:1024 + HG * 64], 8, HG * 64)
        load_w(wblk[:, :, 768:776], 'wblk', w_in[:, 1536:1544], 8, 8)
        mset(carry, 0.0, ['carry'])
        Vv = PS[:, 1024:3072].rearrange("p (q x) -> p q x", q=4)[:, :, 0:2 * T2].rearrange("p q (r j) -> p q r j", r=2)
        VK = [pk(2), pk(3), pk(4), pk(5)]
        def s5_state(o, jh):
            uo = uT[:, o, :].rearrange("p (j s) -> p j s", s=8)
            uk = ['uT:%d' % o]
            Xbuf = Xbufs[o % 2]; xk_ = 'Xbuf%d' % (o % 2)
            if jh == 0:
                mset(Xbuf[:, :, :, 0:1], 0.0, [xk_])
            j0 = jh * T2
            mset(Vv, 0.0, VK)
            for s_ in range(8):
                for ri in range(2):
                    for q in range(4):
                        mm(Vv[:, q, ri, :], Ast32[32 * q:32 * q + 32, o, s_, ri, :], uo[32 * q:32 * q + 32, j0:j0 + T2, s_], False, s_ == 7,
                           ['Ast32'] + uk, VK, skip=True, tp=(32 * q, 0))
            return Xbuf, xk_, j0

        def s5_rot(o, jh, Xbuf, xk_, j0):
            er = E8r[:, 4 * o:4 * o + 4, 0:T2]; ei = E8i[:, 4 * o:4 * o + 4, 0:T2]
            Vr = Vv[:, :, 0, :]; Vi = Vv[:, :, 1, :]
            KS = ['s5w']
            tt(vre, Vr, er, ALU.mult, VK + ['E8'], ['vre']); tt(vim, Vi, er, ALU.mult, VK + ['E8'], ['vim'])
            tt(wre, Vi, ei, ALU.mult, VK + ['E8'], ['wre']); tt(wim, Vr, ei, ALU.mult, VK + ['E8'], ['wim'])
            tt(vre, vre, wre, ALU.add, ['vre', 'wre'], ['vre']); tt(vim, vim, wim, ALU.subtract, ['vim', 'wim'], ['vim'])
            for q in range(4):
                pr = 4 * o + q
                rb = r8[:, pr:pr + 1].to_broadcast([128, T2])
                for (w_, v_, ci_, wk_, vk_) in ((wre, vre, 0, 'wre', 'vre'), (wim, vim, 1, 'wim', 'vim')):
                    P.add('dve', lambda rb=rb, pr=pr, w_=w_, v_=v_, ci_=ci_, q=q: nc.vector.tensor_tensor_scan(
                        out=w_[:, q, :], data0=rb, data1=v_[:, q, :], initial=carry[:, pr, ci_:ci_ + 1], op0=ALU.mult, op1=ALU.add),
                        [vk_, 'carry', 'r8'], [wk_])
            xo_r = Xbuf[:, :, 0, 1 + j0:1 + j0 + T2]; xo_i = Xbuf[:, :, 1, 1 + j0:1 + j0 + T2]
            wl_r = wre[:, :, T2 - 1:T2]; wl_i = wim[:, :, T2 - 1:T2]
            eTr = E8r[:, 4 * o:4 * o + 4, T2:T2 + 1]; eTi = E8i[:, 4 * o:4 * o + 4, T2:T2 + 1]
            tt(c1, wl_r, eTr, ALU.mult, ['wre', 'E8'], ['c1']); tt(c2, wl_i, eTi, ALU.mult, ['wim', 'E8'], ['c2'])
            tt(c3, wl_i, eTr, ALU.mult, ['wim', 'E8'], ['c3']); tt(c4, wl_r, eTi, ALU.mult, ['wre', 'E8'], ['c4'])
            tt(carry[:, 4 * o:4 * o + 4, 0:1], c1, c2, ALU.subtract, ['c1', 'c2'], ['carry'])
            tt(carry[:, 4 * o:4 * o + 4, 1:2], c3, c4, ALU.add, ['c3', 'c4'], ['carry'])
            tt(pA, wre, er, ALU.mult, ['wre', 'E8'], ['sqT:0'], 'pool'); tt(pB, wim, ei, ALU.mult, ['wim', 'E8'], ['sqT:1'], 'pool')
            tt(xo_r, pA, pB, ALU.subtract, ['sqT:0', 'sqT:1'], [xk_], 'pool')
            tt(pA, wre, ei, ALU.mult, ['wre', 'E8'], ['sqT:0'], 'pool'); tt(pB, wim, er, ALU.mult, ['wim', 'E8'], ['sqT:1'], 'pool')
            tt(xo_i, pA, pB, ALU.add, ['sqT:0', 'sqT:1'], [xk_], 'pool')

        def s5_Y(o, t):
            uo = uT[:, o, :].rearrange("p (j s) -> p j s", s=8)
            uk = ['uT:%d' % o]
            Xbuf = Xbufs[o % 2]; xk_ = 'Xbuf%d' % (o % 2)
            gyo = gyT[:, o, :].rearrange("p (j s) -> p j s", s=8)
            yb = t % 2
            for s_ in range(t + 1):
                mm(pb[yb][:, 0:256], Mlag[:, o, t - s_, :], uo[:, :, s_], s_ == 0, False, ['Mlag'] + uk, [pk(yb)])
            for q in range(4):
                for ri in range(2):
                    mm(pb[yb][32 * q:32 * q + 32, 0:256], Qst32[:, 4 * o + q, t, ri, :], Xbuf[:, q, ri, 0:256], False, ri == 1,
                       ['Qst32', xk_], [pk(yb)], tp=(0, 32 * q))
            yf = ytmp[:, (t % 2) * 256:(t % 2) * 256 + 256]; yq = ytmp2[:, (t % 2) * 256:(t % 2) * 256 + 256]
            ky = 'yf%d' % (t % 2); kq = 'yq%d' % (t % 2)
            act(yf, pb[yb][:, 0:256], AF.Copy, [pk(yb)], [ky])
            tt(yq, yf, yf, ALU.mult, [ky], [kq])
            ts(yq, yq, 0.044715, 1.0, ALU.mult, ALU.add, [kq], [kq])
            tt(yq, yq, yf, ALU.mult, [ky, kq], [kq])
            act(yq, yq, AF.Sigmoid, [kq], [kq], scale=1.5957691216)
            tt(gyo[:, :, t], yf, yq, ALU.mult, [ky, kq], ['gyT:%d' % o], 'pool')
        for o in range(5):
            for jh in range(2):
                if o < 4:
                    st = s5_state(o, jh)
                if o >= 1:
                    for t in range(4 * jh, 4 * jh + 4):
                        s5_Y(o - 1, t)
                if o < 4:
                    s5_rot(o, jh, *st)

        for tb in range(4):
            cols = slice(tb * 512, (tb + 1) * 512)
            gk = ['gyT:%d' % j for j in range(4)]
            for nch in range(4):
                bank = nch % 2
                for kc in range(4):
                    mm(pb[bank], wglu[:, kc, nch * 128:(nch + 1) * 128], gyT[:, kc, cols], kc == 0, kc == 3, ['wglu'] + gk, [pk(bank)])
                sg_ = (ytmp, vre.rearrange("p q j -> p (q j)"))[nch % 2]; sgk = ('ytmp', 'vre')[nch % 2]
                act(sg_, pb[bank], AF.Sigmoid, [pk(bank), 'b_glu'], [sgk], bias=b_glu[:, nch:nch + 1])
                tt(y2T[:, nch, :], gyT[:, nch, cols], sg_, ALU.mult, gk + [sgk], ['y2T:%d' % nch])
                tt(sqT[:, nch, :], y2T[:, nch, :], y2T[:, nch, :], ALU.mult, ['y2T:%d' % nch], ['sqT:%d' % nch])
            for nch in range(4):
                mm(pb[5], ones_b, sqT[:, nch, :], nch == 0, nch == 3, ['ones_b', 'sqT:%d' % nch], [pk(5)])
            ts(ytmp2, pb[5], 1.0 / 512, EPS, ALU.mult, ALU.add, [pk(5)], ['ytmp2'])
            act(ytmp2, ytmp2, AF.Sqrt, ['ytmp2'], ['ytmp2'])
            recip(ytmp2, ytmp2, ['ytmp2'], ['ytmp2'])
            for nch in range(4):
                stt(sqT[:, nch, :], y2T[:, nch, :], g_s5[:, nch:nch + 1], ytmp2, ALU.mult, ALU.mult, ['y2T:%d' % nch, 'g_s5', 'ytmp2', 'sqT:%d' % nch], ['sqT:%d' % nch])
                dma(mix_d[b, 4 * tb:4 * tb + 4, :, 4 + nch, :].rearrange("j p t -> p j t"), sqT[:, nch, :].rearrange("p (j t) -> p j t", j=4),
                    ['sqT:%d' % nch], ['mixd:%d' % tb])

        areset()
        QT = al([68, HG, L], BF16); KT = al([68, HG, L], BF16)
        Vp = al([128, NT, HG, 65], BF16)
        fox = al([128, NT, 512], BF16)
        caug = al([128, NT, 8, 2], BF16); ncaug = al([128, NT, 8, 2], BF16)
        lfs = al([128, NT, 3, 8], BF16)
        TG = 4
        qk_sb = al([128, TG, 2 * HG, 64]); qk_sq = al([128, TG, 2 * HG, 64])
        aug = [al([128, TG, 2 * HG, 68], BF16) for _ in range(2)]
        gqk = al([128, 2 * HG, 64])
        fzA = al([128, NT, 8]); fzB = al([128, NT, 8]); fzC = al([128, NT, 8]); ones_f16 = al([128, NT])
        rsq = al([128, TG, 2 * HG]); rsq2 = al([128, TG, 2 * HG])
        mset(ones_f16, 1.0, ['ones_f16'])
        PT = [al([128, 512], BF16) for _ in range(3)]
        foxn = [al([128, 512], BF16) for _ in range(2)]; foxT = [al([128, 4, 128], BF16) for _ in range(2)]
        fss = al([128, NT]); frs = al([128, NT])
        for a in range(2):
            mset(aug[a], 1.0, ['aug%d' % a])
        for hh in range(HG):
            cp(gqk[:, hh, :], gq_bc, ['gq_bc'], ['gqk']); cp(gqk[:, HG + hh, :], gk_bc, ['gk_bc'], ['gqk'])
        pbt2 = PS[:, 2048:3072].bitcast(BF16)
        for hg in range(8 // HG):
            h0 = hg * HG
            wq = wblk[:, :, 0:HG * 64]; wk = wblk[:, :, HG * 64:2 * HG * 64]; wv = wblk[:, :, 2 * HG * 64:3 * HG * 64]; wf = wblk[:, :, 768:776]
            if hg > 0:
                load_w(wq, 'wblk', w_in[:, h0 * 64:(h0 + HG) * 64], 8, HG * 64)
                load_w(wk, 'wblk', w_in[:, 512 + h0 * 64:512 + (h0 + HG) * 64], 8, HG * 64)
                load_w(wv, 'wblk', w_in[:, 1024 + h0 * 64:1024 + (h0 + HG) * 64], 8, HG * 64)
            if hg == 0:
                load_w(wblk2, 'wblk2', w_out, 8, 1024)
            mset(Vp, 1.0, ['Vp:%d' % i for i in range(NT)])
            if hg == 0:
                ALLK = [aK(i) for i in range(NT)]
                for i in range(NT):
                    for kc in range(8):
                        mm(pb[7][:, i * 8:(i + 1) * 8], aT[:, kc, i * 128:(i + 1) * 128], wf[:, kc, :], kc == 0, kc == 7, ['wblk', aK(i)], [pk(7)])
                tt(fzA, pb[7][:, 0:NT * 8].rearrange("p (t h) -> p t h", t=NT), fb_bc.unsqueeze(1).to_broadcast([128, NT, 8]), ALU.add,
                   [pk(7), 'fb_bc'], ['fzA'])
                act(fzA, fzA, AF.Exp, ['fzA'], ['fzA'], scale=-1.0)
                act(fzA, fzA, AF.Ln, ['fzA'], ['fzA'], bias=1.0)
                cp(lfs[:, :, 0, :], fzA, ['fzA'], ['lfs'])
                tt(fzB, fzA, lfs[:, :, 0, :], ALU.subtract, ['fzA', 'lfs'], ['fzB'])
                cp(lfs[:, :, 1, :], fzB, ['fzB'], ['lfs'])
                tt(fzC, fzB, lfs[:, :, 1, :], ALU.subtract, ['fzB', 'lfs'], ['fzC'])
                cp(lfs[:, :, 2, :], fzC, ['fzC'], ['lfs'])
                mm(pb[6][:, 0:NT * 24], ones_b, lfs.rearrange("p t k h -> p (t k h)"), True, True, ['ones_b', 'lfs'], [pk(6)])
                red(fzB, pb[6][:, 0:NT * 24].rearrange("p (t k h) -> p t h k", t=NT, k=3), [pk(6)], ['fzB'])
                for h in range(8):
                    P.add('dve', lambda h=h: nc.vector.tensor_tensor_scan(out=fzC[:, :, h], data0=ones_f16, data1=fzB[:, :, h], initial=0.0,
                                                                          op0=ALU.mult, op1=ALU.add), ['fzB', 'ones_f16'], ['fzC'])
                tt(fzC, fzC, fzB, ALU.subtract, ['fzC', 'fzB'], ['fzC'])
                for part in range(3):
                    mm(pb[1][:, 0:NT * 8], tri_b, lfs[:, :, part, :], part == 0, part == 2, ['tri_b', 'lfs'], [pk(1)])
                tt(fzA, pb[1][:, 0:NT * 8].rearrange("p (t h) -> p t h", t=NT), fzC, ALU.add, [pk(1), 'fzC'], ['fzA'])
                ts(fzA, fzA, -8.0, None, ALU.mult, None, ['fzA'], ['fzA'])
                CK = ['caug:%d' % i for i in range(NT)]; NCK = ['ncaug:%d' % i for i in range(NT)]
                cp(caug[:, :, :, 0], fzA, ['fzA'], CK)
                tt(caug[:, :, :, 1], fzA, caug[:, :, :, 0], ALU.subtract, ['fzA'] + CK, CK)
                ts(ncaug, caug, -1.0, None, ALU.mult, None, CK, NCK)
            W_ = HG * 64

            PBK = (0, 1, 6, 7)

            def proj_mm(gi):
                for tl, i in enumerate(range(gi * TG, (gi + 1) * TG)):
                    tcols = slice(i * 128, (i + 1) * 128)
                    for kc in range(8):
                        mm(pb[PBK[tl]][:, 0:3 * W_], aT[:, kc, tcols], wblk[:, kc, 0:3 * W_], kc == 0, kc == 7, ['wblk', aK(i)], [pk(PBK[tl])])
            proj_mm(0)
            for gi in range(NT // TG):
                tiles = range(gi * TG, (gi + 1) * TG)
                for half, (b0, b1) in enumerate(((0, 1), (6, 7))):
                    pv_ = PS[:, b0 * 512:(b1 + 1) * 512].rearrange("p (t c) -> p t c", t=2)
                    tsl = slice(2 * half, 2 * half + 2)
                    act(qk_sb[:, tsl, 0:HG, :], pv_[:, :, 0:W_].rearrange("p t (h d) -> p t h d", h=HG), AF.Copy, [pk(b0), pk(b1)], ['qk_sb'])
                    act(qk_sb[:, tsl, HG:2 * HG, :], pv_[:, :, W_:2 * W_].rearrange("p t (h d) -> p t h d", h=HG), AF.Copy, [pk(b0), pk(b1)], ['qk_sb'])
                    act(Vp[:, gi * TG + 2 * half:gi * TG + 2 * half + 2, :, 0:64], pv_[:, :, 2 * W_:3 * W_].rearrange("p t (h d) -> p t h d", h=HG), AF.Copy,
                        [pk(b0), pk(b1)], ['Vp:%d' % i for i in tiles])
                if gi + 1 < NT // TG:
                    proj_mm(gi + 1)
                tt(qk_sq, qk_sb, qk_sb, ALU.mult, ['qk_sb'], ['qk_sq'])
                red(rsq, qk_sq, ['qk_sq'], ['rsq'])
                rsqrt_chain(rsq2, rsq, 64, ['rsq'], ['rsq2'], rsq)
                tt(qk_sq, qk_sb, rsq2.unsqueeze(3).to_broadcast([128, TG, 2 * HG, 64]), ALU.mult, ['qk_sb', 'rsq2'], ['qk_sq'])
                ag = aug[gi % 2]; agk = 'aug%d' % (gi % 2)
                tt(ag[:, :, :, 0:64], qk_sq, gqk.unsqueeze(1).to_broadcast([128, TG, 2 * HG, 64]), ALU.mult, ['qk_sq', 'gqk'], [agk])
                cp(ag[:, :, 0:HG, 64:66], caug[:, gi * TG:(gi + 1) * TG, h0:h0 + HG, :], ['caug:%d' % i for i in tiles], [agk])
                cp(ag[:, :, HG:2 * HG, 66:68], ncaug[:, gi * TG:(gi + 1) * TG, h0:h0 + HG, :], ['ncaug:%d' % i for i in tiles], [agk])
                for tl in range(TG):
                    for j in range(2 * HG):
                        blk = tl * 2 * HG + j
                        tr(pbt2[0:68, blk * 128:(blk + 1) * 128], ag[:, tl, j, :], ident_b, [agk, 'ident_b'], [pk(4), pk(5)])
                pv = pbt2[0:68, :].rearrange("p (t j x) -> p j t x", t=TG, j=2 * HG)
                gcols = slice(gi * TG * 128, (gi + 1) * TG * 128)
                cp(QT[:, :, gcols].rearrange("p h (t x) -> p h t x", t=TG), pv[:, 0:HG], [pk(4), pk(5)], ['QT:%d' % i for i in tiles])
                cp(KT[:, :, gcols].rearrange("p h (t x) -> p h t x", t=TG), pv[:, HG:2 * HG], [pk(4), pk(5)], ['KT:%d' % i for i in tiles])
            for hh in range(HG):
                h = h0 + hh
                for qg in range(4):
                    ob = 4 + (gcount[0] % 2); gcount[0] += 1
                    mset(pb[ob], 0.0, [pk(ob)])
                    its = []
                    for kt in range(4 * qg + 4):
                        q0 = max(kt, 4 * qg); N = (4 * qg + 4 - q0) * 128
                        its.append((kt, q0, N, (2, 3, 6)[itc[0] % 3], PT[itc[0] % 3], 'PT%d' % (itc[0] % 3))); itc[0] += 1

                    def qk(itm):
                        kt, q0, N, sbk, ptt, ptk = itm
                        mm(pb[sbk][:, 0:N], KT[:, hh, kt * 128:(kt + 1) * 128], QT[:, hh, q0 * 128:(4 * qg + 4) * 128], True, True,
                           ['KT:%d' % kt] + ['QT:%d' % j for j in range(q0, 4 * qg + 4)], [pk(sbk)])
                    qk(its[0])
                    if len(its) > 1: qk(its[1])
                    for n_, itm in enumerate(its):
                        kt, q0, N, sbk, ptt, ptk = itm
                        if n_ + 2 < len(its): qk(its[n_ + 2])
                        act(ptt[:, 0:N], pb[sbk][:, 0:N], AF.Exp, [pk(sbk)], [ptk], scale=0.125)
                        if kt >= 4 * qg:
                            tt(ptt[:, 0:128], ptt[:, 0:128], tri_b, ALU.mult, [ptk, 'tri_b'], [ptk])
                        for qb in range(q0, 4 * qg + 4):
                            j = qb - 4 * qg
                            mm(pb[ob][:, j * 128:j * 128 + 65], ptt[:, (qb - q0) * 128:(qb - q0 + 1) * 128], Vp[:, kt, hh, :],
                               False, kt == qb, [ptk, 'Vp:%d' % kt], [pk(ob)], skip=True)
                    ov = pb[ob].rearrange("p (j c) -> p j c", j=4)
                    recip(sm[:, 12:16], ov[:, :, 64], [pk(ob)], ['sml'])
                    tt(fox[:, 4 * qg:4 * qg + 4, h * 64:(h + 1) * 64], ov[:, :, 0:64], sm[:, 12:16].unsqueeze(2).to_broadcast([128, 4, 64]),
                       ALU.mult, [pk(ob), 'sml'], ['fox:%d' % qg])
        for i in range(NT):
            fk = 'fox:%d' % (i // 4)
            jn = foxn[i % 2]; jk = 'foxn%d' % (i % 2)
            tt(jn, fox[:, i, :], fox[:, i, :], ALU.mult, [fk], [jk])
            red(fss[:, i:i + 1], jn, [jk], ['fss'])
        rsqrt_chain(frs, fss, 512, ['fss'], ['frs'], fss)
        def fox_norm(i):
            stt(foxn[i % 2], fox[:, i, :], frs[:, i:i + 1], gfox_bc, ALU.mult, ALU.mult, ['fox:%d' % (i // 4), 'frs', 'gfox_bc'], ['foxn%d' % (i % 2)])
        fox_norm(0)
        for i in range(NT):
            fk = 'fox:%d' % (i // 4); p = i % 2
            if i + 1 < NT: fox_norm(i + 1)
            pbt = pb[4 + p].bitcast(BF16)
            for j in range(4):
                tr(pbt[:, j * 128:(j + 1) * 128], foxn[p][:, j * 128:(j + 1) * 128], ident_b, ['foxn%d' % p, 'ident_b'], [pk(4 + p)])
            cp(foxT[p], pbt[:, 0:512].rearrange("p (c t) -> p c t", c=4), [pk(4 + p)], ['foxT%d' % p])
            dma(mix_d[b, i, :, 0:4, :], foxT[p], ['foxT%d' % p], ['mixf:%d' % i])

        areset()
        mixt = [al([128, 8, 128], BF16) for _ in range(2)]
        h1t = [al([128, D]) for _ in range(2)]
        load_w(wblk, 'wblk', w_xq, 8, 1024)

        def p3_L(i):
            tcols = slice(i * 128, (i + 1) * 128)
            s = i % 2
            dma(mixt[s], mix_d[b, i], ['mixf:%d' % i, 'mixd:%d' % (i // 4)], ['mixt%d' % s])
            dma(xt[s], x[b, tcols, :], (), ['xt%d' % s])

        def p3_A(i):
            tcols = slice(i * 128, (i + 1) * 128)
            s = i % 2; mt_ = mixt[s]; mk = 'mixt%d' % s; hk_ = 'h1t%d' % s
            for hf in range(2):
                for kc in range(8):
                    mm(pb[hf], mt_[:, kc, :], wblk2[:, kc, hf * 512:(hf + 1) * 512], kc == 0, kc == 7, [mk, 'wblk2'], [pk(hf)])
                tt(h1t[s][:, hf * 512:(hf + 1) * 512], pb[hf], xt[s][:, hf * 512:(hf + 1) * 512], ALU.add, [pk(hf), 'xt%d' % s], [hk_])
            dma(h1_d[b, tcols, :], h1t[s], [hk_], ['h1d:%d' % i])

        p3_L(0)
        for step in range(NT + 1):
            if step + 1 < NT: p3_L(step + 1)
            if step < NT:
                p3_A(step)
                stats_chain(h1t[step % 2], 'h1t%d' % (step % 2), par=step % 2)
            if step >= 1:
                j = step - 1
                stats_T(aT[:, :, j * 128:(j + 1) * 128], aK(j), g_cross, 'g_cross', par=j % 2)

        areset()
        xkT = al([128, 4, 2, NMEM], BF16)
        xvp = al([128, 2, 4, 257], BF16)
        a_save = aoff[0]
        memT = al([128, 8, NMEM], BF16)
        xk_sb = al([128, 2, 256]); xk_sq = al([128, 2, 256]); xkn = al([128, 2, 256], BF16)
        mset(xvp, 1.0, ['xvp'])
        for mt in range(2):
            s = mt % 2
            dma(xt[s], mem[b, mt * 128:(mt + 1) * 128, :], (), ['xt%d' % s])
            tile_stats_T(xt[s], 'xt%d' % s, memT[:, :, mt * 128:(mt + 1) * 128], 'memT', g_mem, 'g_mem', par=s)
        for cbk in range(4):
            wkv = wblk2[:, :, (cbk % 2) * 512:(cbk % 2 + 1) * 512]; wkvk = 'wblk2h%d' % (cbk % 2)
            P.add('gq', lambda wkv=wkv, cbk=cbk: nc.gpsimd.dma_start(out=wkv, in_=w_xkv[:, cbk * 512:(cbk + 1) * 512].rearrange("(c p) n -> p c n", p=128)),
                  (), [wkvk] + (['wblk2'] if cbk < 2 else []))
            for mt in range(2):
                bank = mt
                for kc in range(8):
                    mm(pb[bank], memT[:, kc, mt * 128:(mt + 1) * 128], wkv[:, kc, :], kc == 0, kc == 7, ['memT', wkvk], [pk(bank)])
                if cbk < 2:
                    act(xk_sb, pb[bank].rearrange("p (h d) -> p h d", h=2), AF.Copy, [pk(bank)], ['xk_sb'])
                    tt(xk_sq, xk_sb, xk_sb, ALU.mult, ['xk_sb'], ['xk_sq'])
                    red(sm[:, 48:50], xk_sq, ['xk_sq'], ['smq'])
                    rsqrt_chain(sm[:, 56:58], sm[:, 48:50], 256, ['smq'], ['smr'], sm[:, 4:6])
                    tt(xk_sq, xk_sb, sm[:, 56:58].unsqueeze(2).to_broadcast([128, 2, 256]), ALU.mult, ['xk_sb', 'smr'], ['xk_sq'])
                    tt(xkn, xk_sq, gxk_bc.unsqueeze(1).to_broadcast([128, 2, 256]), ALU.mult, ['xk_sq', 'gxk_bc'], ['xkn'])
                    pbt = pb[5].bitcast(BF16)
                    for j in range(4):
                        tr(pbt[:, j * 128:(j + 1) * 128], xkn[:, j // 2, (j % 2) * 128:(j % 2 + 1) * 128], ident_b, ['xkn', 'ident_b'], [pk(5)])
                    cp(xkT[:, 2 * cbk:2 * cbk + 2, :, mt * 128:(mt + 1) * 128],
                       pbt[:, 0:512].rearrange("p (h c m) -> p h c m", h=2, c=2), [pk(5)], ['xkT'])
                else:
                    hv = 2 * (cbk - 2)
                    act(xvp[:, mt, hv:hv + 2, 0:256], pb[bank].rearrange("p (h d) -> p h d", h=2), AF.Copy, [pk(bank)], ['xvp'])
        P.add('gq', lambda: nc.gpsimd.dma_start(out=wblk2, in_=w_xo.rearrange("(c p) n -> p c n", p=128)), (), ['wblk2', 'wblk2h0', 'wblk2h1'])
        P.barrier(); aoff[0] = a_save
        xq_sb = [al([128, 4, 256]) for _ in range(2)]; xq_sq = [al([128, 4, 256]) for _ in range(2)]
        xqn = [al([128, 4, 256], BF16) for _ in range(2)]; xqT = [al([128, 8, 128], BF16) for _ in range(2)]
        PTx = [al([128, 1024], BF16) for _ in range(2)]; xo_sb = [al([128, 1024], BF16) for _ in range(2)]
        xoT = [al([128, 8, 128], BF16) for _ in range(2)]
        h1t = [al([128, D]) for _ in range(2)]
        pbt = pb[5].bitcast(BF16)

        def p4_A1(i):
            tcols = slice(i * 128, (i + 1) * 128)
            p = i % 2
            dma(xt[p], h1_d[b, tcols, :], ['h1d:%d' % i], ['xt%d' % p])
            for hf in range(2):
                for kc in range(8):
                    mm(pb[hf], aT[:, kc, tcols], wblk[:, kc, hf * 512:(hf + 1) * 512], kc == 0, kc == 7, [aK(i), 'wblk'], [pk(hf)])

        def p4_A1n(i):
            p = i % 2
            c0_ = 24 + 8 * p
            mset(sm[:, c0_:c0_ + 4], 0.0, ['smq%d' % p])
            for h in range(4):
                src = pb[h // 2][:, (h % 2) * 256:(h % 2 + 1) * 256]
                act(xq_sq[p][:, h, :], src, AF.Square, [pk(h // 2)], ['xq_sq%d' % p, 'smq%d' % p], accum=sm[:, c0_ + h:c0_ + h + 1])
            rsqrt_chain(sm[:, c0_ + 4:c0_ + 8], sm[:, c0_:c0_ + 4], 256, ['smq%d' % p], ['smr%d' % p], sm[:, c0_:c0_ + 4])
            for h in range(4):
                src = pb[h // 2][:, (h % 2) * 256:(h % 2 + 1) * 256]
                stt(xqn[p][:, h, :], src, sm[:, c0_ + 4 + h:c0_ + 5 + h], gxq_bc, ALU.mult, ALU.mult, [pk(h // 2), 'smr%d' % p, 'gxq_bc'], ['xqn%d' % p])

        def p4_A2(i):
            p = i % 2
            for j in range(8):
                tr(pbt[:, j * 128:(j + 1) * 128], xqn[p][:, j // 2, (j % 2) * 128:(j % 2 + 1) * 128], ident_b, ['xqn%d' % p, 'ident_b'], [pk(5)])
            cp(xqT[p], pbt.rearrange("p (j t) -> p j t", j=8), [pk(5)], ['xqT%d' % p])

        def p4_B1(i):
            p = i % 2
            for h in range(4):
                for mt in range(2):
                    bank = 2 + h // 2; c0_ = ((h % 2) * 2 + mt) * 128
                    for dc in range(2):
                        mm(pb[bank][:, c0_:c0_ + 128], xkT[:, h, dc, mt * 128:(mt + 1) * 128], xqT[p][:, h * 2 + dc, :], dc == 0, dc == 1,
                           ['xkT', 'xqT%d' % p], [pk(bank)])
            for hb in range(2):
                act(PTx[p][:, hb * 512:(hb + 1) * 512], pb[2 + hb], AF.Exp, [pk(2 + hb)], ['PTx%d' % p], scale=1.0 / 16)

        def p4_B2(i):
            p = i % 2
            for h in range(4):
                bank = 4 if h % 2 == 0 else 7
                for mt in range(2):
                    mm(pb[bank][:, 0:257], PTx[p][:, (h * 2 + mt) * 128:(h * 2 + mt + 1) * 128], xvp[:, mt, h, :], mt == 0, mt == 1,
                       ['PTx%d' % p, 'xvp'], [pk(bank)])
                c_ = 40 + 4 * p + h
                recip(sm[:, c_:c_ + 1], pb[bank][:, 256:257], [pk(bank)], ['sml%d' % c_])
                ts(xo_sb[p][:, h * 256:(h + 1) * 256], pb[bank][:, 0:256], sm[:, c_:c_ + 1], None, ALU.mult, None, [pk(bank), 'sml%d' % c_], ['xo_sb%d' % p])

        def p4_B2t(i):
            p = i % 2
            for j in range(8):
                tr(pbt[:, j * 128:(j + 1) * 128], xo_sb[p][:, j * 128:(j + 1) * 128], ident_b, ['xo_sb%d' % p, 'ident_b'], [pk(5)])
            cp(xoT[p], pbt.rearrange("p (j t) -> p j t", j=8), [pk(5)], ['xoT%d' % p])

        def p4_C1(i):
            tcols = slice(i * 128, (i + 1) * 128)
            p = i % 2
            for hf in range(2):
                for kc in range(8):
                    mm(pb[hf], xoT[p][:, kc, :], wblk2[:, kc, hf * 512:(hf + 1) * 512], kc == 0, kc == 7, ['xoT%d' % p, 'wblk2'], [pk(hf)])
                tt(h1t[p][:, hf * 512:(hf + 1) * 512], pb[hf], xt[p][:, hf * 512:(hf + 1) * 512], ALU.add, [pk(hf), 'xt%d' % p], ['h1t%d' % p])
            dma(h2_d[b, tcols, :], h1t[p], ['h1t%d' % p], ['h2d:%d' % i])
        for step in range(NT + 2):
            ic = step - 2; ib = step - 1; ia = step
            if 0 <= ib < NT: p4_A2(ib)
            if 0 <= ib < NT: p4_B1(ib)
            if 0 <= ic < NT: p4_C1(ic)
            if 0 <= ic < NT: stats_chain(h1t[ic % 2], 'h1t%d' % (ic % 2), par=ic % 2)
            if 0 <= ib < NT: p4_B2(ib)
            if 0 <= ia < NT: p4_A1(ia)
            if 0 <= ib < NT: p4_B2t(ib)
            if 0 <= ic < NT: stats_T(aT[:, :, ic * 128:(ic + 1) * 128], aK(ic), g_ffn, 'g_ffn', par=ic % 2)
            if 0 <= ia < NT: p4_A1n(ia)

        areset()
        Gs = [al([128, 514]) for _ in range(2)]; acc = [al([128, 512]) for _ in range(2)]; sl = [al([128, 512]) for _ in range(2)]
        hidb = [al([128, 512], BF16) for _ in range(2)]
        wd = al([128, NC_FF, 512], BF16)
        hidt = [al([128, NC_FF, 256], BF16) for _ in range(2)]
        it5 = 0
        p5_pend = []

        def p5_fin():
            while p5_pend:
                k_, c_, tb_, bu__ = p5_pend.pop()
                tt(hidb[k_], sl[k_], pb[bu__], ALU.mult, ['sl%d' % k_, pk(bu__)], ['hidb%d' % k_])
                for hh_ in range(2):
                    dma(hid_d[b, 2 * tb_ + hh_, :, c_, :], hidb[k_][:, hh_ * 256:(hh_ + 1) * 256], ['hidb%d' % k_], ['hid:%d:%d' % (tb_, hh_)])
        for cg in range((NC_FF + 3) // 4):
            cbase = cg * 4; ncg = min(4, NC_FF - cbase)
            if cg == 4:
                load_w(wd, 'wd', w_ffn_down[:, 0:512], NC_FF, 512)
            wb = wblk if cg % 2 == 0 else wblk2; wkk = 'wblk' if cg % 2 == 0 else 'wblk2'
            load_w(wb[:, :, 0:ncg * 128], wkk, w_ffn_up[:, cbase * 128:(cbase + ncg) * 128], 8, ncg * 128)
            load_w(wb[:, :, 512:512 + ncg * 128], wkk, w_ffn_up[:, DFF + cbase * 128:DFF + (cbase + ncg) * 128], 8, ncg * 128)
            for ci in range(ncg):
                c = cbase + ci
                for tb in range(4):
                    k = it5 % 2; it5 += 1
                    Gk = Gs[k]; gkk = 'Gs%d' % k; ak_ = 'acc%d' % k; sk_ = 'sl%d' % k; hk = 'hidb%d' % k
                    cols = slice(tb * 512, (tb + 1) * 512)
                    ak = [aK(4 * tb + j) for j in range(4)]
                    bg = 2 * k; bu_ = bg + 1
                    if tb == 0:
                        mset(Gk[:, 0:2], 0.0, [gkk])
                    for kc in range(8):
                        mm(pb[bg], wb[:, kc, ci * 128:(ci + 1) * 128], aT[:, kc, cols], kc == 0, kc == 7, [wkk] + ak, [pk(bg)])
                    for kc in range(8):
                        mm(pb[bu_], wb[:, kc, 512 + ci * 128:512 + (ci + 1) * 128], aT[:, kc, cols], kc == 0, kc == 7, [wkk] + ak, [pk(bu_)])
                    act(Gk[:, 2:514], pb[bg], AF.Copy, [pk(bg)], [gkk])
                    if tb < 3:
                        cp(Gs[1 - k][:, 0:2], Gk[:, 512:514], [gkk], ['Gs%d' % (1 - k)])
                    ts(acc[k], Gk[:, 2:514], cw[:, 2, c:c + 1], cb[:, c:c + 1], ALU.mult, ALU.add, [gkk, 'cw', 'cb'], [ak_])
                    stt(acc[k], Gk[:, 1:513], cw[:, 1, c:c + 1], acc[k], ALU.mult, ALU.add, [gkk, 'cw', ak_], [ak_])
                    stt(acc[k], Gk[:, 0:512], cw[:, 0, c:c + 1], acc[k], ALU.mult, ALU.add, [gkk, 'cw', ak_], [ak_])
                    act(sl[k], acc[k], AF.Silu, [ak_], [sk_])
                    p5_fin()
                    p5_pend.append((k, c, tb, bu_))

        p5_fin()
        for hf in range(2):
            if hf == 1:
                load_w(wd, 'wd', w_ffn_down[:, hf * 512:(hf + 1) * 512], NC_FF, 512)

            def p5_L(i):
                s = i % 2; hs = (i // 2) % 2
                if i % 2 == 0:
                    dma(hidt[hs], hid_d[b, i // 2], ['hid:%d:%d' % (i // 4, (i // 2) % 2)], ['hidt%d' % hs])
                dma(xt[s][:, 0:512], h2_d[b, i * 128:(i + 1) * 128, hf * 512:(hf + 1) * 512], ['h2d:%d' % i], ['xt%d' % s])
            p5_L(0)
            for i in range(NT):
                tcols = slice(i * 128, (i + 1) * 128)
                s = i % 2; hs = (i // 2) % 2
                if i + 1 < NT: p5_L(i + 1)
                for c in range(NC_FF):
                    mm(pb[s], hidt[hs][:, c, (i % 2) * 128:(i % 2 + 1) * 128], wd[:, c, :], c == 0, c == NC_FF - 1, ['hidt%d' % hs, 'wd'], [pk(s)])
                tt(xt[s][:, 512:1024], pb[s], xt[s][:, 0:512], ALU.add, [pk(s), 'xt%d' % s], ['ot%d' % s])
                dma(out[b, tcols, hf * 512:(hf + 1) * 512], xt[s][:, 512:1024], ['ot%d' % s], ['out:%d:%d:%d' % (b, i, hf)], q='aq')
    P.emit(es)
    return nc, es


_PARAMS = ["norm_mix", "w_in", "fox_q_norm", "fox_k_norm", "fox_f_bias", "s5_a_re", "s5_a_im", "s5_log_dt", "s5_b_re", "s5_b_im",
           "s5_c_re", "s5_c_im", "s5_d", "s5_w_glu", "s5_b_glu", "out_norm_fox", "out_norm_s5", "w_out", "norm_cross", "norm_mem",
           "w_xq", "w_xkv", "xq_norm", "xk_norm", "w_xo", "norm_ffn", "w_ffn_up", "ffn_conv_w", "ffn_conv_b", "w_ffn_down"]


def kernel(**inputs):
    nc, es = build()
    with es:
        params = {k: np.ascontiguousarray(np.asarray(inputs[k], dtype=np.float32)[0]) for k in _PARAMS}
        x = np.asarray(inputs["x"], dtype=np.float32); mem = np.asarray(inputs["mem"], dtype=np.float32)
        in_maps = []
        for c in range(8):
            m = dict(params)
            m["x"] = np.ascontiguousarray(x[NB * c:NB * (c + 1)])
            m["mem"] = np.ascontiguousarray(mem[NB * c:NB * (c + 1)])
            in_maps.append(m)
        res = run_bass_kernel_spmd(nc, in_maps, core_ids=list(range(8)))
    return np.concatenate([r["out"] for r in res.results], axis=0).astype(np.float32)
```

```python
import math
from contextlib import ExitStack
import numpy as np
import concourse.bass as bass
import concourse.mybir as mybir
from concourse.bass_utils import run_bass_kernel_spmd

F32 = mybir.dt.float32
BF16 = mybir.dt.bfloat16
AF = mybir.ActivationFunctionType
ALU = mybir.AluOpType
AX = mybir.AxisListType

D = 1024; L = 2048; NT = 16; NB = 2; NMEM = 256; DFF = 2816; NC_FF = 22
EPS = 1e-6
ENGS = ['sp', 'pe', 'act', 'dve', 'pool']
NDS = 16
DEBUG = None


class Prog:
    def __init__(self, nc):
        self.nc = nc; self.ops = []; self.lastw = {}; self.rd = {}
        self.bar = set(); self.last_eng = {}; self.ndma = 0; self.last_slot = {}; self.gq_since = []

    def barrier(self):
        self.bar = set(self.last_eng.values()) | set(self.last_slot.values()) | set(self.gq_since)
        self.gq_since = []

    def add(self, eng, fn, r=(), w=()):
        i = len(self.ops); deps = set(self.bar)
        for k in list(r) + list(w):
            if k in self.lastw: deps.add(self.lastw[k])
        for k in w:
            rdk = self.rd.get(k)
            if rdk:
                deps.update(rdk[0].values()); deps.update(rdk[1])
        self.ops.append(dict(eng=eng, fn=fn, deps=deps, sig=False))
        for k in w:
            self.lastw[k] = i; self.rd[k] = ({}, [])
        for k in r:
            rdk = self.rd.setdefault(k, ({}, []))
            if eng in ('sp', 'gq', 'aq'): rdk[1].append(i)
            else: rdk[0][eng] = i
        if eng in ('sp', 'aq'):
            self.last_slot[self.ndma % NDS] = i; self.ndma += 1
        elif eng == 'gq':
            self.gq_since.append(i)
        else:
            self.last_eng[eng] = i
        return i

    def emit(self, es):
        nc = self.nc; ops = self.ops
        for op in ops:
            for d in op['deps']:
                if ops[d]['eng'] == 'pe' and op['eng'] == 'pe': continue
                ops[d]['sig'] = True
        cnt = {e: 0 for e in ENGS}; di = 0; qi = 0
        for op in ops:
            e = op['eng']
            if e in ('sp', 'aq'):
                op['dsem'] = di % NDS; op['dval'] = 16 * (di // NDS + 1); di += 1
            elif e == 'gq':
                op['dsem'] = NDS + qi; op['dval'] = 16; qi += 1
            elif op['sig']:
                cnt[e] += 1; op['ord'] = cnt[e]
        esem = {e: es.enter_context(nc.semaphore("s_" + e)) for e in ENGS if e != 'sp'}
        dsem = [es.enter_context(nc.semaphore("d_%d" % i)) for i in range(NDS + qi)]
        dfinal = [0] * (NDS + qi)
        for op in ops:
            if op['eng'] in ('sp', 'gq', 'aq'): dfinal[op['dsem']] = op['dval']

        def run(e, eng):
            waited = {}

            def wait(key, sem, val):
                if waited.get(key, 0) >= val: return
                eng.wait_ge(sem, val); waited[key] = val
            for op in ops:
                oe = op['eng']
                if {'gq': 'pool', 'aq': 'act'}.get(oe, oe) != e: continue
                for d in sorted(op['deps']):
                    dop = ops[d]
                    if dop['eng'] == 'pe' and oe == 'pe': continue
                    if dop['eng'] in ('sp', 'gq', 'aq'): wait(('d', dop['dsem']), dsem[dop['dsem']], dop['dval'])
                    else: wait(dop['eng'], esem[dop['eng']], dop['ord'])
                if oe in ('sp', 'gq', 'aq'):
                    if op['dval'] > 16: wait(('d', op['dsem']), dsem[op['dsem']], op['dval'] - 16)
                    op['fn']().then_inc(dsem[op['dsem']], 16)
                else:
                    ins = op['fn']()
                    if op['sig']: ins.then_inc(esem[e], 1)
            if e == 'sp':
                for i in range(len(dfinal)):
                    if dfinal[i]: wait(('d', i), dsem[i], dfinal[i])
        block = es.enter_context(nc.Block())

        @block.sync
        def _(eng): run('sp', eng)

        @block.tensor
        def _(eng): run('pe', eng)

        @block.scalar
        def _(eng): run('act', eng)

        @block.vector
        def _(eng): run('dve', eng)

        @block.gpsimd
        def _(eng): run('pool', eng)
        print("ops:", len(ops), {e: sum(1 for o in ops if o['eng'] == e) for e in ENGS + ['gq']}, "signals:", cnt)


def build():
    nc = bass.Bass("TRN2", target_bir_lowering=False)
    es = ExitStack()
    P = Prog(nc)

    def din(name, shape): return nc.dram_tensor(name, shape, F32, kind="ExternalInput").ap()
    x = din("x", [NB, L, D]); mem = din("mem", [NB, NMEM, D])
    norm_mix = din("norm_mix", [D]); w_in = din("w_in", [D, 2056])
    fox_q_norm = din("fox_q_norm", [64]); fox_k_norm = din("fox_k_norm", [64]); fox_f_bias = din("fox_f_bias", [8])
    s5_a_re = din("s5_a_re", [32, 64]); s5_a_im = din("s5_a_im", [32, 64]); s5_log_dt = din("s5_log_dt", [32])
    s5_b_re = din("s5_b_re", [32, 64, 16]); s5_b_im = din("s5_b_im", [32, 64, 16])
    s5_c_re = din("s5_c_re", [32, 16, 64]); s5_c_im = din("s5_c_im", [32, 16, 64]); s5_d = din("s5_d", [32, 16])
    s5_w_glu = din("s5_w_glu", [512, 512]); s5_b_glu = din("s5_b_glu", [512])
    out_norm_fox = din("out_norm_fox", [512]); out_norm_s5 = din("out_norm_s5", [512]); w_out = din("w_out", [D, D])
    norm_cross = din("norm_cross", [D]); norm_mem = din("norm_mem", [D]); w_xq = din("w_xq", [D, D])
    w_xkv = din("w_xkv", [D, 2 * D]); xq_norm = din("xq_norm", [256]); xk_norm = din("xk_norm", [256])
    w_xo = din("w_xo", [D, D]); norm_ffn = din("norm_ffn", [D]); w_ffn_up = din("w_ffn_up", [D, 2 * DFF])
    ffn_conv_w = din("ffn_conv_w", [3, DFF]); ffn_conv_b = din("ffn_conv_b", [DFF]); w_ffn_down = din("w_ffn_down", [DFF, D])
    out = nc.dram_tensor("out", [NB, L, D], F32, kind="ExternalOutput").ap()
    h1_d = nc.dram_tensor("h1_d", [NB, L, D], F32, kind="Internal").ap()
    h2_d = nc.dram_tensor("h2_d", [NB, L, D], F32, kind="Internal").ap()
    hid_d = nc.dram_tensor("hid_d", [NB, 8, 128, NC_FF, 256], BF16, kind="Internal").ap()
    mix_d = nc.dram_tensor("mix_d", [NB, NT, 128, 8, 128], BF16, kind="Internal").ap()

    def sb(name, shape, dt=F32): return es.enter_context(nc.sbuf_tensor(name, shape, dt))[:]

    AW = 15580
    arena = es.enter_context(nc.sbuf_tensor("arena", [128, AW], F32))
    aoff = [0]

    def areset():
        aoff[0] = 0; P.barrier()

    def al(shape, dt=F32):
        n = 1
        for v in shape[1:]: n *= v
        nw = (n + 1) // 2 if dt == BF16 else n
        nw = (nw + 7) // 8 * 8
        assert aoff[0] + nw <= AW, ("arena overflow", aoff[0], nw)
        v = arena[:, aoff[0]:aoff[0] + nw]; aoff[0] += nw
        if dt == BF16: v = v.bitcast(BF16)
        v = v[0:shape[0], 0:n]
        if len(shape) > 2:
            names = " ".join("a%d" % i for i in range(len(shape) - 1))
            v = v.rearrange("p (%s) -> p %s" % (names, names), **{"a%d" % i: shape[i + 1] for i in range(len(shape) - 1)})
        return v

    def dma(o, i, r, w, q='sp'):
        e = nc.sync if q == 'sp' else nc.scalar
        P.add(q, lambda: e.dma_start(out=o, in_=i, allow_slow_non_contiguous=True), r, w)

    def mm(o, lhsT, rhs, start, stop, r, w, skip=False, tp=None):
        if tp is None:
            P.add('pe', lambda: nc.tensor.matmul(o, lhsT=lhsT, rhs=rhs, start=start, stop=stop, skip_group_check=skip), r, w)
        else:
            P.add('pe', lambda: nc.tensor.matmul(o, lhsT=lhsT, rhs=rhs, start=start, stop=stop, skip_group_check=skip, tile_position=tp), r, w)

    def tr(o, i, ident, r, w):
        P.add('pe', lambda: nc.tensor.transpose(o, i, ident), r, w)

    def act(o, i, func, r, w, scale=None, bias=None, accum=None):
        kw = {}
        if scale is not None: kw['scale'] = scale
        if bias is not None: kw['bias'] = bias
        if accum is not None: kw['accum_out'] = accum
        P.add('act', lambda: nc.scalar.activation(out=o, in_=i, func=func, **kw), r, w)

    DEF = ['dve']

    def tt(o, a, b, op, r, w, eng=None):
        eng = eng or DEF[0]
        e = nc.vector if eng == 'dve' else nc.gpsimd
        P.add(eng, lambda: e.tensor_tensor(out=o, in0=a, in1=b, op=op), r, w)

    def ts(o, a, s1, s2, op0, op1, r, w, eng=None):
        eng = eng or DEF[0]
        e = nc.vector if eng == 'dve' else nc.gpsimd
        if op1 is None:
            P.add(eng, lambda: e.tensor_scalar(out=o, in0=a, scalar1=s1, scalar2=None, op0=op0), r, w)
        else:
            P.add(eng, lambda: e.tensor_scalar(out=o, in0=a, scalar1=s1, scalar2=s2, op0=op0, op1=op1), r, w)

    def stt(o, a, s, b, op0, op1, r, w):
        P.add('dve', lambda: nc.vector.scalar_tensor_tensor(out=o, in0=a, scalar=s, in1=b, op0=op0, op1=op1), r, w)

    def cp(o, i, r, w, eng=None):
        eng = eng or DEF[0]
        e = nc.vector if eng == 'dve' else nc.gpsimd
        P.add(eng, lambda: e.tensor_copy(out=o, in_=i), r, w)

    def recip(o, i, r, w):
        P.add('dve', lambda: nc.vector.reciprocal(out=o, in_=i), r, w)

    def red(o, i, r, w):
        P.add('dve', lambda: nc.vector.tensor_reduce(out=o, in_=i, axis=AX.X, op=ALU.add), r, w)

    def mset(o, v, w, eng=None):
        eng = eng or DEF[0]
        e = nc.vector if eng == 'dve' else nc.gpsimd
        P.add(eng, lambda: e.memset(o, v), (), w)

    def rsqrt_chain(o, ss, n, keyr, keyw, tmp):
        ts(tmp, ss, 1.0 / n, EPS, ALU.mult, ALU.add, keyr, ['rs_tmp'])
        act(tmp, tmp, AF.Ln, ['rs_tmp'], ['rs_tmp'])
        act(o, tmp, AF.Exp, ['rs_tmp'], keyw, scale=-0.5)

    ident_f = sb("ident_f", [128, 128]); ident_b = sb("ident_b", [128, 128], BF16)
    ones_f = al([128, 128]); ones_b = sb("ones_b", [128, 128], BF16)
    tri_f = al([128, 128]); tri_b = sb("tri_b", [128, 128], BF16)
    mset(ones_f, 1.0, ['ones_f'], 'pool')
    cp(ones_b, ones_f, ['ones_f'], ['ones_b'], 'pool')
    P.add('pool', lambda: nc.gpsimd.affine_select(out=ident_f, in_=ones_f, pattern=[[1, 128]], compare_op=ALU.is_equal,
                                                  fill=0.0, base=0, channel_multiplier=-1), ['ones_f'], ['ident_f'])
    P.add('pool', lambda: nc.gpsimd.affine_select(out=tri_f, in_=ones_f, pattern=[[1, 128]], compare_op=ALU.is_ge,
                                                  fill=0.0, base=0, channel_multiplier=-1), ['ones_f'], ['tri_f'])
    cp(ident_b, ident_f, ['ident_f'], ['ident_b'], 'pool')
    cp(tri_b, tri_f, ['tri_f'], ['tri_b'], 'pool')

    PS = es.enter_context(nc.psum_tensor("PS", [128, 4096], F32))[:]
    pb = [PS[:, 512 * i:512 * (i + 1)] for i in range(8)]
    def pk(i): return 'pb%d' % i

    def colload(name, src, n):
        t = sb(name, [128, n])
        dma(t, src.rearrange("(c p) -> p c", p=128), (), [name])
        return t

    def bcload(name, src, n):
        t = sb(name, [128, n])
        dma(t, src.partition_broadcast(128), (), [name])
        return t
    g_mix = colload("g_mix", norm_mix, 8); g_cross = colload("g_cross", norm_cross, 8)
    g_mem = colload("g_mem", norm_mem, 8); g_ffn = colload("g_ffn", norm_ffn, 8)
    g_s5 = colload("g_s5", out_norm_s5, 4); b_glu = colload("b_glu", s5_b_glu, 4)
    cb = colload("cb", ffn_conv_b, NC_FF)
    cw = sb("cw", [128, 3, NC_FF])
    for k in range(3):
        dma(cw[:, k, :], ffn_conv_w[k].rearrange("(c p) -> p c", p=128), (), ['cw'])
    gq_bc = bcload("gq_bc", fox_q_norm, 64); gk_bc = bcload("gk_bc", fox_k_norm, 64)
    fb_bc = bcload("fb_bc", fox_f_bias, 8); gfox_bc = bcload("gfox_bc", out_norm_fox, 512)
    gxq_bc = bcload("gxq_bc", xq_norm, 256); gxk_bc = bcload("gxk_bc", xk_norm, 256)

    def load_w(dst, dkey, src, kc, ncols, gain=None):
        for k0 in range(0, kc, 22):
            kn = min(22, kc - k0)
            d_ = dst[:, k0:k0 + kn, :]; s_ = src[k0 * 128:(k0 + kn) * 128, :].rearrange("(c p) n -> p c n", p=128)
            P.add('gq', lambda d_=d_, s_=s_: nc.gpsimd.dma_start(out=d_, in_=s_), (), [dkey])

    aT = sb("aT", [128, 8, L], BF16)
    xsc = sb("xsc", [128, D]); xsc2p = sb("xsc2p", [128, D])
    xt = [sb("xt%d" % i, [128, D]) for i in range(2)]
    sm = sb("sm", [128, 64])
    wblk = sb("wblk", [128, 8, 1024], BF16)
    wblk2 = sb("wblk2", [128, 8, 1024], BF16)
    wglu = sb("wglu", [128, 4, 512], BF16)

    def aK(i): return 'aT:%d' % i

    stat_bufs = [xsc, xsc2p]

    def stats_chain(src_tile, skey, par=0):
        xs_ = stat_bufs[par]; c0_ = 0 if par == 0 else 16
        kss = 'ss%d' % par; kr = 'rstd%d' % par; kx = 'xsc%d' % par
        mset(sm[:, c0_:c0_ + 1], 0.0, [kss])
        act(xs_, src_tile, AF.Square, [skey], [kx, kss], accum=sm[:, c0_:c0_ + 1])
        rsqrt_chain(sm[:, c0_ + 2:c0_ + 3], sm[:, c0_:c0_ + 1], D, [kss], [kr], sm[:, c0_ + 1:c0_ + 2])
        act(xs_, src_tile, AF.Copy, [skey, kr], [kx], scale=sm[:, c0_ + 2:c0_ + 3])

    def stats_T(dstT, dkey, gcol, gkey, par=0):
        xs_ = stat_bufs[par]; kx = 'xsc%d' % par
        for hf in range(2):
            bank = 6 + hf
            for j in range(4):
                kc = hf * 4 + j
                tr(pb[bank][:, j * 128:(j + 1) * 128], xs_[:, kc * 128:(kc + 1) * 128], ident_f, [kx, 'ident_f'], [pk(bank)])
            tt(dstT[:, 4 * hf:4 * hf + 4, :], pb[bank].rearrange("p (c t) -> p c t", c=4),
               gcol[:, 4 * hf:4 * hf + 4].unsqueeze(2).to_broadcast([128, 4, 128]), ALU.mult, [pk(bank), gkey], [dkey])

    def tile_stats_T(src_tile, skey, dstT, dkey, gcol, gkey, par=0):
        stats_chain(src_tile, skey, par)
        stats_T(dstT, dkey, gcol, gkey, par)

    p1_pending = []

    def p1_tile(b, i):
        s = i % 2
        dma(xt[s], x[b, i * 128:(i + 1) * 128, :], (), ['xt%d' % s])
        stats_chain(xt[s], 'xt%d' % s, par=s)
        p1_flush()
        p1_pending.append(i)

    def p1_flush():
        while p1_pending:
            j = p1_pending.pop()
            stats_T(aT[:, :, j * 128:(j + 1) * 128], aK(j), g_mix, 'g_mix', par=j % 2)

    def p1_tiles(b):
        for i in range(NT):
            p1_tile(b, i)
        p1_flush()
    DEF[0] = 'pool'
    T2 = 128
    Mlag = sb("Mlag", [128, 4, 8, 128], BF16)
    Ast32 = sb("Ast32", [128, 4, 8, 2, 128], BF16)
    Qst32 = sb("Qst32", [128, 16, 8, 2, 32], BF16)
    E8r = sb("E8r", [128, 16, T2 + 1]); E8i = sb("E8i", [128, 16, T2 + 1])
    r8 = sb("r8", [128, 16]); dcol = sb("dcol", [128, 4]); carry = sb("carry", [128, 16, 2])
    dma(dcol, s5_d.rearrange("(o g) c -> (g c) o", o=4), (), ['dcol'])
    load_w(wglu, 'wglu', s5_w_glu, 4, 512)

    are = al([128, 16]); aim = al([128, 16]); ldt = al([128, 16])
    for h in range(2):
        dma(are[h * 64:(h + 1) * 64, :], s5_a_re.rearrange("(q h) p -> h p q", h=2)[h], (), ['are'])
        dma(aim[h * 64:(h + 1) * 64, :], s5_a_im.rearrange("(q h) p -> h p q", h=2)[h], (), ['aim'])
        dma(ldt[h * 64:(h + 1) * 64, :], s5_log_dt.rearrange("(q h) -> h q", h=2)[h].partition_broadcast(64), (), ['ldt'])
    bre = al([128, 16, 16]); bim = al([128, 16, 16])
    for h in range(2):
        dma(bre[h * 64:(h + 1) * 64], s5_b_re.rearrange("(q h) p c -> h p q c", h=2)[h], (), ['bre'])
        dma(bim[h * 64:(h + 1) * 64], s5_b_im.rearrange("(q h) p c -> h p q c", h=2)[h], (), ['bim'])
    craw = [al([128, 4, 2, 64]) for k in range(2)]
    for k, src in enumerate((s5_c_re, s5_c_im)):
        for dup in range(2):
            dma(craw[k][:, :, dup, :], src.rearrange("(o g) c p -> (g c) o p", o=4), (), ['craw%d' % k])
    cre = al([128, 16, 16]); cim = al([128, 16, 16])
    for k, dst in enumerate((cre, cim)):
        for o in range(4):
            tr(pb[0][:, 0:128], craw[k][:, o].rearrange("p a b -> p (a b)"), ident_f, ['craw%d' % k, 'ident_f'], [pk(0)])
            v = pb[0][:, 0:128].rearrange("p (q h c) -> p q h c", q=4, h=2)
            for h in range(2):
                cp(dst[h * 64:(h + 1) * 64, o * 4:(o + 1) * 4, :], v[h * 64:(h + 1) * 64, :, h, :], [pk(0)], ['cre' if k == 0 else 'cim'], 'dve')
    S = al([128, 24, 16])
    def pl(i): return S[:, i, :]
    K5 = ['s5s']
    dt_ = pl(0); rho = pl(1); th = pl(2); mag = pl(3); cs = pl(4); sn = pl(5); lbr = pl(6); lbi = pl(7)
    den = pl(8); nr = pl(9); cfr = pl(10); cfi = pl(11); t1 = pl(12); t2 = pl(13); inv8 = pl(14)
    act(dt_, ldt, AF.Exp, ['ldt'], K5)
    tt(rho, are, dt_, ALU.mult, K5 + ['are'], K5)
    tt(th, aim, dt_, ALU.mult, K5 + ['aim'], K5)
    act(mag, rho, AF.Exp, K5, K5)
    act(r8, rho, AF.Exp, K5, ['r8'], scale=8.0)
    recip(inv8, r8, ['r8'], K5)
    MAGIC = 12582912.0
    TWO_PI = 2.0 * math.pi

    def sincos(o_s, o_c, ang, tmp1, tmp2, keys):
        for (o, off) in ((o_s, 0.0), (o_c, math.pi / 2)):
            ts(tmp1, ang, off, 1.0 / TWO_PI, ALU.add, ALU.mult, keys, keys)
            ts(tmp2, tmp1, MAGIC, None, ALU.add, None, keys, keys)
            ts(tmp2, tmp2, MAGIC, None, ALU.subtract, None, keys, keys)
            tt(tmp1, tmp1, tmp2, ALU.subtract, keys, keys)
            ts(tmp1, tmp1, TWO_PI, None, ALU.mult, None, keys, keys)
            ts(tmp1, tmp1, math.pi, -math.pi, ALU.min, ALU.max, keys, keys)
            act(o, tmp1, AF.Sin, keys, keys)
    sincos(sn, cs, th, t1, t2, K5)
    tt(lbr, mag, cs, ALU.mult, K5, K5); tt(lbi, mag, sn, ALU.mult, K5, K5)
    tt(den, are, are, ALU.mult, ['are'] + K5, K5); tt(t1, aim, aim, ALU.mult, ['aim'] + K5, K5)
    tt(den, den, t1, ALU.add, K5, K5); recip(den, den, K5, K5)
    ts(nr, lbr, -1.0, None, ALU.add, None, K5, K5)
    tt(t1, nr, are, ALU.mult, K5, K5); tt(t2, lbi, aim, ALU.mult, K5, K5); tt(cfr, t1, t2, ALU.add, K5, K5)
    tt(cfr, cfr, den, ALU.mult, K5, K5)
    tt(t1, lbi, are, ALU.mult, K5, K5); tt(t2, nr, aim, ALU.mult, K5, K5); tt(cfi, t1, t2, ALU.subtract, K5, K5)
    tt(cfi, cfi, den, ALU.mult, K5, K5)
    bbr = al([128, 16, 16]); bbi = al([128, 16, 16]); tb1 = al([128, 16, 16]); nbbi = al([128, 16, 16])
    KB = ['bb']
    cfr_b = cfr.unsqueeze(2).to_broadcast([128, 16, 16]); cfi_b = cfi.unsqueeze(2).to_broadcast([128, 16, 16])
    tt(bbr, bre, cfr_b, ALU.mult, K5 + ['bre'], KB); tt(tb1, bim, cfi_b, ALU.mult, K5 + ['bim'], KB)
    tt(bbr, bbr, tb1, ALU.subtract, KB, KB)
    tt(bbi, bim, cfr_b, ALU.mult, K5 + ['bim'], KB); tt(tb1, bre, cfi_b, ALU.mult, K5 + ['bre'], KB)
    tt(bbi, bbi, tb1, ALU.add, KB, KB)
    ts(nbbi, bbi, -1.0, None, ALU.mult, None, KB, ['nbbi'])

    def cdouble(Tr, Ti, nmax, tA, tB, key, tkey):
        n = 1
        while n < nmax:
            m = min(n, nmax - n)
            ar = Tr[:, :, 1:1 + m]; ai = Ti[:, :, 1:1 + m]
            br_ = Tr[:, :, n:n + 1].to_broadcast([128, 16, m]); bi_ = Ti[:, :, n:n + 1].to_broadcast([128, 16, m])
            tt(tA[:, :, 0:m], ar, br_, ALU.mult, [key], [tkey]); tt(tB[:, :, 0:m], ai, bi_, ALU.mult, [key], [tkey])
            tt(Tr[:, :, n + 1:n + 1 + m], tA[:, :, 0:m], tB[:, :, 0:m], ALU.subtract, [tkey], [key])
            tt(tA[:, :, 0:m], ar, bi_, ALU.mult, [key], [tkey]); tt(tB[:, :, 0:m], ai, br_, ALU.mult, [key], [tkey])
            tt(Ti[:, :, n + 1:n + 1 + m], tA[:, :, 0:m], tB[:, :, 0:m], ALU.add, [tkey], [key])
            n += m
    Et1 = al([128, 16, 64]); Et2 = al([128, 16, 64])
    Lr = al([128, 16, 9]); Li = al([128, 16, 9])
    mset(Lr[:, :, 0:1], 1.0, ['L']); mset(Li[:, :, 0:1], 0.0, ['L'])
    cp(Lr[:, :, 1:2], lbr.unsqueeze(2), K5, ['L']); cp(Li[:, :, 1:2], lbi.unsqueeze(2), K5, ['L'])
    cdouble(Lr, Li, 8, Et1, Et2, 'L', 'Et')
    KE = ['E8']
    mset(E8r[:, :, 0:1], 1.0, KE); mset(E8i[:, :, 0:1], 0.0, KE)
    tt(E8r[:, :, 1:2], Lr[:, :, 8:9], inv8.unsqueeze(2), ALU.mult, ['L'] + K5, KE)
    tt(E8i[:, :, 1:2], Li[:, :, 8:9], inv8.unsqueeze(2), ALU.mult, ['L'] + K5, KE)
    Gi = al([8, 128]); bdmask = al([128, 128]); mtmp = al([128, 128])
    mset(Gi, 1.0, ['Gi'], 'pool')
    P.add('pool', lambda: nc.gpsimd.affine_select(out=Gi, in_=Gi, pattern=[[1, 128]], compare_op=ALU.is_ge, fill=0.0, base=0,
                                                  channel_multiplier=-16), ['Gi'], ['Gi'])
    P.add('pool', lambda: nc.gpsimd.affine_select(out=Gi, in_=Gi, pattern=[[-1, 128]], compare_op=ALU.is_ge, fill=0.0, base=15,
                                                  channel_multiplier=16), ['Gi'], ['Gi'])
    mm(pb[1][:, 0:128], Gi, Gi, True, True, ['Gi'], [pk(1)])
    cp(bdmask, pb[1][:, 0:128], [pk(1)], ['bdmask'], 'dve')
    SOB = al([128, 4, 2, 128]); SOC = al([128, 4, 2, 128])

    def fill_SO(dst, dkey, srcs, keys):
        v = dst.rearrange("p o r (q h c) -> p o r q h c", q=4, h=2)
        for ri, src in enumerate(srcs):
            sv = src.rearrange("p (o q) c -> p o q c", q=4)
            for h in range(2):
                cp(v[h * 64:(h + 1) * 64, :, ri, :, h, :], sv[h * 64:(h + 1) * 64], keys, [dkey])
    mset(SOB, 0.0, ['SOB']); mset(SOC, 0.0, ['SOC']); mset(Qst32, 0.0, ['Qst32'])
    fill_SO(SOB, 'SOB', (bbr, nbbi), KB + ['nbbi'])
    DEF[0] = 'dve'
    SOA = al([128, 4, 2, 128]); LBr = al([128, 16, 16]); LBi = al([128, 16, 16]); LBt = al([128, 16, 16])
    mset(SOA, 0.0, ['SOA'])
    for s_ in range(8):
        k_ = 7 - s_
        lr_b = Lr[:, :, k_:k_ + 1].to_broadcast([128, 16, 16]); li_b = Li[:, :, k_:k_ + 1].to_broadcast([128, 16, 16])
        KC = ['LB']
        tt(LBr, bbr, lr_b, ALU.mult, KB + ['L'], KC); tt(LBt, bbi, li_b, ALU.mult, KB + ['L'], KC); tt(LBr, LBr, LBt, ALU.subtract, KC, KC)
        tt(LBi, bbi, lr_b, ALU.mult, KB + ['L'], KC); tt(LBt, bbr, li_b, ALU.mult, KB + ['L'], KC); tt(LBi, LBi, LBt, ALU.add, KC, KC)
        fill_SO(SOA, 'SOA', (LBr, LBi), KC)
        for o in range(4):
            for ri in range(2):
                tr(pb[1][:, 0:128], SOA[:, o, ri, :], ident_f, ['SOA', 'ident_f'], [pk(1)])
                cp(Ast32[:, o, s_, ri, :], pb[1][:, 0:128], [pk(1)], ['Ast32'], 'dve')

    cdouble(E8r, E8i, T2, Et1, Et2, 'E8', 'Et')
    DEF[0] = 'pool'
    CLr = al([128, 16, 16]); CLi = al([128, 16, 16]); CLt = al([128, 16, 16])
    for tau in range(9):
        if tau < 8:
            DEF[0] = 'dve'
            p1_tile(0, 2 * tau); p1_tile(0, 2 * tau + 1)
            if tau == 7: p1_flush()
            DEF[0] = 'pool'
        lr_b = Lr[:, :, tau:tau + 1].to_broadcast([128, 16, 16]); li_b = Li[:, :, tau:tau + 1].to_broadcast([128, 16, 16])
        KC = ['CL']
        tt(CLr, cre, lr_b, ALU.mult, ['cre', 'L'], KC); tt(CLt, cim, li_b, ALU.mult, ['cim', 'L'], KC); tt(CLr, CLr, CLt, ALU.subtract, KC, KC)
        tt(CLi, cre, li_b, ALU.mult, ['cre', 'L'], KC); tt(CLt, cim, lr_b, ALU.mult, ['cim', 'L'], KC); tt(CLi, CLi, CLt, ALU.add, KC, KC)
        if tau >= 1:
            for h in range(2):
                cp(Qst32[h * 64:(h + 1) * 64, :, tau - 1, 0, 16 * h:16 * h + 16], CLr[h * 64:(h + 1) * 64], KC, ['Qst32'])
                ts(Qst32[h * 64:(h + 1) * 64, :, tau - 1, 1, 16 * h:16 * h + 16], CLi[h * 64:(h + 1) * 64], -1.0, None, ALU.mult, None, KC, ['Qst32'])
        if tau <= 7:
            fill_SO(SOC, 'SOC', (CLr, CLi), KC)
            for o in range(4):
                bk = (0, 2, 3, 4, 5)[(tau * 4 + o) % 5]
                for ri in range(2):
                    mm(pb[bk][:, 0:128], SOB[:, o, ri, :], SOC[:, o, ri, :], ri == 0, ri == 1, ['SOB', 'SOC'], [pk(bk)])
                if tau == 0:
                    tt(mtmp, pb[bk][:, 0:128], bdmask, ALU.mult, [pk(bk), 'bdmask'], ['mtmp'], 'dve')
                    stt(Mlag[:, o, tau, :], ident_f, dcol[:, o:o + 1], mtmp, ALU.mult, ALU.add, ['ident_f', 'dcol', 'mtmp'], ['Mlag'])
                else:
                    tt(Mlag[:, o, tau, :], pb[bk][:, 0:128], bdmask, ALU.mult, [pk(bk), 'bdmask'], ['Mlag'], 'dve')
    DEF[0] = 'dve'
    HG = 2
    gcount = [0]; itc = [0]
    for b in range(NB):
        areset()
        if b > 0:
            p1_tiles(b)

        uT = al([128, 4, L], BF16); gyT = al([128, 4, L], BF16)
        y2T = al([128, 4, 512], BF16); sqT = al([128, 4, 512], BF16)
        Xbufs = [al([128, 4, 2, 257], BF16) for _ in range(2)]
        vre = al([128, 4, T2]); vim = al([128, 4, T2]); wre = al([128, 4, T2]); wim = al([128, 4, T2])
        c1 = al([128, 4, 1]); c2 = al([128, 4, 1]); c3 = al([128, 4, 1]); c4 = al([128, 4, 1])
        pA = sqT[:, 0, :].rearrange("p (q j) -> p q j", q=4); pB = sqT[:, 1, :].rearrange("p (q j) -> p q j", q=4)
        ytmp = al([128, 512]); ytmp2 = al([128, 512])
        for o in range(4):
            wb = wblk if o % 2 == 0 else wblk2; wkk = 'wblk' if o % 2 == 0 else 'wblk2'
            load_w(wb[:, :, 0:128], wkk, w_in[:, 1544 + o * 128:1544 + (o + 1) * 128], 8, 128)
            for tb in range(4):
                bank = tb % 2
                for kc in range(8):
                    mm(pb[bank], wb[:, kc, 0:128], aT[:, kc, tb * 512:(tb + 1) * 512], kc == 0, kc == 7,
                       [wkk] + [aK(4 * tb + j) for j in range(4)], [pk(bank)])
                act(uT[:, o, tb * 512:(tb + 1) * 512], pb[bank], AF.Copy, [pk(bank)], ['uT:%d' % o])

        load_w(wblk[:, :, 0:HG * 64], 'wblk', w_in[:, 0:HG * 64], 8, HG * 64)
        load_w(wblk[:, :, HG * 64:2 * HG * 64], 'wblk', w_in[:, 512:512 + HG * 64], 8, HG * 64)
        load_w(wblk[:, :, 2 * HG * 64:3 * HG * 64], 'wblk', w_in[:, 1024:1024 + HG * 64], 8, HG * 64)
        load_w(wblk[:, :, 768:776], 'wblk', w_in[:, 1536:1544], 8, 8)
        mset(carry, 0.0, ['carry'])
        Vv = PS[:, 1024:3072].rearrange("p (q x) -> p q x", q=4)[:, :, 0:2 * T2].rearrange("p q (r j) -> p q r j", r=2)
        VK = [pk(2), pk(3), pk(4), pk(5)]
        def s5_state(o, jh):
            uo = uT[:, o, :].rearrange("p (j s) -> p j s", s=8)
            uk = ['uT:%d' % o]
            Xbuf = Xbufs[o % 2]; xk_ = 'Xbuf%d' % (o % 2)
            if jh == 0:
                mset(Xbuf[:, :, :, 0:1], 0.0, [xk_])
            j0 = jh * T2
            mset(Vv, 0.0, VK)
            for s_ in range(8):
                for ri in range(2):
                    for q in range(4):
                        mm(Vv[:, q, ri, :], Ast32[32 * q:32 * q + 32, o, s_, ri, :], uo[32 * q:32 * q + 32, j0:j0 + T2, s_], False, s_ == 7,
                           ['Ast32'] + uk, VK, skip=True, tp=(32 * q, 0))
            return Xbuf, xk_, j0

        def s5_rot(o, jh, Xbuf, xk_, j0):
            er = E8r[:, 4 * o:4 * o + 4, 0:T2]; ei = E8i[:, 4 * o:4 * o + 4, 0:T2]
            Vr = Vv[:, :, 0, :]; Vi = Vv[:, :, 1, :]
            KS = ['s5w']
            tt(vre, Vr, er, ALU.mult, VK + ['E8'], ['vre']); tt(vim, Vi, er, ALU.mult, VK + ['E8'], ['vim'])
            tt(wre, Vi, ei, ALU.mult, VK + ['E8'], ['wre']); tt(wim, Vr, ei, ALU.mult, VK + ['E8'], ['wim'])
            tt(vre, vre, wre, ALU.add, ['vre', 'wre'], ['vre']); tt(vim, vim, wim, ALU.subtract, ['vim', 'wim'], ['vim'])
            for q in range(4):
                pr = 4 * o + q
                rb = r8[:, pr:pr + 1].to_broadcast([128, T2])
                for (w_, v_, ci_, wk_, vk_) in ((wre, vre, 0, 'wre', 'vre'), (wim, vim, 1, 'wim', 'vim')):
                    P.add('dve', lambda rb=rb, pr=pr, w_=w_, v_=v_, ci_=ci_, q=q: nc.vector.tensor_tensor_scan(
                        out=w_[:, q, :], data0=rb, data1=v_[:, q, :], initial=carry[:, pr, ci_:ci_ + 1], op0=ALU.mult, op1=ALU.add),
                        [vk_, 'carry', 'r8'], [wk_])
            xo_r = Xbuf[:, :, 0, 1 + j0:1 + j0 + T2]; xo_i = Xbuf[:, :, 1, 1 + j0:1 + j0 + T2]
            wl_r = wre[:, :, T2 - 1:T2]; wl_i = wim[:, :, T2 - 1:T2]
            eTr = E8r[:, 4 * o:4 * o + 4, T2:T2 + 1]; eTi = E8i[:, 4 * o:4 * o + 4, T2:T2 + 1]
            tt(c1, wl_r, eTr, ALU.mult, ['wre', 'E8'], ['c1']); tt(c2, wl_i, eTi, ALU.mult, ['wim', 'E8'], ['c2'])
            tt(c3, wl_i, eTr, ALU.mult, ['wim', 'E8'], ['c3']); tt(c4, wl_r, eTi, ALU.mult, ['wre', 'E8'], ['c4'])
            tt(carry[:, 4 * o:4 * o + 4, 0:1], c1, c2, ALU.subtract, ['c1', 'c2'], ['carry'])
            tt(carry[:, 4 * o:4 * o + 4, 1:2], c3, c4, ALU.add, ['c3', 'c4'], ['carry'])
            tt(pA, wre, er, ALU.mult, ['wre', 'E8'], ['sqT:0'], 'pool'); tt(pB, wim, ei, ALU.mult, ['wim', 'E8'], ['sqT:1'], 'pool')
            tt(xo_r, pA, pB, ALU.subtract, ['sqT:0', 'sqT:1'], [xk_], 'pool')
            tt(pA, wre, ei, ALU.mult, ['wre', 'E8'], ['sqT:0'], 'pool'); tt(pB, wim, er, ALU.mult, ['wim', 'E8'], ['sqT:1'], 'pool')
            tt(xo_i, pA, pB, ALU.add, ['sqT:0', 'sqT:1'], [xk_], 'pool')

        def s5_Y(o, t):
            uo = uT[:, o, :].rearrange("p (j s) -> p j s", s=8)
            uk = ['uT:%d' % o]
            Xbuf = Xbufs[o % 2]; xk_ = 'Xbuf%d' % (o % 2)
            gyo = gyT[:, o, :].rearrange("p (j s) -> p j s", s=8)
            yb = t % 2
            for s_ in range(t + 1):
                mm(pb[yb][:, 0:256], Mlag[:, o, t - s_, :], uo[:, :, s_], s_ == 0, False, ['Mlag'] + uk, [pk(yb)])
            for q in range(4):
                for ri in range(2):
                    mm(pb[yb][32 * q:32 * q + 32, 0:256], Qst32[:, 4 * o + q, t, ri, :], Xbuf[:, q, ri, 0:256], False, ri == 1,
                       ['Qst32', xk_], [pk(yb)], tp=(0, 32 * q))
            yf = ytmp[:, (t % 2) * 256:(t % 2) * 256 + 256]; yq = ytmp2[:, (t % 2) * 256:(t % 2) * 256 + 256]
            ky = 'yf%d' % (t % 2); kq = 'yq%d' % (t % 2)
            act(yf, pb[yb][:, 0:256], AF.Copy, [pk(yb)], [ky])
            tt(yq, yf, yf, ALU.mult, [ky], [kq])
            ts(yq, yq, 0.044715, 1.0, ALU.mult, ALU.add, [kq], [kq])
            tt(yq, yq, yf, ALU.mult, [ky, kq], [kq])
            act(yq, yq, AF.Sigmoid, [kq], [kq], scale=1.5957691216)
            tt(gyo[:, :, t], yf, yq, ALU.mult, [ky, kq], ['gyT:%d' % o], 'pool')
        for o in range(5):
            for jh in range(2):
                if o < 4:
                    st = s5_state(o, jh)
                if o >= 1:
                    for t in range(4 * jh, 4 * jh + 4):
                        s5_Y(o - 1, t)
                if o < 4:
                    s5_rot(o, jh, *st)

        for tb in range(4):
            cols = slice(tb * 512, (tb + 1) * 512)
            gk = ['gyT:%d' % j for j in range(4)]
            for nch in range(4):
                bank = nch % 2
                for kc in range(4):
                    mm(pb[bank], wglu[:, kc, nch * 128:(nch + 1) * 128], gyT[:, kc, cols], kc == 0, kc == 3, ['wglu'] + gk, [pk(bank)])
                sg_ = (ytmp, vre.rearrange("p q j -> p (q j)"))[nch % 2]; sgk = ('ytmp', 'vre')[nch % 2]
                act(sg_, pb[bank], AF.Sigmoid, [pk(bank), 'b_glu'], [sgk], bias=b_glu[:, nch:nch + 1])
                tt(y2T[:, nch, :], gyT[:, nch, cols], sg_, ALU.mult, gk + [sgk], ['y2T:%d' % nch])
                tt(sqT[:, nch, :], y2T[:, nch, :], y2T[:, nch, :], ALU.mult, ['y2T:%d' % nch], ['sqT:%d' % nch])
            for nch in range(4):
                mm(pb[5], ones_b, sqT[:, nch, :], nch == 0, nch == 3, ['ones_b', 'sqT:%d' % nch], [pk(5)])
            ts(ytmp2, pb[5], 1.0 / 512, EPS, ALU.mult, ALU.add, [pk(5)], ['ytmp2'])
            act(ytmp2, ytmp2, AF.Sqrt, ['ytmp2'], ['ytmp2'])
            recip(ytmp2, ytmp2, ['ytmp2'], ['ytmp2'])
            for nch in range(4):
                stt(sqT[:, nch, :], y2T[:, nch, :], g_s5[:, nch:nch + 1], ytmp2, ALU.mult, ALU.mult, ['y2T:%d' % nch, 'g_s5', 'ytmp2', 'sqT:%d' % nch], ['sqT:%d' % nch])
                dma(mix_d[b, 4 * tb:4 * tb + 4, :, 4 + nch, :].rearrange("j p t -> p j t"), sqT[:, nch, :].rearrange("p (j t) -> p j t", j=4),
                    ['sqT:%d' % nch], ['mixd:%d' % tb])

        areset()
        QT = al([68, HG, L], BF16); KT = al([68, HG, L], BF16)
        Vp = al([128, NT, HG, 65], BF16)
        fox = al([128, NT, 512], BF16)
        caug = al([128, NT, 8, 2], BF16); ncaug = al([128, NT, 8, 2], BF16)
        lfs = al([128, NT, 3, 8], BF16)
        TG = 4
        qk_sb = al([128, TG, 2 * HG, 64]); qk_sq = al([128, TG, 2 * HG, 64])
        aug = [al([128, TG, 2 * HG, 68], BF16) for _ in range(2)]
        gqk = al([128, 2 * HG, 64])
        fzA = al([128, NT, 8]); fzB = al([128, NT, 8]); fzC = al([128, NT, 8]); ones_f16 = al([128, NT])
        rsq = al([128, TG, 2 * HG]); rsq2 = al([128, TG, 2 * HG])
        mset(ones_f16, 1.0, ['ones_f16'])
        PT = [al([128, 512], BF16) for _ in range(3)]
        foxn = [al([128, 512], BF16) for _ in range(2)]; foxT = [al([128, 4, 128], BF16) for _ in range(2)]
        fss = al([128, NT]); frs = al([128, NT])
        for a in range(2):
            mset(aug[a], 1.0, ['aug%d' % a])
        for hh in range(HG):
            cp(gqk[:, hh, :], gq_bc, ['gq_bc'], ['gqk']); cp(gqk[:, HG + hh, :], gk_bc, ['gk_bc'], ['gqk'])
        pbt2 = PS[:, 2048:3072].bitcast(BF16)
        for hg in range(8 // HG):
            h0 = hg * HG
            wq = wblk[:, :, 0:HG * 64]; wk = wblk[:, :, HG * 64:2 * HG * 64]; wv = wblk[:, :, 2 * HG * 64:3 * HG * 64]; wf = wblk[:, :, 768:776]
            if hg > 0:
                load_w(wq, 'wblk', w_in[:, h0 * 64:(h0 + HG) * 64], 8, HG * 64)
                load_w(wk, 'wblk', w_in[:, 512 + h0 * 64:512 + (h0 + HG) * 64], 8, HG * 64)
                load_w(wv, 'wblk', w_in[:, 1024 + h0 * 64:1024 + (h0 + HG) * 64], 8, HG * 64)
            if hg == 0:
                load_w(wblk2, 'wblk2', w_out, 8, 1024)
            mset(Vp, 1.0, ['Vp:%d' % i for i in range(NT)])
            if hg == 0:
                ALLK = [aK(i) for i in range(NT)]
                for i in range(NT):
                    for kc in range(8):
                        mm(pb[7][:, i * 8:(i + 1) * 8], aT[:, kc, i * 128:(i + 1) * 128], wf[:, kc, :], kc == 0, kc == 7, ['wblk', aK(i)], [pk(7)])
                tt(fzA, pb[7][:, 0:NT * 8].rearrange("p (t h) -> p t h", t=NT), fb_bc.unsqueeze(1).to_broadcast([128, NT, 8]), ALU.add,
                   [pk(7), 'fb_bc'], ['fzA'])
                act(fzA, fzA, AF.Exp, ['fzA'], ['fzA'], scale=-1.0)
                act(fzA, fzA, AF.Ln, ['fzA'], ['fzA'], bias=1.0)
                cp(lfs[:, :, 0, :], fzA, ['fzA'], ['lfs'])
                tt(fzB, fzA, lfs[:, :, 0, :], ALU.subtract, ['fzA', 'lfs'], ['fzB'])
                cp(lfs[:, :, 1, :], fzB, ['fzB'], ['lfs'])
                tt(fzC, fzB, lfs[:, :, 1, :], ALU.subtract, ['fzB', 'lfs'], ['fzC'])
                cp(lfs[:, :, 2, :], fzC, ['fzC'], ['lfs'])
                mm(pb[6][:, 0:NT * 24], ones_b, lfs.rearrange("p t k h -> p (t k h)"), True, True, ['ones_b', 'lfs'], [pk(6)])
                red(fzB, pb[6][:, 0:NT * 24].rearrange("p (t k h) -> p t h k", t=NT, k=3), [pk(6)], ['fzB'])
                for h in range(8):
                    P.add('dve', lambda h=h: nc.vector.tensor_tensor_scan(out=fzC[:, :, h], data0=ones_f16, data1=fzB[:, :, h], initial=0.0,
                                                                          op0=ALU.mult, op1=ALU.add), ['fzB', 'ones_f16'], ['fzC'])
                tt(fzC, fzC, fzB, ALU.subtract, ['fzC', 'fzB'], ['fzC'])
                for part in range(3):
                    mm(pb[1][:, 0:NT * 8], tri_b, lfs[:, :, part, :], part == 0, part == 2, ['tri_b', 'lfs'], [pk(1)])
                tt(fzA, pb[1][:, 0:NT * 8].rearrange("p (t h) -> p t h", t=NT), fzC, ALU.add, [pk(1), 'fzC'], ['fzA'])
                ts(fzA, fzA, -8.0, None, ALU.mult, None, ['fzA'], ['fzA'])
                CK = ['caug:%d' % i for i in range(NT)]; NCK = ['ncaug:%d' % i for i in range(NT)]
                cp(caug[:, :, :, 0], fzA, ['fzA'], CK)
                tt(caug[:, :, :, 1], fzA, caug[:, :, :, 0], ALU.subtract, ['fzA'] + CK, CK)
                ts(ncaug, caug, -1.0, None, ALU.mult, None, CK, NCK)
            W_ = HG * 64

            PBK = (0, 1, 6, 7)

            def proj_mm(gi):
                for tl, i in enumerate(range(gi * TG, (gi + 1) * TG)):
                    tcols = slice(i * 128, (i + 1) * 128)
                    for kc in range(8):
                        mm(pb[PBK[tl]][:, 0:3 * W_], aT[:, kc, tcols], wblk[:, kc, 0:3 * W_], kc == 0, kc == 7, ['wblk', aK(i)], [pk(PBK[tl])])
            proj_mm(0)
            for gi in range(NT // TG):
                tiles = range(gi * TG, (gi + 1) * TG)
                for half, (b0, b1) in enumerate(((0, 1), (6, 7))):
                    pv_ = PS[:, b0 * 512:(b1 + 1) * 512].rearrange("p (t c) -> p t c", t=2)
                    tsl = slice(2 * half, 2 * half + 2)
                    act(qk_sb[:, tsl, 0:HG, :], pv_[:, :, 0:W_].rearrange("p t (h d) -> p t h d", h=HG), AF.Copy, [pk(b0), pk(b1)], ['qk_sb'])
                    act(qk_sb[:, tsl, HG:2 * HG, :], pv_[:, :, W_:2 * W_].rearrange("p t (h d) -> p t h d", h=HG), AF.Copy, [pk(b0), pk(b1)], ['qk_sb'])
                    act(Vp[:, gi * TG + 2 * half:gi * TG + 2 * half + 2, :, 0:64], pv_[:, :, 2 * W_:3 * W_].rearrange("p t (h d) -> p t h d", h=HG), AF.Copy,
                        [pk(b0), pk(b1)], ['Vp:%d' % i for i in tiles])
                if gi + 1 < NT // TG:
                    proj_mm(gi + 1)
                tt(qk_sq, qk_sb, qk_sb, ALU.mult, ['qk_sb'], ['qk_sq'])
                red(rsq, qk_sq, ['qk_sq'], ['rsq'])
                rsqrt_chain(rsq2, rsq, 64, ['rsq'], ['rsq2'], rsq)
                tt(qk_sq, qk_sb, rsq2.unsqueeze(3).to_broadcast([128, TG, 2 * HG, 64]), ALU.mult, ['qk_sb', 'rsq2'], ['qk_sq'])
                ag = aug[gi % 2]; agk = 'aug%d' % (gi % 2)
                tt(ag[:, :, :, 0:64], qk_sq, gqk.unsqueeze(1).to_broadcast([128, TG, 2 * HG, 64]), ALU.mult, ['qk_sq', 'gqk'], [agk])
                cp(ag[:, :, 0:HG, 64:66], caug[:, gi * TG:(gi + 1) * TG, h0:h0 + HG, :], ['caug:%d' % i for i in tiles], [agk])
                cp(ag[:, :, HG:2 * HG, 66:68], ncaug[:, gi * TG:(gi + 1) * TG, h0:h0 + HG, :], ['ncaug:%d' % i for i in tiles], [agk])
                for tl in range(TG):
                    for j in range(2 * HG):
                        blk = tl * 2 * HG + j
                        tr(pbt2[0:68, blk * 128:(blk + 1) * 128], ag[:, tl, j, :], ident_b, [agk, 'ident_b'], [pk(4), pk(5)])
                pv = pbt2[0:68, :].rearrange("p (t j x) -> p j t x", t=TG, j=2 * HG)
                gcols = slice(gi * TG * 128, (gi + 1) * TG * 128)
                cp(QT[:, :, gcols].rearrange("p h (t x) -> p h t x", t=TG), pv[:, 0:HG], [pk(4), pk(5)], ['QT:%d' % i for i in tiles])
                cp(KT[:, :, gcols].rearrange("p h (t x) -> p h t x", t=TG), pv[:, HG:2 * HG], [pk(4), pk(5)], ['KT:%d' % i for i in tiles])
            for hh in range(HG):
                h = h0 + hh
                for qg in range(4):
                    ob = 4 + (gcount[0] % 2); gcount[0] += 1
                    mset(pb[ob], 0.0, [pk(ob)])
                    its = []
                    for kt in range(4 * qg + 4):
                        q0 = max(kt, 4 * qg); N = (4 * qg + 4 - q0) * 128
                        its.append((kt, q0, N, (2, 3, 6)[itc[0] % 3], PT[itc[0] % 3], 'PT%d' % (itc[0] % 3))); itc[0] += 1

                    def qk(itm):
                        kt, q0, N, sbk, ptt, ptk = itm
                        mm(pb[sbk][:, 0:N], KT[:, hh, kt * 128:(kt + 1) * 128], QT[:, hh, q0 * 128:(4 * qg + 4) * 128], True, True,
                           ['KT:%d' % kt] + ['QT:%d' % j for j in range(q0, 4 * qg + 4)], [pk(sbk)])
                    qk(its[0])
                    if len(its) > 1: qk(its[1])
                    for n_, itm in enumerate(its):
                        kt, q0, N, sbk, ptt, ptk = itm
                        if n_ + 2 < len(its): qk(its[n_ + 2])
                        act(ptt[:, 0:N], pb[sbk][:, 0:N], AF.Exp, [pk(sbk)], [ptk], scale=0.125)
                        if kt >= 4 * qg:
                            tt(ptt[:, 0:128], ptt[:, 0:128], tri_b, ALU.mult, [ptk, 'tri_b'], [ptk])
                        for qb in range(q0, 4 * qg + 4):
                            j = qb - 4 * qg
                            mm(pb[ob][:, j * 128:j * 128 + 65], ptt[:, (qb - q0) * 128:(qb - q0 + 1) * 128], Vp[:, kt, hh, :],
                               False, kt == qb, [ptk, 'Vp:%d' % kt], [pk(ob)], skip=True)
                    ov = pb[ob].rearrange("p (j c) -> p j c", j=4)
                    recip(sm[:, 12:16], ov[:, :, 64], [pk(ob)], ['sml'])
                    tt(fox[:, 4 * qg:4 * qg + 4, h * 64:(h + 1) * 64], ov[:, :, 0:64], sm[:, 12:16].unsqueeze(2).to_broadcast([128, 4, 64]),
                       ALU.mult, [pk(ob), 'sml'], ['fox:%d' % qg])
        for i in range(NT):
            fk = 'fox:%d' % (i // 4)
            jn = foxn[i % 2]; jk = 'foxn%d' % (i % 2)
            tt(jn, fox[:, i, :], fox[:, i, :], ALU.mult, [fk], [jk])
            red(fss[:, i:i + 1], jn, [jk], ['fss'])
        rsqrt_chain(frs, fss, 512, ['fss'], ['frs'], fss)
        def fox_norm(i):
            stt(foxn[i % 2], fox[:, i, :], frs[:, i:i + 1], gfox_bc, ALU.mult, ALU.mult, ['fox:%d' % (i // 4), 'frs', 'gfox_bc'], ['foxn%d' % (i % 2)])
        fox_norm(0)
        for i in range(NT):
            fk = 'fox:%d' % (i // 4); p = i % 2
            if i + 1 < NT: fox_norm(i + 1)
            pbt = pb[4 + p].bitcast(BF16)
            for j in range(4):
                tr(pbt[:, j * 128:(j + 1) * 128], foxn[p][:, j * 128:(j + 1) * 128], ident_b, ['foxn%d' % p, 'ident_b'], [pk(4 + p)])
            cp(foxT[p], pbt[:, 0:512].rearrange("p (c t) -> p c t", c=4), [pk(4 + p)], ['foxT%d' % p])
            dma(mix_d[b, i, :, 0:4, :], foxT[p], ['foxT%d' % p], ['mixf:%d' % i])

        areset()
        mixt = [al([128, 8, 128], BF16) for _ in range(2)]
        h1t = [al([128, D]) for _ in range(2)]
        load_w(wblk, 'wblk', w_xq, 8, 1024)

        def p3_L(i):
            tcols = slice(i * 128, (i + 1) * 128)
            s = i % 2
            dma(mixt[s], mix_d[b, i], ['mixf:%d' % i, 'mixd:%d' % (i // 4)], ['mixt%d' % s])
            dma(xt[s], x[b, tcols, :], (), ['xt%d' % s])

        def p3_A(i):
            tcols = slice(i * 128, (i + 1) * 128)
            s = i % 2; mt_ = mixt[s]; mk = 'mixt%d' % s; hk_ = 'h1t%d' % s
            for hf in range(2):
                for kc in range(8):
                    mm(pb[hf], mt_[:, kc, :], wblk2[:, kc, hf * 512:(hf + 1) * 512], kc == 0, kc == 7, [mk, 'wblk2'], [pk(hf)])
                tt(h1t[s][:, hf * 512:(hf + 1) * 512], pb[hf], xt[s][:, hf * 512:(hf + 1) * 512], ALU.add, [pk(hf), 'xt%d' % s], [hk_])
            dma(h1_d[b, tcols, :], h1t[s], [hk_], ['h1d:%d' % i])

        p3_L(0)
        for step in range(NT + 1):
            if step + 1 < NT: p3_L(step + 1)
            if step < NT:
                p3_A(step)
                stats_chain(h1t[step % 2], 'h1t%d' % (step % 2), par=step % 2)
            if step >= 1:
                j = step - 1
                stats_T(aT[:, :, j * 128:(j + 1) * 128], aK(j), g_cross, 'g_cross', par=j % 2)

        areset()
        xkT = al([128, 4, 2, NMEM], BF16)
        xvp = al([128, 2, 4, 257], BF16)
        a_save = aoff[0]
        memT = al([128, 8, NMEM], BF16)
        xk_sb = al([128, 2, 256]); xk_sq = al([128, 2, 256]); xkn = al([128, 2, 256], BF16)
        mset(xvp, 1.0, ['xvp'])
        for mt in range(2):
            s = mt % 2
            dma(xt[s], mem[b, mt * 128:(mt + 1) * 128, :], (), ['xt%d' % s])
            tile_stats_T(xt[s], 'xt%d' % s, memT[:, :, mt * 128:(mt + 1) * 128], 'memT', g_mem, 'g_mem', par=s)
        for cbk in range(4):
            wkv = wblk2[:, :, (cbk % 2) * 512:(cbk % 2 + 1) * 512]; wkvk = 'wblk2h%d' % (cbk % 2)
            P.add('gq', lambda wkv=wkv, cbk=cbk: nc.gpsimd.dma_start(out=wkv, in_=w_xkv[:, cbk * 512:(cbk + 1) * 512].rearrange("(c p) n -> p c n", p=128)),
                  (), [wkvk] + (['wblk2'] if cbk < 2 else []))
            for mt in range(2):
                bank = mt
                for kc in range(8):
                    mm(pb[bank], memT[:, kc, mt * 128:(mt + 1) * 128], wkv[:, kc, :], kc == 0, kc == 7, ['memT', wkvk], [pk(bank)])
                if cbk < 2:
                    act(xk_sb, pb[bank].rearrange("p (h d) -> p h d", h=2), AF.Copy, [pk(bank)], ['xk_sb'])
                    tt(xk_sq, xk_sb, xk_sb, ALU.mult, ['xk_sb'], ['xk_sq'])
                    red(sm[:, 48:50], xk_sq, ['xk_sq'], ['smq'])
                    rsqrt_chain(sm[:, 56:58], sm[:, 48:50], 256, ['smq'], ['smr'], sm[:, 4:6])
                    tt(xk_sq, xk_sb, sm[:, 56:58].unsqueeze(2).to_broadcast([128, 2, 256]), ALU.mult, ['xk_sb', 'smr'], ['xk_sq'])
                    tt(xkn, xk_sq, gxk_bc.unsqueeze(1).to_broadcast([128, 2, 256]), ALU.mult, ['xk_sq', 'gxk_bc'], ['xkn'])
                    pbt = pb[5].bitcast(BF16)
                    for j in range(4):
                        tr(pbt[:, j * 128:(j + 1) * 128], xkn[:, j // 2, (j % 2) * 128:(j % 2 + 1) * 128], ident_b, ['xkn', 'ident_b'], [pk(5)])
                    cp(xkT[:, 2 * cbk:2 * cbk + 2, :, mt * 128:(mt + 1) * 128],
                       pbt[:, 0:512].rearrange("p (h c m) -> p h c m", h=2, c=2), [pk(5)], ['xkT'])
                else:
                    hv = 2 * (cbk - 2)
                    act(xvp[:, mt, hv:hv + 2, 0:256], pb[bank].rearrange("p (h d) -> p h d", h=2), AF.Copy, [pk(bank)], ['xvp'])
        P.add('gq', lambda: nc.gpsimd.dma_start(out=wblk2, in_=w_xo.rearrange("(c p) n -> p c n", p=128)), (), ['wblk2', 'wblk2h0', 'wblk2h1'])
        P.barrier(); aoff[0] = a_save
        xq_sb = [al([128, 4, 256]) for _ in range(2)]; xq_sq = [al([128, 4, 256]) for _ in range(2)]
        xqn = [al([128, 4, 256], BF16) for _ in range(2)]; xqT = [al([128, 8, 128], BF16) for _ in range(2)]
        PTx = [al([128, 1024], BF16) for _ in range(2)]; xo_sb = [al([128, 1024], BF16) for _ in range(2)]
        xoT = [al([128, 8, 128], BF16) for _ in range(2)]
        h1t = [al([128, D]) for _ in range(2)]
        pbt = pb[5].bitcast(BF16)

        def p4_A1(i):
            tcols = slice(i * 128, (i + 1) * 128)
            p = i % 2
            dma(xt[p], h1_d[b, tcols, :], ['h1d:%d' % i], ['xt%d' % p])
            for hf in range(2):
                for kc in range(8):
                    mm(pb[hf], aT[:, kc, tcols], wblk[:, kc, hf * 512:(hf + 1) * 512], kc == 0, kc == 7, [aK(i), 'wblk'], [pk(hf)])

        def p4_A1n(i):
            p = i % 2
            c0_ = 24 + 8 * p
            mset(sm[:, c0_:c0_ + 4], 0.0, ['smq%d' % p])
            for h in range(4):
                src = pb[h // 2][:, (h % 2) * 256:(h % 2 + 1) * 256]
                act(xq_sq[p][:, h, :], src, AF.Square, [pk(h // 2)], ['xq_sq%d' % p, 'smq%d' % p], accum=sm[:, c0_ + h:c0_ + h + 1])
            rsqrt_chain(sm[:, c0_ + 4:c0_ + 8], sm[:, c0_:c0_ + 4], 256, ['smq%d' % p], ['smr%d' % p], sm[:, c0_:c0_ + 4])
            for h in range(4):
                src = pb[h // 2][:, (h % 2) * 256:(h % 2 + 1) * 256]
                stt(xqn[p][:, h, :], src, sm[:, c0_ + 4 + h:c0_ + 5 + h], gxq_bc, ALU.mult, ALU.mult, [pk(h // 2), 'smr%d' % p, 'gxq_bc'], ['xqn%d' % p])

        def p4_A2(i):
            p = i % 2
            for j in range(8):
                tr(pbt[:, j * 128:(j + 1) * 128], xqn[p][:, j // 2, (j % 2) * 128:(j % 2 + 1) * 128], ident_b, ['xqn%d' % p, 'ident_b'], [pk(5)])
            cp(xqT[p], pbt.rearrange("p (j t) -> p j t", j=8), [pk(5)], ['xqT%d' % p])

        def p4_B1(i):
            p = i % 2
            for h in range(4):
                for mt in range(2):
                    bank = 2 + h // 2; c0_ = ((h % 2) * 2 + mt) * 128
                    for dc in range(2):
                        mm(pb[bank][:, c0_:c0_ + 128], xkT[:, h, dc, mt * 128:(mt + 1) * 128], xqT[p][:, h * 2 + dc, :], dc == 0, dc == 1,
                           ['xkT', 'xqT%d' % p], [pk(bank)])
            for hb in range(2):
                act(PTx[p][:, hb * 512:(hb + 1) * 512], pb[2 + hb], AF.Exp, [pk(2 + hb)], ['PTx%d' % p], scale=1.0 / 16)

        def p4_B2(i):
            p = i % 2
            for h in range(4):
                bank = 4 if h % 2 == 0 else 7
                for mt in range(2):
                    mm(pb[bank][:, 0:257], PTx[p][:, (h * 2 + mt) * 128:(h * 2 + mt + 1) * 128], xvp[:, mt, h, :], mt == 0, mt == 1,
                       ['PTx%d' % p, 'xvp'], [pk(bank)])
                c_ = 40 + 4 * p + h
                recip(sm[:, c_:c_ + 1], pb[bank][:, 256:257], [pk(bank)], ['sml%d' % c_])
                ts(xo_sb[p][:, h * 256:(h + 1) * 256], pb[bank][:, 0:256], sm[:, c_:c_ + 1], None, ALU.mult, None, [pk(bank), 'sml%d' % c_], ['xo_sb%d' % p])
            for j in range(8):
                tr(pbt[:, j * 128:(j + 1) * 128], xo_sb[p][:, j * 128:(j + 1) * 128], ident_b, ['xo_sb%d' % p, 'ident_b'], [pk(5)])
            cp(xoT[p], pbt.rearrange("p (j t) -> p j t", j=8), [pk(5)], ['xoT%d' % p])

        def p4_C1(i):
            tcols = slice(i * 128, (i + 1) * 128)
            p = i % 2
            for hf in range(2):
                for kc in range(8):
                    mm(pb[hf], xoT[p][:, kc, :], wblk2[:, kc, hf * 512:(hf + 1) * 512], kc == 0, kc == 7, ['xoT%d' % p, 'wblk2'], [pk(hf)])
                tt(h1t[p][:, hf * 512:(hf + 1) * 512], pb[hf], xt[p][:, hf * 512:(hf + 1) * 512], ALU.add, [pk(hf), 'xt%d' % p], ['h1t%d' % p])
            dma(h2_d[b, tcols, :], h1t[p], ['h1t%d' % p], ['h2d:%d' % i])
        for step in range(NT + 2):
            ic = step - 2; ib = step - 1; ia = step
            if 0 <= ib < NT: p4_B1(ib)
            if 0 <= ic < NT: p4_C1(ic)
            if 0 <= ib < NT: p4_B2(ib)
            if 0 <= ic < NT: stats_chain(h1t[ic % 2], 'h1t%d' % (ic % 2), par=ic % 2)
            if 0 <= ia < NT: p4_A1(ia)
            if 0 <= ia < NT: p4_A1n(ia)
            if 0 <= ic < NT: stats_T(aT[:, :, ic * 128:(ic + 1) * 128], aK(ic), g_ffn, 'g_ffn', par=ic % 2)
            if 0 <= ia < NT: p4_A2(ia)

        areset()
        Gs = [al([128, 514]) for _ in range(2)]; acc = [al([128, 512]) for _ in range(2)]; sl = [al([128, 512]) for _ in range(2)]
        hidb = [al([128, 512], BF16) for _ in range(2)]
        wd = al([128, NC_FF, 512], BF16)
        hidt = [al([128, NC_FF, 256], BF16) for _ in range(2)]
        it5 = 0
        p5_pend = []

        def p5_fin():
            while p5_pend:
                k_, c_, tb_, bu__ = p5_pend.pop()
                tt(hidb[k_], sl[k_], pb[bu__], ALU.mult, ['sl%d' % k_, pk(bu__)], ['hidb%d' % k_])
                for hh_ in range(2):
                    dma(hid_d[b, 2 * tb_ + hh_, :, c_, :], hidb[k_][:, hh_ * 256:(hh_ + 1) * 256], ['hidb%d' % k_], ['hid:%d:%d' % (tb_, hh_)])
        for cg in range((NC_FF + 3) // 4):
            cbase = cg * 4; ncg = min(4, NC_FF - cbase)
            if cg == 4:
                load_w(wd, 'wd', w_ffn_down[:, 0:512], NC_FF, 512)
            wb = wblk if cg % 2 == 0 else wblk2; wkk = 'wblk' if cg % 2 == 0 else 'wblk2'
            load_w(wb[:, :, 0:ncg * 128], wkk, w_ffn_up[:, cbase * 128:(cbase + ncg) * 128], 8, ncg * 128)
            load_w(wb[:, :, 512:512 + ncg * 128], wkk, w_ffn_up[:, DFF + cbase * 128:DFF + (cbase + ncg) * 128], 8, ncg * 128)
            for ci in range(ncg):
                c = cbase + ci
                for tb in range(4):
                    k = it5 % 2; it5 += 1
                    Gk = Gs[k]; gkk = 'Gs%d' % k; ak_ = 'acc%d' % k; sk_ = 'sl%d' % k; hk = 'hidb%d' % k
                    cols = slice(tb * 512, (tb + 1) * 512)
                    ak = [aK(4 * tb + j) for j in range(4)]
                    bg = 2 * k; bu_ = bg + 1
                    if tb == 0:
                        mset(Gk[:, 0:2], 0.0, [gkk])
                    for kc in range(8):
                        mm(pb[bg], wb[:, kc, ci * 128:(ci + 1) * 128], aT[:, kc, cols], kc == 0, kc == 7, [wkk] + ak, [pk(bg)])
                    for kc in range(8):
                        mm(pb[bu_], wb[:, kc, 512 + ci * 128:512 + (ci + 1) * 128], aT[:, kc, cols], kc == 0, kc == 7, [wkk] + ak, [pk(bu_)])
                    act(Gk[:, 2:514], pb[bg], AF.Copy, [pk(bg)], [gkk])
                    if tb < 3:
                        cp(Gs[1 - k][:, 0:2], Gk[:, 512:514], [gkk], ['Gs%d' % (1 - k)])
                    ts(acc[k], Gk[:, 2:514], cw[:, 2, c:c + 1], cb[:, c:c + 1], ALU.mult, ALU.add, [gkk, 'cw', 'cb'], [ak_])
                    stt(acc[k], Gk[:, 1:513], cw[:, 1, c:c + 1], acc[k], ALU.mult, ALU.add, [gkk, 'cw', ak_], [ak_])
                    stt(acc[k], Gk[:, 0:512], cw[:, 0, c:c + 1], acc[k], ALU.mult, ALU.add, [gkk, 'cw', ak_], [ak_])
                    act(sl[k], acc[k], AF.Silu, [ak_], [sk_])
                    p5_fin()
                    p5_pend.append((k, c, tb, bu_))

        p5_fin()
        for hf in range(2):
            if hf == 1:
                load_w(wd, 'wd', w_ffn_down[:, hf * 512:(hf + 1) * 512], NC_FF, 512)

            def p5_L(i):
                s = i % 2; hs = (i // 2) % 2
                if i % 2 == 0:
                    dma(hidt[hs], hid_d[b, i // 2], ['hid:%d:%d' % (i // 4, (i // 2) % 2)], ['hidt%d' % hs])
                dma(xt[s][:, 0:512], h2_d[b, i * 128:(i + 1) * 128, hf * 512:(hf + 1) * 512], ['h2d:%d' % i], ['xt%d' % s])
            p5_L(0)
            for i in range(NT):
                tcols = slice(i * 128, (i + 1) * 128)
                s = i % 2; hs = (i // 2) % 2
                if i + 1 < NT: p5_L(i + 1)
                for c in range(NC_FF):
                    mm(pb[s], hidt[hs][:, c, (i % 2) * 128:(i % 2 + 1) * 128], wd[:, c, :], c == 0, c == NC_FF - 1, ['hidt%d' % hs, 'wd'], [pk(s)])
                tt(xt[s][:, 512:1024], pb[s], xt[s][:, 0:512], ALU.add, [pk(s), 'xt%d' % s], ['ot%d' % s])
                dma(out[b, tcols, hf * 512:(hf + 1) * 512], xt[s][:, 512:1024], ['ot%d' % s], ['out:%d:%d:%d' % (b, i, hf)], q='aq')
    P.emit(es)
    return nc, es


_PARAMS = ["norm_mix", "w_in", "fox_q_norm", "fox_k_norm", "fox_f_bias", "s5_a_re", "s5_a_im", "s5_log_dt", "s5_b_re", "s5_b_im",
           "s5_c_re", "s5_c_im", "s5_d", "s5_w_glu", "s5_b_glu", "out_norm_fox", "out_norm_s5", "w_out", "norm_cross", "norm_mem",
           "w_xq", "w_xkv", "xq_norm", "xk_norm", "w_xo", "norm_ffn", "w_ffn_up", "ffn_conv_w", "ffn_conv_b", "w_ffn_down"]


def kernel(**inputs):
    nc, es = build()
    with es:
        params = {k: np.ascontiguousarray(np.asarray(inputs[k], dtype=np.float32)[0]) for k in _PARAMS}
        x = np.asarray(inputs["x"], dtype=np.float32); mem = np.asarray(inputs["mem"], dtype=np.float32)
        in_maps = []
        for c in range(8):
            m = dict(params)
            m["x"] = np.ascontiguousarray(x[NB * c:NB * (c + 1)])
            m["mem"] = np.ascontiguousarray(mem[NB * c:NB * (c + 1)])
            in_maps.append(m)
        res = run_bass_kernel_spmd(nc, in_maps, core_ids=list(range(8)))
    return np.concatenate([r["out"] for r in res.results], axis=0).astype(np.float32)
```

```python
import math
from contextlib import ExitStack
import numpy as np
import concourse.bass as bass
import concourse.mybir as mybir
from concourse.bass_utils import run_bass_kernel_spmd

F32 = mybir.dt.float32
BF16 = mybir.dt.bfloat16
AF = mybir.ActivationFunctionType
ALU = mybir.AluOpType
AX = mybir.AxisListType

D = 1024; L = 2048; NT = 16; NB = 2; NMEM = 256; DFF = 2816; NC_FF = 22
EPS = 1e-6
ENGS = ['sp', 'pe', 'act', 'dve', 'pool']
NDS = 16
DEBUG = None


class Prog:
    def __init__(self, nc):
        self.nc = nc; self.ops = []; self.lastw = {}; self.rd = {}
        self.bar = set(); self.last_eng = {}; self.ndma = 0; self.last_slot = {}; self.gq_since = []

    def barrier(self):
        self.bar = set(self.last_eng.values()) | set(self.last_slot.values()) | set(self.gq_since)
        self.gq_since = []

    def add(self, eng, fn, r=(), w=()):
        i = len(self.ops); deps = set(self.bar)
        for k in list(r) + list(w):
            if k in self.lastw: deps.add(self.lastw[k])
        for k in w:
            rdk = self.rd.get(k)
            if rdk:
                deps.update(rdk[0].values()); deps.update(rdk[1])
        self.ops.append(dict(eng=eng, fn=fn, deps=deps, sig=False))
        for k in w:
            self.lastw[k] = i; self.rd[k] = ({}, [])
        for k in r:
            rdk = self.rd.setdefault(k, ({}, []))
            if eng in ('sp', 'gq', 'aq'): rdk[1].append(i)
            else: rdk[0][eng] = i
        if eng in ('sp', 'aq'):
            self.last_slot[self.ndma % NDS] = i; self.ndma += 1
        elif eng == 'gq':
            self.gq_since.append(i)
        else:
            self.last_eng[eng] = i
        return i

    def emit(self, es):
        nc = self.nc; ops = self.ops
        for op in ops:
            for d in op['deps']:
                if ops[d]['eng'] == 'pe' and op['eng'] == 'pe': continue
                ops[d]['sig'] = True
        cnt = {e: 0 for e in ENGS}; di = 0; qi = 0
        for op in ops:
            e = op['eng']
            if e in ('sp', 'aq'):
                op['dsem'] = di % NDS; op['dval'] = 16 * (di // NDS + 1); di += 1
            elif e == 'gq':
                op['dsem'] = NDS + qi; op['dval'] = 16; qi += 1
            elif op['sig']:
                cnt[e] += 1; op['ord'] = cnt[e]
        esem = {e: es.enter_context(nc.semaphore("s_" + e)) for e in ENGS if e != 'sp'}
        dsem = [es.enter_context(nc.semaphore("d_%d" % i)) for i in range(NDS + qi)]
        dfinal = [0] * (NDS + qi)
        for op in ops:
            if op['eng'] in ('sp', 'gq', 'aq'): dfinal[op['dsem']] = op['dval']

        def run(e, eng):
            waited = {}

            def wait(key, sem, val):
                if waited.get(key, 0) >= val: return
                eng.wait_ge(sem, val); waited[key] = val
            for op in ops:
                oe = op['eng']
                if {'gq': 'pool', 'aq': 'act'}.get(oe, oe) != e: continue
                for d in sorted(op['deps']):
                    dop = ops[d]
                    if dop['eng'] == 'pe' and oe == 'pe': continue
                    if dop['eng'] in ('sp', 'gq', 'aq'): wait(('d', dop['dsem']), dsem[dop['dsem']], dop['dval'])
                    else: wait(dop['eng'], esem[dop['eng']], dop['ord'])
                if oe in ('sp', 'gq', 'aq'):
                    if op['dval'] > 16: wait(('d', op['dsem']), dsem[op['dsem']], op['dval'] - 16)
                    op['fn']().then_inc(dsem[op['dsem']], 16)
                else:
                    ins = op['fn']()
                    if op['sig']: ins.then_inc(esem[e], 1)
            if e == 'sp':
                for i in range(len(dfinal)):
                    if dfinal[i]: wait(('d', i), dsem[i], dfinal[i])
        block = es.enter_context(nc.Block())

        @block.sync
        def _(eng): run('sp', eng)

        @block.tensor
        def _(eng): run('pe', eng)

        @block.scalar
        def _(eng): run('act', eng)

        @block.vector
        def _(eng): run('dve', eng)

        @block.gpsimd
        def _(eng): run('pool', eng)
        print("ops:", len(ops), {e: sum(1 for o in ops if o['eng'] == e) for e in ENGS + ['gq']}, "signals:", cnt)


def build():
    nc = bass.Bass("TRN2", target_bir_lowering=False)
    es = ExitStack()
    P = Prog(nc)

    def din(name, shape): return nc.dram_tensor(name, shape, F32, kind="ExternalInput").ap()
    x = din("x", [NB, L, D]); mem = din("mem", [NB, NMEM, D])
    norm_mix = din("norm_mix", [D]); w_in = din("w_in", [D, 2056])
    fox_q_norm = din("fox_q_norm", [64]); fox_k_norm = din("fox_k_norm", [64]); fox_f_bias = din("fox_f_bias", [8])
    s5_a_re = din("s5_a_re", [32, 64]); s5_a_im = din("s5_a_im", [32, 64]); s5_log_dt = din("s5_log_dt", [32])
    s5_b_re = din("s5_b_re", [32, 64, 16]); s5_b_im = din("s5_b_im", [32, 64, 16])
    s5_c_re = din("s5_c_re", [32, 16, 64]); s5_c_im = din("s5_c_im", [32, 16, 64]); s5_d = din("s5_d", [32, 16])
    s5_w_glu = din("s5_w_glu", [512, 512]); s5_b_glu = din("s5_b_glu", [512])
    out_norm_fox = din("out_norm_fox", [512]); out_norm_s5 = din("out_norm_s5", [512]); w_out = din("w_out", [D, D])
    norm_cross = din("norm_cross", [D]); norm_mem = din("norm_mem", [D]); w_xq = din("w_xq", [D, D])
    w_xkv = din("w_xkv", [D, 2 * D]); xq_norm = din("xq_norm", [256]); xk_norm = din("xk_norm", [256])
    w_xo = din("w_xo", [D, D]); norm_ffn = din("norm_ffn", [D]); w_ffn_up = din("w_ffn_up", [D, 2 * DFF])
    ffn_conv_w = din("ffn_conv_w", [3, DFF]); ffn_conv_b = din("ffn_conv_b", [DFF]); w_ffn_down = din("w_ffn_down", [DFF, D])
    out = nc.dram_tensor("out", [NB, L, D], F32, kind="ExternalOutput").ap()
    h1_d = nc.dram_tensor("h1_d", [NB, L, D], F32, kind="Internal").ap()
    h2_d = nc.dram_tensor("h2_d", [NB, L, D], F32, kind="Internal").ap()
    hid_d = nc.dram_tensor("hid_d", [NB, 8, 128, NC_FF, 256], BF16, kind="Internal").ap()
    mix_d = nc.dram_tensor("mix_d", [NB, NT, 128, 8, 128], BF16, kind="Internal").ap()

    def sb(name, shape, dt=F32): return es.enter_context(nc.sbuf_tensor(name, shape, dt))[:]

    AW = 15580
    arena = es.enter_context(nc.sbuf_tensor("arena", [128, AW], F32))
    aoff = [0]

    def areset():
        aoff[0] = 0; P.barrier()

    def al(shape, dt=F32):
        n = 1
        for v in shape[1:]: n *= v
        nw = (n + 1) // 2 if dt == BF16 else n
        nw = (nw + 7) // 8 * 8
        assert aoff[0] + nw <= AW, ("arena overflow", aoff[0], nw)
        v = arena[:, aoff[0]:aoff[0] + nw]; aoff[0] += nw
        if dt == BF16: v = v.bitcast(BF16)
        v = v[0:shape[0], 0:n]
        if len(shape) > 2:
            names = " ".join("a%d" % i for i in range(len(shape) - 1))
            v = v.rearrange("p (%s) -> p %s" % (names, names), **{"a%d" % i: shape[i + 1] for i in range(len(shape) - 1)})
        return v

    def dma(o, i, r, w, q='sp'):
        e = nc.sync if q == 'sp' else nc.scalar
        P.add(q, lambda: e.dma_start(out=o, in_=i, allow_slow_non_contiguous=True), r, w)

    def mm(o, lhsT, rhs, start, stop, r, w, skip=False, tp=None):
        if tp is None:
            P.add('pe', lambda: nc.tensor.matmul(o, lhsT=lhsT, rhs=rhs, start=start, stop=stop, skip_group_check=skip), r, w)
        else:
            P.add('pe', lambda: nc.tensor.matmul(o, lhsT=lhsT, rhs=rhs, start=start, stop=stop, skip_group_check=skip, tile_position=tp), r, w)

    def tr(o, i, ident, r, w):
        P.add('pe', lambda: nc.tensor.transpose(o, i, ident), r, w)

    def act(o, i, func, r, w, scale=None, bias=None, accum=None):
        kw = {}
        if scale is not None: kw['scale'] = scale
        if bias is not None: kw['bias'] = bias
        if accum is not None: kw['accum_out'] = accum
        P.add('act', lambda: nc.scalar.activation(out=o, in_=i, func=func, **kw), r, w)

    DEF = ['dve']

    def tt(o, a, b, op, r, w, eng=None):
        eng = eng or DEF[0]
        e = nc.vector if eng == 'dve' else nc.gpsimd
        P.add(eng, lambda: e.tensor_tensor(out=o, in0=a, in1=b, op=op), r, w)

    def ts(o, a, s1, s2, op0, op1, r, w, eng=None):
        eng = eng or DEF[0]
        e = nc.vector if eng == 'dve' else nc.gpsimd
        if op1 is None:
            P.add(eng, lambda: e.tensor_scalar(out=o, in0=a, scalar1=s1, scalar2=None, op0=op0), r, w)
        else:
            P.add(eng, lambda: e.tensor_scalar(out=o, in0=a, scalar1=s1, scalar2=s2, op0=op0, op1=op1), r, w)

    def stt(o, a, s, b, op0, op1, r, w):
        P.add('dve', lambda: nc.vector.scalar_tensor_tensor(out=o, in0=a, scalar=s, in1=b, op0=op0, op1=op1), r, w)

    def cp(o, i, r, w, eng=None):
        eng = eng or DEF[0]
        e = nc.vector if eng == 'dve' else nc.gpsimd
        P.add(eng, lambda: e.tensor_copy(out=o, in_=i), r, w)

    def recip(o, i, r, w):
        P.add('dve', lambda: nc.vector.reciprocal(out=o, in_=i), r, w)

    def red(o, i, r, w):
        P.add('dve', lambda: nc.vector.tensor_reduce(out=o, in_=i, axis=AX.X, op=ALU.add), r, w)

    def mset(o, v, w, eng=None):
        eng = eng or DEF[0]
        e = nc.vector if eng == 'dve' else nc.gpsimd
        P.add(eng, lambda: e.memset(o, v), (), w)

    def rsqrt_chain(o, ss, n, keyr, keyw, tmp):
        ts(tmp, ss, 1.0 / n, EPS, ALU.mult, ALU.add, keyr, ['rs_tmp'])
        act(tmp, tmp, AF.Ln, ['rs_tmp'], ['rs_tmp'])
        act(o, tmp, AF.Exp, ['rs_tmp'], keyw, scale=-0.5)

    ident_f = sb("ident_f", [128, 128]); ident_b = sb("ident_b", [128, 128], BF16)
    ones_f = al([128, 128]); ones_b = sb("ones_b", [128, 128], BF16)
    tri_f = al([128, 128]); tri_b = sb("tri_b", [128, 128], BF16)
    mset(ones_f, 1.0, ['ones_f'], 'pool')
    cp(ones_b, ones_f, ['ones_f'], ['ones_b'], 'pool')
    P.add('pool', lambda: nc.gpsimd.affine_select(out=ident_f, in_=ones_f, pattern=[[1, 128]], compare_op=ALU.is_equal,
                                                  fill=0.0, base=0, channel_multiplier=-1), ['ones_f'], ['ident_f'])
    P.add('pool', lambda: nc.gpsimd.affine_select(out=tri_f, in_=ones_f, pattern=[[1, 128]], compare_op=ALU.is_ge,
                                                  fill=0.0, base=0, channel_multiplier=-1), ['ones_f'], ['tri_f'])
    cp(ident_b, ident_f, ['ident_f'], ['ident_b'], 'pool')
    cp(tri_b, tri_f, ['tri_f'], ['tri_b'], 'pool')

    PS = es.enter_context(nc.psum_tensor("PS", [128, 4096], F32))[:]
    pb = [PS[:, 512 * i:512 * (i + 1)] for i in range(8)]
    def pk(i): return 'pb%d' % i

    def colload(name, src, n):
        t = sb(name, [128, n])
        dma(t, src.rearrange("(c p) -> p c", p=128), (), [name])
        return t

    def bcload(name, src, n):
        t = sb(name, [128, n])
        dma(t, src.partition_broadcast(128), (), [name])
        return t
    g_mix = colload("g_mix", norm_mix, 8); g_cross = colload("g_cross", norm_cross, 8)
    g_mem = colload("g_mem", norm_mem, 8); g_ffn = colload("g_ffn", norm_ffn, 8)
    g_s5 = colload("g_s5", out_norm_s5, 4); b_glu = colload("b_glu", s5_b_glu, 4)
    cb = colload("cb", ffn_conv_b, NC_FF)
    cw = sb("cw", [128, 3, NC_FF])
    for k in range(3):
        dma(cw[:, k, :], ffn_conv_w[k].rearrange("(c p) -> p c", p=128), (), ['cw'])
    gq_bc = bcload("gq_bc", fox_q_norm, 64); gk_bc = bcload("gk_bc", fox_k_norm, 64)
    fb_bc = bcload("fb_bc", fox_f_bias, 8); gfox_bc = bcload("gfox_bc", out_norm_fox, 512)
    gxq_bc = bcload("gxq_bc", xq_norm, 256); gxk_bc = bcload("gxk_bc", xk_norm, 256)

    def load_w(dst, dkey, src, kc, ncols, gain=None):
        for k0 in range(0, kc, 22):
            kn = min(22, kc - k0)
            d_ = dst[:, k0:k0 + kn, :]; s_ = src[k0 * 128:(k0 + kn) * 128, :].rearrange("(c p) n -> p c n", p=128)
            P.add('gq', lambda d_=d_, s_=s_: nc.gpsimd.dma_start(out=d_, in_=s_), (), [dkey])

    aT = sb("aT", [128, 8, L], BF16)
    xsc = sb("xsc", [128, D]); xsc2p = sb("xsc2p", [128, D])
    xt = [sb("xt%d" % i, [128, D]) for i in range(2)]
    sm = sb("sm", [128, 64])
    wblk = sb("wblk", [128, 8, 1024], BF16)
    wblk2 = sb("wblk2", [128, 8, 1024], BF16)
    wglu = sb("wglu", [128, 4, 512], BF16)

    def aK(i): return 'aT:%d' % i

    stat_bufs = [xsc, xsc2p]

    def stats_chain(src_tile, skey, par=0):
        xs_ = stat_bufs[par]; c0_ = 0 if par == 0 else 16
        kss = 'ss%d' % par; kr = 'rstd%d' % par; kx = 'xsc%d' % par
        mset(sm[:, c0_:c0_ + 1], 0.0, [kss])
        act(xs_, src_tile, AF.Square, [skey], [kx, kss], accum=sm[:, c0_:c0_ + 1])
        rsqrt_chain(sm[:, c0_ + 2:c0_ + 3], sm[:, c0_:c0_ + 1], D, [kss], [kr], sm[:, c0_ + 1:c0_ + 2])
        act(xs_, src_tile, AF.Copy, [skey, kr], [kx], scale=sm[:, c0_ + 2:c0_ + 3])

    def stats_T(dstT, dkey, gcol, gkey, par=0):
        xs_ = stat_bufs[par]; kx = 'xsc%d' % par
        for hf in range(2):
            bank = 6 + hf
            for j in range(4):
                kc = hf * 4 + j
                tr(pb[bank][:, j * 128:(j + 1) * 128], xs_[:, kc * 128:(kc + 1) * 128], ident_f, [kx, 'ident_f'], [pk(bank)])
            tt(dstT[:, 4 * hf:4 * hf + 4, :], pb[bank].rearrange("p (c t) -> p c t", c=4),
               gcol[:, 4 * hf:4 * hf + 4].unsqueeze(2).to_broadcast([128, 4, 128]), ALU.mult, [pk(bank), gkey], [dkey])

    def tile_stats_T(src_tile, skey, dstT, dkey, gcol, gkey, par=0):
        stats_chain(src_tile, skey, par)
        stats_T(dstT, dkey, gcol, gkey, par)

    p1_pending = []

    def p1_tile(b, i):
        s = i % 2
        dma(xt[s], x[b, i * 128:(i + 1) * 128, :], (), ['xt%d' % s])
        stats_chain(xt[s], 'xt%d' % s, par=s)
        p1_flush()
        p1_pending.append(i)

    def p1_flush():
        while p1_pending:
            j = p1_pending.pop()
            stats_T(aT[:, :, j * 128:(j + 1) * 128], aK(j), g_mix, 'g_mix', par=j % 2)

    def p1_tiles(b):
        for i in range(NT):
            p1_tile(b, i)
        p1_flush()
    DEF[0] = 'pool'
    T2 = 128
    Mlag = sb("Mlag", [128, 4, 8, 128], BF16)
    Ast32 = sb("Ast32", [128, 4, 8, 2, 128], BF16)
    Qst32 = sb("Qst32", [128, 16, 8, 2, 32], BF16)
    E8r = sb("E8r", [128, 16, T2 + 1]); E8i = sb("E8i", [128, 16, T2 + 1])
    r8 = sb("r8", [128, 16]); dcol = sb("dcol", [128, 4]); carry = sb("carry", [128, 16, 2])
    dma(dcol, s5_d.rearrange("(o g) c -> (g c) o", o=4), (), ['dcol'])
    load_w(wglu, 'wglu', s5_w_glu, 4, 512)

    are = al([128, 16]); aim = al([128, 16]); ldt = al([128, 16])
    for h in range(2):
        dma(are[h * 64:(h + 1) * 64, :], s5_a_re.rearrange("(q h) p -> h p q", h=2)[h], (), ['are'])
        dma(aim[h * 64:(h + 1) * 64, :], s5_a_im.rearrange("(q h) p -> h p q", h=2)[h], (), ['aim'])
        dma(ldt[h * 64:(h + 1) * 64, :], s5_log_dt.rearrange("(q h) -> h q", h=2)[h].partition_broadcast(64), (), ['ldt'])
    bre = al([128, 16, 16]); bim = al([128, 16, 16])
    for h in range(2):
        dma(bre[h * 64:(h + 1) * 64], s5_b_re.rearrange("(q h) p c -> h p q c", h=2)[h], (), ['bre'])
        dma(bim[h * 64:(h + 1) * 64], s5_b_im.rearrange("(q h) p c -> h p q c", h=2)[h], (), ['bim'])
    craw = [al([128, 4, 2, 64]) for k in range(2)]
    for k, src in enumerate((s5_c_re, s5_c_im)):
        for dup in range(2):
            dma(craw[k][:, :, dup, :], src.rearrange("(o g) c p -> (g c) o p", o=4), (), ['craw%d' % k])
    cre = al([128, 16, 16]); cim = al([128, 16, 16])
    for k, dst in enumerate((cre, cim)):
        for o in range(4):
            tr(pb[0][:, 0:128], craw[k][:, o].rearrange("p a b -> p (a b)"), ident_f, ['craw%d' % k, 'ident_f'], [pk(0)])
            v = pb[0][:, 0:128].rearrange("p (q h c) -> p q h c", q=4, h=2)
            for h in range(2):
                cp(dst[h * 64:(h + 1) * 64, o * 4:(o + 1) * 4, :], v[h * 64:(h + 1) * 64, :, h, :], [pk(0)], ['cre' if k == 0 else 'cim'], 'dve')
    S = al([128, 24, 16])
    def pl(i): return S[:, i, :]
    K5 = ['s5s']
    dt_ = pl(0); rho = pl(1); th = pl(2); mag = pl(3); cs = pl(4); sn = pl(5); lbr = pl(6); lbi = pl(7)
    den = pl(8); nr = pl(9); cfr = pl(10); cfi = pl(11); t1 = pl(12); t2 = pl(13); inv8 = pl(14)
    act(dt_, ldt, AF.Exp, ['ldt'], K5)
    tt(rho, are, dt_, ALU.mult, K5 + ['are'], K5)
    tt(th, aim, dt_, ALU.mult, K5 + ['aim'], K5)
    act(mag, rho, AF.Exp, K5, K5)
    act(r8, rho, AF.Exp, K5, ['r8'], scale=8.0)
    recip(inv8, r8, ['r8'], K5)
    MAGIC = 12582912.0
    TWO_PI = 2.0 * math.pi

    def sincos(o_s, o_c, ang, tmp1, tmp2, keys):
        for (o, off) in ((o_s, 0.0), (o_c, math.pi / 2)):
            ts(tmp1, ang, off, 1.0 / TWO_PI, ALU.add, ALU.mult, keys, keys)
            ts(tmp2, tmp1, MAGIC, None, ALU.add, None, keys, keys)
            ts(tmp2, tmp2, MAGIC, None, ALU.subtract, None, keys, keys)
            tt(tmp1, tmp1, tmp2, ALU.subtract, keys, keys)
            ts(tmp1, tmp1, TWO_PI, None, ALU.mult, None, keys, keys)
            ts(tmp1, tmp1, math.pi, -math.pi, ALU.min, ALU.max, keys, keys)
            act(o, tmp1, AF.Sin, keys, keys)
    sincos(sn, cs, th, t1, t2, K5)
    tt(lbr, mag, cs, ALU.mult, K5, K5); tt(lbi, mag, sn, ALU.mult, K5, K5)
    tt(den, are, are, ALU.mult, ['are'] + K5, K5); tt(t1, aim, aim, ALU.mult, ['aim'] + K5, K5)
    tt(den, den, t1, ALU.add, K5, K5); recip(den, den, K5, K5)
    ts(nr, lbr, -1.0, None, ALU.add, None, K5, K5)
    tt(t1, nr, are, ALU.mult, K5, K5); tt(t2, lbi, aim, ALU.mult, K5, K5); tt(cfr, t1, t2, ALU.add, K5, K5)
    tt(cfr, cfr, den, ALU.mult, K5, K5)
    tt(t1, lbi, are, ALU.mult, K5, K5); tt(t2, nr, aim, ALU.mult, K5, K5); tt(cfi, t1, t2, ALU.subtract, K5, K5)
    tt(cfi, cfi, den, ALU.mult, K5, K5)
    bbr = al([128, 16, 16]); bbi = al([128, 16, 16]); tb1 = al([128, 16, 16]); nbbi = al([128, 16, 16])
    KB = ['bb']
    cfr_b = cfr.unsqueeze(2).to_broadcast([128, 16, 16]); cfi_b = cfi.unsqueeze(2).to_broadcast([128, 16, 16])
    tt(bbr, bre, cfr_b, ALU.mult, K5 + ['bre'], KB); tt(tb1, bim, cfi_b, ALU.mult, K5 + ['bim'], KB)
    tt(bbr, bbr, tb1, ALU.subtract, KB, KB)
    tt(bbi, bim, cfr_b, ALU.mult, K5 + ['bim'], KB); tt(tb1, bre, cfi_b, ALU.mult, K5 + ['bre'], KB)
    tt(bbi, bbi, tb1, ALU.add, KB, KB)
    ts(nbbi, bbi, -1.0, None, ALU.mult, None, KB, ['nbbi'])

    def cdouble(Tr, Ti, nmax, tA, tB, key, tkey):
        n = 1
        while n < nmax:
            m = min(n, nmax - n)
            ar = Tr[:, :, 1:1 + m]; ai = Ti[:, :, 1:1 + m]
            br_ = Tr[:, :, n:n + 1].to_broadcast([128, 16, m]); bi_ = Ti[:, :, n:n + 1].to_broadcast([128, 16, m])
            tt(tA[:, :, 0:m], ar, br_, ALU.mult, [key], [tkey]); tt(tB[:, :, 0:m], ai, bi_, ALU.mult, [key], [tkey])
            tt(Tr[:, :, n + 1:n + 1 + m], tA[:, :, 0:m], tB[:, :, 0:m], ALU.subtract, [tkey], [key])
            tt(tA[:, :, 0:m], ar, bi_, ALU.mult, [key], [tkey]); tt(tB[:, :, 0:m], ai, br_, ALU.mult, [key], [tkey])
            tt(Ti[:, :, n + 1:n + 1 + m], tA[:, :, 0:m], tB[:, :, 0:m], ALU.add, [tkey], [key])
            n += m
    Et1 = al([128, 16, 64]); Et2 = al([128, 16, 64])
    Lr = al([128, 16, 9]); Li = al([128, 16, 9])
    mset(Lr[:, :, 0:1], 1.0, ['L']); mset(Li[:, :, 0:1], 0.0, ['L'])
    cp(Lr[:, :, 1:2], lbr.unsqueeze(2), K5, ['L']); cp(Li[:, :, 1:2], lbi.unsqueeze(2), K5, ['L'])
    cdouble(Lr, Li, 8, Et1, Et2, 'L', 'Et')
    KE = ['E8']
    mset(E8r[:, :, 0:1], 1.0, KE); mset(E8i[:, :, 0:1], 0.0, KE)
    tt(E8r[:, :, 1:2], Lr[:, :, 8:9], inv8.unsqueeze(2), ALU.mult, ['L'] + K5, KE)
    tt(E8i[:, :, 1:2], Li[:, :, 8:9], inv8.unsqueeze(2), ALU.mult, ['L'] + K5, KE)
    cdouble(E8r, E8i, T2, Et1, Et2, 'E8', 'Et')
    Gi = al([8, 128]); bdmask = al([128, 128]); mtmp = al([128, 128])
    mset(Gi, 1.0, ['Gi'], 'pool')
    P.add('pool', lambda: nc.gpsimd.affine_select(out=Gi, in_=Gi, pattern=[[1, 128]], compare_op=ALU.is_ge, fill=0.0, base=0,
                                                  channel_multiplier=-16), ['Gi'], ['Gi'])
    P.add('pool', lambda: nc.gpsimd.affine_select(out=Gi, in_=Gi, pattern=[[-1, 128]], compare_op=ALU.is_ge, fill=0.0, base=15,
                                                  channel_multiplier=16), ['Gi'], ['Gi'])
    mm(pb[1][:, 0:128], Gi, Gi, True, True, ['Gi'], [pk(1)])
    cp(bdmask, pb[1][:, 0:128], [pk(1)], ['bdmask'], 'dve')
    SOB = al([128, 4, 2, 128]); SOC = al([128, 4, 2, 128])

    def fill_SO(dst, dkey, srcs, keys):
        v = dst.rearrange("p o r (q h c) -> p o r q h c", q=4, h=2)
        for ri, src in enumerate(srcs):
            sv = src.rearrange("p (o q) c -> p o q c", q=4)
            for h in range(2):
                cp(v[h * 64:(h + 1) * 64, :, ri, :, h, :], sv[h * 64:(h + 1) * 64], keys, [dkey])
    mset(SOB, 0.0, ['SOB']); mset(SOC, 0.0, ['SOC']); mset(Qst32, 0.0, ['Qst32'])
    fill_SO(SOB, 'SOB', (bbr, nbbi), KB + ['nbbi'])
    DEF[0] = 'dve'
    SOA = al([128, 4, 2, 128]); LBr = al([128, 16, 16]); LBi = al([128, 16, 16]); LBt = al([128, 16, 16])
    mset(SOA, 0.0, ['SOA'])
    for s_ in range(8):
        k_ = 7 - s_
        lr_b = Lr[:, :, k_:k_ + 1].to_broadcast([128, 16, 16]); li_b = Li[:, :, k_:k_ + 1].to_broadcast([128, 16, 16])
        KC = ['LB']
        tt(LBr, bbr, lr_b, ALU.mult, KB + ['L'], KC); tt(LBt, bbi, li_b, ALU.mult, KB + ['L'], KC); tt(LBr, LBr, LBt, ALU.subtract, KC, KC)
        tt(LBi, bbi, lr_b, ALU.mult, KB + ['L'], KC); tt(LBt, bbr, li_b, ALU.mult, KB + ['L'], KC); tt(LBi, LBi, LBt, ALU.add, KC, KC)
        fill_SO(SOA, 'SOA', (LBr, LBi), KC)
        for o in range(4):
            for ri in range(2):
                tr(pb[1][:, 0:128], SOA[:, o, ri, :], ident_f, ['SOA', 'ident_f'], [pk(1)])
                cp(Ast32[:, o, s_, ri, :], pb[1][:, 0:128], [pk(1)], ['Ast32'], 'dve')

    DEF[0] = 'pool'
    CLr = al([128, 16, 16]); CLi = al([128, 16, 16]); CLt = al([128, 16, 16])
    for tau in range(9):
        if tau < 8:
            DEF[0] = 'dve'
            p1_tile(0, 2 * tau); p1_tile(0, 2 * tau + 1)
            if tau == 7: p1_flush()
            DEF[0] = 'pool'
        lr_b = Lr[:, :, tau:tau + 1].to_broadcast([128, 16, 16]); li_b = Li[:, :, tau:tau + 1].to_broadcast([128, 16, 16])
        KC = ['CL']
        tt(CLr, cre, lr_b, ALU.mult, ['cre', 'L'], KC); tt(CLt, cim, li_b, ALU.mult, ['cim', 'L'], KC); tt(CLr, CLr, CLt, ALU.subtract, KC, KC)
        tt(CLi, cre, li_b, ALU.mult, ['cre', 'L'], KC); tt(CLt, cim, lr_b, ALU.mult, ['cim', 'L'], KC); tt(CLi, CLi, CLt, ALU.add, KC, KC)
        if tau >= 1:
            for h in range(2):
                cp(Qst32[h * 64:(h + 1) * 64, :, tau - 1, 0, 16 * h:16 * h + 16], CLr[h * 64:(h + 1) * 64], KC, ['Qst32'])
                ts(Qst32[h * 64:(h + 1) * 64, :, tau - 1, 1, 16 * h:16 * h + 16], CLi[h * 64:(h + 1) * 64], -1.0, None, ALU.mult, None, KC, ['Qst32'])
        if tau <= 7:
            fill_SO(SOC, 'SOC', (CLr, CLi), KC)
            for o in range(4):
                bk = (0, 2, 3, 4, 5)[(tau * 4 + o) % 5]
                for ri in range(2):
                    mm(pb[bk][:, 0:128], SOB[:, o, ri, :], SOC[:, o, ri, :], ri == 0, ri == 1, ['SOB', 'SOC'], [pk(bk)])
                if tau == 0:
                    tt(mtmp, pb[bk][:, 0:128], bdmask, ALU.mult, [pk(bk), 'bdmask'], ['mtmp'], 'dve')
                    stt(Mlag[:, o, tau, :], ident_f, dcol[:, o:o + 1], mtmp, ALU.mult, ALU.add, ['ident_f', 'dcol', 'mtmp'], ['Mlag'])
                else:
                    tt(Mlag[:, o, tau, :], pb[bk][:, 0:128], bdmask, ALU.mult, [pk(bk), 'bdmask'], ['Mlag'], 'dve')
    DEF[0] = 'dve'
    HG = 2
    gcount = [0]; itc = [0]
    for b in range(NB):
        areset()
        if b > 0:
            p1_tiles(b)

        uT = al([128, 4, L], BF16); gyT = al([128, 4, L], BF16)
        y2T = al([128, 4, 512], BF16); sqT = al([128, 4, 512], BF16)
        Xbufs = [al([128, 4, 2, 257], BF16) for _ in range(2)]
        vre = al([128, 4, T2]); vim = al([128, 4, T2]); wre = al([128, 4, T2]); wim = al([128, 4, T2])
        c1 = al([128, 4, 1]); c2 = al([128, 4, 1]); c3 = al([128, 4, 1]); c4 = al([128, 4, 1])
        pA = sqT[:, 0, :].rearrange("p (q j) -> p q j", q=4); pB = sqT[:, 1, :].rearrange("p (q j) -> p q j", q=4)
        ytmp = al([128, 512]); ytmp2 = al([128, 512])
        for o in range(4):
            wb = wblk if o % 2 == 0 else wblk2; wkk = 'wblk' if o % 2 == 0 else 'wblk2'
            load_w(wb[:, :, 0:128], wkk, w_in[:, 1544 + o * 128:1544 + (o + 1) * 128], 8, 128)
            for tb in range(4):
                bank = tb % 2
                for kc in range(8):
                    mm(pb[bank], wb[:, kc, 0:128], aT[:, kc, tb * 512:(tb + 1) * 512], kc == 0, kc == 7,
                       [wkk] + [aK(4 * tb + j) for j in range(4)], [pk(bank)])
                act(uT[:, o, tb * 512:(tb + 1) * 512], pb[bank], AF.Copy, [pk(bank)], ['uT:%d' % o])

        load_w(wblk[:, :, 0:HG * 64], 'wblk', w_in[:, 0:HG * 64], 8, HG * 64)
        load_w(wblk[:, :, HG * 64:2 * HG * 64], 'wblk', w_in[:, 512:512 + HG * 64], 8, HG * 64)
        load_w(wblk[:, :, 2 * HG * 64:3 * HG * 64], 'wblk', w_in[:, 1024:1024 + HG * 64], 8, HG * 64)
        load_w(wblk[:, :, 768:776], 'wblk', w_in[:, 1536:1544], 8, 8)
        mset(carry, 0.0, ['carry'])
        Vv = PS[:, 1024:3072].rearrange("p (q x) -> p q x", q=4)[:, :, 0:2 * T2].rearrange("p q (r j) -> p q r j", r=2)
        VK = [pk(2), pk(3), pk(4), pk(5)]
        def s5_state(o, jh):
            uo = uT[:, o, :].rearrange("p (j s) -> p j s", s=8)
            uk = ['uT:%d' % o]
            Xbuf = Xbufs[o % 2]; xk_ = 'Xbuf%d' % (o % 2)
            if jh == 0:
                mset(Xbuf[:, :, :, 0:1], 0.0, [xk_])
            j0 = jh * T2
            mset(Vv, 0.0, VK)
            for s_ in range(8):
                for ri in range(2):
                    for q in range(4):
                        mm(Vv[:, q, ri, :], Ast32[32 * q:32 * q + 32, o, s_, ri, :], uo[32 * q:32 * q + 32, j0:j0 + T2, s_], False, s_ == 7,
                           ['Ast32'] + uk, VK, skip=True, tp=(32 * q, 0))
            return Xbuf, xk_, j0

        def s5_rot(o, jh, Xbuf, xk_, j0):
            er = E8r[:, 4 * o:4 * o + 4, 0:T2]; ei = E8i[:, 4 * o:4 * o + 4, 0:T2]
            Vr = Vv[:, :, 0, :]; Vi = Vv[:, :, 1, :]
            KS = ['s5w']
            tt(vre, Vr, er, ALU.mult, VK + ['E8'], ['vre']); tt(vim, Vi, er, ALU.mult, VK + ['E8'], ['vim'])
            tt(wre, Vi, ei, ALU.mult, VK + ['E8'], ['wre']); tt(wim, Vr, ei, ALU.mult, VK + ['E8'], ['wim'])
            tt(vre, vre, wre, ALU.add, ['vre', 'wre'], ['vre']); tt(vim, vim, wim, ALU.subtract, ['vim', 'wim'], ['vim'])
            for q in range(4):
                pr = 4 * o + q
                rb = r8[:, pr:pr + 1].to_broadcast([128, T2])
                for (w_, v_, ci_, wk_, vk_) in ((wre, vre, 0, 'wre', 'vre'), (wim, vim, 1, 'wim', 'vim')):
                    P.add('dve', lambda rb=rb, pr=pr, w_=w_, v_=v_, ci_=ci_, q=q: nc.vector.tensor_tensor_scan(
                        out=w_[:, q, :], data0=rb, data1=v_[:, q, :], initial=carry[:, pr, ci_:ci_ + 1], op0=ALU.mult, op1=ALU.add),
                        [vk_, 'carry', 'r8'], [wk_])
            xo_r = Xbuf[:, :, 0, 1 + j0:1 + j0 + T2]; xo_i = Xbuf[:, :, 1, 1 + j0:1 + j0 + T2]
            wl_r = wre[:, :, T2 - 1:T2]; wl_i = wim[:, :, T2 - 1:T2]
            eTr = E8r[:, 4 * o:4 * o + 4, T2:T2 + 1]; eTi = E8i[:, 4 * o:4 * o + 4, T2:T2 + 1]
            tt(c1, wl_r, eTr, ALU.mult, ['wre', 'E8'], ['c1']); tt(c2, wl_i, eTi, ALU.mult, ['wim', 'E8'], ['c2'])
            tt(c3, wl_i, eTr, ALU.mult, ['wim', 'E8'], ['c3']); tt(c4, wl_r, eTi, ALU.mult, ['wre', 'E8'], ['c4'])
            tt(carry[:, 4 * o:4 * o + 4, 0:1], c1, c2, ALU.subtract, ['c1', 'c2'], ['carry'])
            tt(carry[:, 4 * o:4 * o + 4, 1:2], c3, c4, ALU.add, ['c3', 'c4'], ['carry'])
            tt(pA, wre, er, ALU.mult, ['wre', 'E8'], ['sqT:0'], 'pool'); tt(pB, wim, ei, ALU.mult, ['wim', 'E8'], ['sqT:1'], 'pool')
            tt(xo_r, pA, pB, ALU.subtract, ['sqT:0', 'sqT:1'], [xk_], 'pool')
            tt(pA, wre, ei, ALU.mult, ['wre', 'E8'], ['sqT:0'], 'pool'); tt(pB, wim, er, ALU.mult, ['wim', 'E8'], ['sqT:1'], 'pool')
            tt(xo_i, pA, pB, ALU.add, ['sqT:0', 'sqT:1'], [xk_], 'pool')

        def s5_Y(o, t):
            uo = uT[:, o, :].rearrange("p (j s) -> p j s", s=8)
            uk = ['uT:%d' % o]
            Xbuf = Xbufs[o % 2]; xk_ = 'Xbuf%d' % (o % 2)
            gyo = gyT[:, o, :].rearrange("p (j s) -> p j s", s=8)
            yb = t % 2
            for s_ in range(t + 1):
                mm(pb[yb][:, 0:256], Mlag[:, o, t - s_, :], uo[:, :, s_], s_ == 0, False, ['Mlag'] + uk, [pk(yb)])
            for q in range(4):
                for ri in range(2):
                    mm(pb[yb][32 * q:32 * q + 32, 0:256], Qst32[:, 4 * o + q, t, ri, :], Xbuf[:, q, ri, 0:256], False, ri == 1,
                       ['Qst32', xk_], [pk(yb)], tp=(0, 32 * q))
            yf = ytmp[:, (t % 2) * 256:(t % 2) * 256 + 256]; yq = ytmp2[:, (t % 2) * 256:(t % 2) * 256 + 256]
            ky = 'yf%d' % (t % 2); kq = 'yq%d' % (t % 2)
            act(yf, pb[yb][:, 0:256], AF.Copy, [pk(yb)], [ky])
            tt(yq, yf, yf, ALU.mult, [ky], [kq])
            ts(yq, yq, 0.044715, 1.0, ALU.mult, ALU.add, [kq], [kq])
            tt(yq, yq, yf, ALU.mult, [ky, kq], [kq])
            act(yq, yq, AF.Sigmoid, [kq], [kq], scale=1.5957691216)
            tt(gyo[:, :, t], yf, yq, ALU.mult, [ky, kq], ['gyT:%d' % o], 'pool')
        for o in range(5):
            for jh in range(2):
                if o < 4:
                    st = s5_state(o, jh)
                if o >= 1:
                    for t in range(4 * jh, 4 * jh + 4):
                        s5_Y(o - 1, t)
                if o < 4:
                    s5_rot(o, jh, *st)

        for tb in range(4):
            cols = slice(tb * 512, (tb + 1) * 512)
            gk = ['gyT:%d' % j for j in range(4)]
            for nch in range(4):
                bank = nch % 2
                for kc in range(4):
                    mm(pb[bank], wglu[:, kc, nch * 128:(nch + 1) * 128], gyT[:, kc, cols], kc == 0, kc == 3, ['wglu'] + gk, [pk(bank)])
                sg_ = (ytmp, vre.rearrange("p q j -> p (q j)"))[nch % 2]; sgk = ('ytmp', 'vre')[nch % 2]
                act(sg_, pb[bank], AF.Sigmoid, [pk(bank), 'b_glu'], [sgk], bias=b_glu[:, nch:nch + 1])
                tt(y2T[:, nch, :], gyT[:, nch, cols], sg_, ALU.mult, gk + [sgk], ['y2T:%d' % nch])
                tt(sqT[:, nch, :], y2T[:, nch, :], y2T[:, nch, :], ALU.mult, ['y2T:%d' % nch], ['sqT:%d' % nch])
            for nch in range(4):
                mm(pb[5], ones_b, sqT[:, nch, :], nch == 0, nch == 3, ['ones_b', 'sqT:%d' % nch], [pk(5)])
            ts(ytmp2, pb[5], 1.0 / 512, EPS, ALU.mult, ALU.add, [pk(5)], ['ytmp2'])
            act(ytmp2, ytmp2, AF.Sqrt, ['ytmp2'], ['ytmp2'])
            recip(ytmp2, ytmp2, ['ytmp2'], ['ytmp2'])
            for nch in range(4):
                stt(sqT[:, nch, :], y2T[:, nch, :], g_s5[:, nch:nch + 1], ytmp2, ALU.mult, ALU.mult, ['y2T:%d' % nch, 'g_s5', 'ytmp2', 'sqT:%d' % nch], ['sqT:%d' % nch])
                dma(mix_d[b, 4 * tb:4 * tb + 4, :, 4 + nch, :].rearrange("j p t -> p j t"), sqT[:, nch, :].rearrange("p (j t) -> p j t", j=4),
                    ['sqT:%d' % nch], ['mixd:%d' % tb])

        areset()
        QT = al([68, HG, L], BF16); KT = al([68, HG, L], BF16)
        Vp = al([128, NT, HG, 65], BF16)
        fox = al([128, NT, 512], BF16)
        caug = al([128, NT, 8, 2], BF16); ncaug = al([128, NT, 8, 2], BF16)
        lfs = al([128, NT, 3, 8], BF16)
        TG = 4
        qk_sb = al([128, TG, 2 * HG, 64]); qk_sq = al([128, TG, 2 * HG, 64])
        aug = [al([128, TG, 2 * HG, 68], BF16) for _ in range(2)]
        gqk = al([128, 2 * HG, 64])
        fzA = al([128, NT, 8]); fzB = al([128, NT, 8]); fzC = al([128, NT, 8]); ones_f16 = al([128, NT])
        rsq = al([128, TG, 2 * HG]); rsq2 = al([128, TG, 2 * HG])
        mset(ones_f16, 1.0, ['ones_f16'])
        PT = [al([128, 512], BF16) for _ in range(3)]
        foxn = [al([128, 512], BF16) for _ in range(2)]; foxT = [al([128, 4, 128], BF16) for _ in range(2)]
        fss = al([128, NT]); frs = al([128, NT])
        for a in range(2):
            mset(aug[a], 1.0, ['aug%d' % a])
        for hh in range(HG):
            cp(gqk[:, hh, :], gq_bc, ['gq_bc'], ['gqk']); cp(gqk[:, HG + hh, :], gk_bc, ['gk_bc'], ['gqk'])
        pbt2 = PS[:, 2048:3072].bitcast(BF16)
        for hg in range(8 // HG):
            h0 = hg * HG
            wq = wblk[:, :, 0:HG * 64]; wk = wblk[:, :, HG * 64:2 * HG * 64]; wv = wblk[:, :, 2 * HG * 64:3 * HG * 64]; wf = wblk[:, :, 768:776]
            if hg > 0:
                load_w(wq, 'wblk', w_in[:, h0 * 64:(h0 + HG) * 64], 8, HG * 64)
                load_w(wk, 'wblk', w_in[:, 512 + h0 * 64:512 + (h0 + HG) * 64], 8, HG * 64)
                load_w(wv, 'wblk', w_in[:, 1024 + h0 * 64:1024 + (h0 + HG) * 64], 8, HG * 64)
            if hg == 0:
                load_w(wblk2, 'wblk2', w_out, 8, 1024)
            mset(Vp, 1.0, ['Vp:%d' % i for i in range(NT)])
            if hg == 0:
                ALLK = [aK(i) for i in range(NT)]
                for i in range(NT):
                    for kc in range(8):
                        mm(pb[7][:, i * 8:(i + 1) * 8], aT[:, kc, i * 128:(i + 1) * 128], wf[:, kc, :], kc == 0, kc == 7, ['wblk', aK(i)], [pk(7)])
                tt(fzA, pb[7][:, 0:NT * 8].rearrange("p (t h) -> p t h", t=NT), fb_bc.unsqueeze(1).to_broadcast([128, NT, 8]), ALU.add,
                   [pk(7), 'fb_bc'], ['fzA'])
                act(fzA, fzA, AF.Exp, ['fzA'], ['fzA'], scale=-1.0)
                act(fzA, fzA, AF.Ln, ['fzA'], ['fzA'], bias=1.0)
                cp(lfs[:, :, 0, :], fzA, ['fzA'], ['lfs'])
                tt(fzB, fzA, lfs[:, :, 0, :], ALU.subtract, ['fzA', 'lfs'], ['fzB'])
                cp(lfs[:, :, 1, :], fzB, ['fzB'], ['lfs'])
                tt(fzC, fzB, lfs[:, :, 1, :], ALU.subtract, ['fzB', 'lfs'], ['fzC'])
                cp(lfs[:, :, 2, :], fzC, ['fzC'], ['lfs'])
                mm(pb[6][:, 0:NT * 24], ones_b, lfs.rearrange("p t k h -> p (t k h)"), True, True, ['ones_b', 'lfs'], [pk(6)])
                red(fzB, pb[6][:, 0:NT * 24].rearrange("p (t k h) -> p t h k", t=NT, k=3), [pk(6)], ['fzB'])
                for h in range(8):
                    P.add('dve', lambda h=h: nc.vector.tensor_tensor_scan(out=fzC[:, :, h], data0=ones_f16, data1=fzB[:, :, h], initial=0.0,
                                                                          op0=ALU.mult, op1=ALU.add), ['fzB', 'ones_f16'], ['fzC'])
                tt(fzC, fzC, fzB, ALU.subtract, ['fzC', 'fzB'], ['fzC'])
                for part in range(3):
                    mm(pb[1][:, 0:NT * 8], tri_b, lfs[:, :, part, :], part == 0, part == 2, ['tri_b', 'lfs'], [pk(1)])
                tt(fzA, pb[1][:, 0:NT * 8].rearrange("p (t h) -> p t h", t=NT), fzC, ALU.add, [pk(1), 'fzC'], ['fzA'])
                ts(fzA, fzA, -8.0, None, ALU.mult, None, ['fzA'], ['fzA'])
                CK = ['caug:%d' % i for i in range(NT)]; NCK = ['ncaug:%d' % i for i in range(NT)]
                cp(caug[:, :, :, 0], fzA, ['fzA'], CK)
                tt(caug[:, :, :, 1], fzA, caug[:, :, :, 0], ALU.subtract, ['fzA'] + CK, CK)
                ts(ncaug, caug, -1.0, None, ALU.mult, None, CK, NCK)
            W_ = HG * 64

            PBK = (0, 1, 6, 7)

            def proj_mm(gi):
                for tl, i in enumerate(range(gi * TG, (gi + 1) * TG)):
                    tcols = slice(i * 128, (i + 1) * 128)
                    for kc in range(8):
                        mm(pb[PBK[tl]][:, 0:3 * W_], aT[:, kc, tcols], wblk[:, kc, 0:3 * W_], kc == 0, kc == 7, ['wblk', aK(i)], [pk(PBK[tl])])
            proj_mm(0)
            for gi in range(NT // TG):
                tiles = range(gi * TG, (gi + 1) * TG)
                for half, (b0, b1) in enumerate(((0, 1), (6, 7))):
                    pv_ = PS[:, b0 * 512:(b1 + 1) * 512].rearrange("p (t c) -> p t c", t=2)
                    tsl = slice(2 * half, 2 * half + 2)
                    act(qk_sb[:, tsl, 0:HG, :], pv_[:, :, 0:W_].rearrange("p t (h d) -> p t h d", h=HG), AF.Copy, [pk(b0), pk(b1)], ['qk_sb'])
                    act(qk_sb[:, tsl, HG:2 * HG, :], pv_[:, :, W_:2 * W_].rearrange("p t (h d) -> p t h d", h=HG), AF.Copy, [pk(b0), pk(b1)], ['qk_sb'])
                    act(Vp[:, gi * TG + 2 * half:gi * TG + 2 * half + 2, :, 0:64], pv_[:, :, 2 * W_:3 * W_].rearrange("p t (h d) -> p t h d", h=HG), AF.Copy,
                        [pk(b0), pk(b1)], ['Vp:%d' % i for i in tiles])
                if gi + 1 < NT // TG:
                    proj_mm(gi + 1)
                tt(qk_sq, qk_sb, qk_sb, ALU.mult, ['qk_sb'], ['qk_sq'])
                red(rsq, qk_sq, ['qk_sq'], ['rsq'])
                rsqrt_chain(rsq2, rsq, 64, ['rsq'], ['rsq2'], rsq)
                tt(qk_sq, qk_sb, rsq2.unsqueeze(3).to_broadcast([128, TG, 2 * HG, 64]), ALU.mult, ['qk_sb', 'rsq2'], ['qk_sq'])
                ag = aug[gi % 2]; agk = 'aug%d' % (gi % 2)
                tt(ag[:, :, :, 0:64], qk_sq, gqk.unsqueeze(1).to_broadcast([128, TG, 2 * HG, 64]), ALU.mult, ['qk_sq', 'gqk'], [agk])
                cp(ag[:, :, 0:HG, 64:66], caug[:, gi * TG:(gi + 1) * TG, h0:h0 + HG, :], ['caug:%d' % i for i in tiles], [agk])
                cp(ag[:, :, HG:2 * HG, 66:68], ncaug[:, gi * TG:(gi + 1) * TG, h0:h0 + HG, :], ['ncaug:%d' % i for i in tiles], [agk])
                for tl in range(TG):
                    for j in range(2 * HG):
                        blk = tl * 2 * HG + j
                        tr(pbt2[0:68, blk * 128:(blk + 1) * 128], ag[:, tl, j, :], ident_b, [agk, 'ident_b'], [pk(4), pk(5)])
                pv = pbt2[0:68, :].rearrange("p (t j x) -> p j t x", t=TG, j=2 * HG)
                gcols = slice(gi * TG * 128, (gi + 1) * TG * 128)
                cp(QT[:, :, gcols].rearrange("p h (t x) -> p h t x", t=TG), pv[:, 0:HG], [pk(4), pk(5)], ['QT:%d' % i for i in tiles])
                cp(KT[:, :, gcols].rearrange("p h (t x) -> p h t x", t=TG), pv[:, HG:2 * HG], [pk(4), pk(5)], ['KT:%d' % i for i in tiles])
            for hh in range(HG):
                h = h0 + hh
                for qg in range(4):
                    ob = 4 + (gcount[0] % 2); gcount[0] += 1
                    mset(pb[ob], 0.0, [pk(ob)])
                    its = []
                    for kt in range(4 * qg + 4):
                        q0 = max(kt, 4 * qg); N = (4 * qg + 4 - q0) * 128
                        its.append((kt, q0, N, (2, 3, 6)[itc[0] % 3], PT[itc[0] % 3], 'PT%d' % (itc[0] % 3))); itc[0] += 1

                    def qk(itm):
                        kt, q0, N, sbk, ptt, ptk = itm
                        mm(pb[sbk][:, 0:N], KT[:, hh, kt * 128:(kt + 1) * 128], QT[:, hh, q0 * 128:(4 * qg + 4) * 128], True, True,
                           ['KT:%d' % kt] + ['QT:%d' % j for j in range(q0, 4 * qg + 4)], [pk(sbk)])
                    qk(its[0])
                    if len(its) > 1: qk(its[1])
                    for n_, itm in enumerate(its):
                        kt, q0, N, sbk, ptt, ptk = itm
                        if n_ + 2 < len(its): qk(its[n_ + 2])
                        act(ptt[:, 0:N], pb[sbk][:, 0:N], AF.Exp, [pk(sbk)], [ptk], scale=0.125)
                        if kt >= 4 * qg:
                            tt(ptt[:, 0:128], ptt[:, 0:128], tri_b, ALU.mult, [ptk, 'tri_b'], [ptk])
                        for qb in range(q0, 4 * qg + 4):
                            j = qb - 4 * qg
                            mm(pb[ob][:, j * 128:j * 128 + 65], ptt[:, (qb - q0) * 128:(qb - q0 + 1) * 128], Vp[:, kt, hh, :],
                               False, kt == qb, [ptk, 'Vp:%d' % kt], [pk(ob)], skip=True)
                    ov = pb[ob].rearrange("p (j c) -> p j c", j=4)
                    recip(sm[:, 12:16], ov[:, :, 64], [pk(ob)], ['sml'])
                    tt(fox[:, 4 * qg:4 * qg + 4, h * 64:(h + 1) * 64], ov[:, :, 0:64], sm[:, 12:16].unsqueeze(2).to_broadcast([128, 4, 64]),
                       ALU.mult, [pk(ob), 'sml'], ['fox:%d' % qg])
        for i in range(NT):
            fk = 'fox:%d' % (i // 4)
            jn = foxn[i % 2]; jk = 'foxn%d' % (i % 2)
            tt(jn, fox[:, i, :], fox[:, i, :], ALU.mult, [fk], [jk])
            red(fss[:, i:i + 1], jn, [jk], ['fss'])
        rsqrt_chain(frs, fss, 512, ['fss'], ['frs'], fss)
        def fox_norm(i):
            stt(foxn[i % 2], fox[:, i, :], frs[:, i:i + 1], gfox_bc, ALU.mult, ALU.mult, ['fox:%d' % (i // 4), 'frs', 'gfox_bc'], ['foxn%d' % (i % 2)])
        fox_norm(0)
        for i in range(NT):
            fk = 'fox:%d' % (i // 4); p = i % 2
            if i + 1 < NT: fox_norm(i + 1)
            pbt = pb[4 + p].bitcast(BF16)
            for j in range(4):
                tr(pbt[:, j * 128:(j + 1) * 128], foxn[p][:, j * 128:(j + 1) * 128], ident_b, ['foxn%d' % p, 'ident_b'], [pk(4 + p)])
            cp(foxT[p], pbt[:, 0:512].rearrange("p (c t) -> p c t", c=4), [pk(4 + p)], ['foxT%d' % p])
            dma(mix_d[b, i, :, 0:4, :], foxT[p], ['foxT%d' % p], ['mixf:%d' % i])

        areset()
        mixt = [al([128, 8, 128], BF16) for _ in range(2)]
        h1t = [al([128, D]) for _ in range(2)]
        load_w(wblk, 'wblk', w_xq, 8, 1024)

        def p3_L(i):
            tcols = slice(i * 128, (i + 1) * 128)
            s = i % 2
            dma(mixt[s], mix_d[b, i], ['mixf:%d' % i, 'mixd:%d' % (i // 4)], ['mixt%d' % s])
            dma(xt[s], x[b, tcols, :], (), ['xt%d' % s])

        def p3_A(i):
            tcols = slice(i * 128, (i + 1) * 128)
            s = i % 2; mt_ = mixt[s]; mk = 'mixt%d' % s; hk_ = 'h1t%d' % s
            for hf in range(2):
                for kc in range(8):
                    mm(pb[hf], mt_[:, kc, :], wblk2[:, kc, hf * 512:(hf + 1) * 512], kc == 0, kc == 7, [mk, 'wblk2'], [pk(hf)])
                tt(h1t[s][:, hf * 512:(hf + 1) * 512], pb[hf], xt[s][:, hf * 512:(hf + 1) * 512], ALU.add, [pk(hf), 'xt%d' % s], [hk_])
            dma(h1_d[b, tcols, :], h1t[s], [hk_], ['h1d:%d' % i])

        p3_L(0)
        for step in range(NT + 1):
            if step + 1 < NT: p3_L(step + 1)
            if step < NT:
                p3_A(step)
                stats_chain(h1t[step % 2], 'h1t%d' % (step % 2), par=step % 2)
            if step >= 1:
                j = step - 1
                stats_T(aT[:, :, j * 128:(j + 1) * 128], aK(j), g_cross, 'g_cross', par=j % 2)

        areset()
        xkT = al([128, 4, 2, NMEM], BF16)
        xvp = al([128, 2, 4, 257], BF16)
        a_save = aoff[0]
        memT = al([128, 8, NMEM], BF16)
        xk_sb = al([128, 2, 256]); xk_sq = al([128, 2, 256]); xkn = al([128, 2, 256], BF16)
        mset(xvp, 1.0, ['xvp'])
        for mt in range(2):
            s = mt % 2
            dma(xt[s], mem[b, mt * 128:(mt + 1) * 128, :], (), ['xt%d' % s])
            tile_stats_T(xt[s], 'xt%d' % s, memT[:, :, mt * 128:(mt + 1) * 128], 'memT', g_mem, 'g_mem', par=s)
        for cbk in range(4):
            wkv = wblk2[:, :, (cbk % 2) * 512:(cbk % 2 + 1) * 512]; wkvk = 'wblk2h%d' % (cbk % 2)
            P.add('gq', lambda wkv=wkv, cbk=cbk: nc.gpsimd.dma_start(out=wkv, in_=w_xkv[:, cbk * 512:(cbk + 1) * 512].rearrange("(c p) n -> p c n", p=128)),
                  (), [wkvk] + (['wblk2'] if cbk < 2 else []))
            for mt in range(2):
                bank = mt
                for kc in range(8):
                    mm(pb[bank], memT[:, kc, mt * 128:(mt + 1) * 128], wkv[:, kc, :], kc == 0, kc == 7, ['memT', wkvk], [pk(bank)])
                if cbk < 2:
                    act(xk_sb, pb[bank].rearrange("p (h d) -> p h d", h=2), AF.Copy, [pk(bank)], ['xk_sb'])
                    tt(xk_sq, xk_sb, xk_sb, ALU.mult, ['xk_sb'], ['xk_sq'])
                    red(sm[:, 48:50], xk_sq, ['xk_sq'], ['smq'])
                    rsqrt_chain(sm[:, 56:58], sm[:, 48:50], 256, ['smq'], ['smr'], sm[:, 4:6])
                    tt(xk_sq, xk_sb, sm[:, 56:58].unsqueeze(2).to_broadcast([128, 2, 256]), ALU.mult, ['xk_sb', 'smr'], ['xk_sq'])
                    tt(xkn, xk_sq, gxk_bc.unsqueeze(1).to_broadcast([128, 2, 256]), ALU.mult, ['xk_sq', 'gxk_bc'], ['xkn'])
                    pbt = pb[5].bitcast(BF16)
                    for j in range(4):
                        tr(pbt[:, j * 128:(j + 1) * 128], xkn[:, j // 2, (j % 2) * 128:(j % 2 + 1) * 128], ident_b, ['xkn', 'ident_b'], [pk(5)])
                    cp(xkT[:, 2 * cbk:2 * cbk + 2, :, mt * 128:(mt + 1) * 128],
                       pbt[:, 0:512].rearrange("p (h c m) -> p h c m", h=2, c=2), [pk(5)], ['xkT'])
                else:
                    hv = 2 * (cbk - 2)
                    act(xvp[:, mt, hv:hv + 2, 0:256], pb[bank].rearrange("p (h d) -> p h d", h=2), AF.Copy, [pk(bank)], ['xvp'])
        P.add('gq', lambda: nc.gpsimd.dma_start(out=wblk2, in_=w_xo.rearrange("(c p) n -> p c n", p=128)), (), ['wblk2', 'wblk2h0', 'wblk2h1'])
        P.barrier(); aoff[0] = a_save
        xq_sb = [al([128, 4, 256]) for _ in range(2)]; xq_sq = [al([128, 4, 256]) for _ in range(2)]
        xqn = [al([128, 4, 256], BF16) for _ in range(2)]; xqT = [al([128, 8, 128], BF16) for _ in range(2)]
        PTx = [al([128, 1024], BF16) for _ in range(2)]; xo_sb = [al([128, 1024], BF16) for _ in range(2)]
        xoT = [al([128, 8, 128], BF16) for _ in range(2)]
        h1t = [al([128, D]) for _ in range(2)]
        pbt = pb[5].bitcast(BF16)

        def p4_A1(i):
            tcols = slice(i * 128, (i + 1) * 128)
            p = i % 2
            dma(xt[p], h1_d[b, tcols, :], ['h1d:%d' % i], ['xt%d' % p])
            for hf in range(2):
                for kc in range(8):
                    mm(pb[hf], aT[:, kc, tcols], wblk[:, kc, hf * 512:(hf + 1) * 512], kc == 0, kc == 7, [aK(i), 'wblk'], [pk(hf)])

        def p4_A1n(i):
            p = i % 2
            c0_ = 24 + 8 * p
            mset(sm[:, c0_:c0_ + 4], 0.0, ['smq%d' % p])
            for h in range(4):
                src = pb[h // 2][:, (h % 2) * 256:(h % 2 + 1) * 256]
                act(xq_sq[p][:, h, :], src, AF.Square, [pk(h // 2)], ['xq_sq%d' % p, 'smq%d' % p], accum=sm[:, c0_ + h:c0_ + h + 1])
            rsqrt_chain(sm[:, c0_ + 4:c0_ + 8], sm[:, c0_:c0_ + 4], 256, ['smq%d' % p], ['smr%d' % p], sm[:, c0_:c0_ + 4])
            for h in range(4):
                src = pb[h // 2][:, (h % 2) * 256:(h % 2 + 1) * 256]
                stt(xqn[p][:, h, :], src, sm[:, c0_ + 4 + h:c0_ + 5 + h], gxq_bc, ALU.mult, ALU.mult, [pk(h // 2), 'smr%d' % p, 'gxq_bc'], ['xqn%d' % p])

        def p4_A2(i):
            p = i % 2
            for j in range(8):
                tr(pbt[:, j * 128:(j + 1) * 128], xqn[p][:, j // 2, (j % 2) * 128:(j % 2 + 1) * 128], ident_b, ['xqn%d' % p, 'ident_b'], [pk(5)])
            cp(xqT[p], pbt.rearrange("p (j t) -> p j t", j=8), [pk(5)], ['xqT%d' % p])

        def p4_B1(i):
            p = i % 2
            for h in range(4):
                for mt in range(2):
                    bank = 2 + h // 2; c0_ = ((h % 2) * 2 + mt) * 128
                    for dc in range(2):
                        mm(pb[bank][:, c0_:c0_ + 128], xkT[:, h, dc, mt * 128:(mt + 1) * 128], xqT[p][:, h * 2 + dc, :], dc == 0, dc == 1,
                           ['xkT', 'xqT%d' % p], [pk(bank)])
            for hb in range(2):
                act(PTx[p][:, hb * 512:(hb + 1) * 512], pb[2 + hb], AF.Exp, [pk(2 + hb)], ['PTx%d' % p], scale=1.0 / 16)

        def p4_B2(i):
            p = i % 2
            for h in range(4):
                bank = 4 if h % 2 == 0 else 7
                for mt in range(2):
                    mm(pb[bank][:, 0:257], PTx[p][:, (h * 2 + mt) * 128:(h * 2 + mt + 1) * 128], xvp[:, mt, h, :], mt == 0, mt == 1,
                       ['PTx%d' % p, 'xvp'], [pk(bank)])
                c_ = 40 + 4 * p + h
                recip(sm[:, c_:c_ + 1], pb[bank][:, 256:257], [pk(bank)], ['sml%d' % c_])
                ts(xo_sb[p][:, h * 256:(h + 1) * 256], pb[bank][:, 0:256], sm[:, c_:c_ + 1], None, ALU.mult, None, [pk(bank), 'sml%d' % c_], ['xo_sb%d' % p])
            for j in range(8):
                tr(pbt[:, j * 128:(j + 1) * 128], xo_sb[p][:, j * 128:(j + 1) * 128], ident_b, ['xo_sb%d' % p, 'ident_b'], [pk(5)])
            cp(xoT[p], pbt.rearrange("p (j t) -> p j t", j=8), [pk(5)], ['xoT%d' % p])

        def p4_C1(i):
            tcols = slice(i * 128, (i + 1) * 128)
            p = i % 2
            for hf in range(2):
                for kc in range(8):
                    mm(pb[hf], xoT[p][:, kc, :], wblk2[:, kc, hf * 512:(hf + 1) * 512], kc == 0, kc == 7, ['xoT%d' % p, 'wblk2'], [pk(hf)])
                tt(h1t[p][:, hf * 512:(hf + 1) * 512], pb[hf], xt[p][:, hf * 512:(hf + 1) * 512], ALU.add, [pk(hf), 'xt%d' % p], ['h1t%d' % p])
            dma(h2_d[b, tcols, :], h1t[p], ['h1t%d' % p], ['h2d:%d' % i])
        for step in range(NT + 2):
            ic = step - 2; ib = step - 1; ia = step
            if 0 <= ib < NT: p4_B1(ib)
            if 0 <= ic < NT: p4_C1(ic)
            if 0 <= ib < NT: p4_B2(ib)
            if 0 <= ic < NT: stats_chain(h1t[ic % 2], 'h1t%d' % (ic % 2), par=ic % 2)
            if 0 <= ia < NT: p4_A1(ia)
            if 0 <= ia < NT: p4_A1n(ia)
            if 0 <= ic < NT: stats_T(aT[:, :, ic * 128:(ic + 1) * 128], aK(ic), g_ffn, 'g_ffn', par=ic % 2)
            if 0 <= ia < NT: p4_A2(ia)

        areset()
        Gs = [al([128, 514]) for _ in range(2)]; acc = [al([128, 512]) for _ in range(2)]; sl = [al([128, 512]) for _ in range(2)]
        hidb = [al([128, 512], BF16) for _ in range(2)]
        wd = al([128, NC_FF, 512], BF16)
        hidt = [al([128, NC_FF, 256], BF16) for _ in range(2)]
        it5 = 0
        p5_pend = []

        def p5_fin():
            while p5_pend:
                k_, c_, tb_, bu__ = p5_pend.pop()
                tt(hidb[k_], sl[k_], pb[bu__], ALU.mult, ['sl%d' % k_, pk(bu__)], ['hidb%d' % k_])
                for hh_ in range(2):
                    dma(hid_d[b, 2 * tb_ + hh_, :, c_, :], hidb[k_][:, hh_ * 256:(hh_ + 1) * 256], ['hidb%d' % k_], ['hid:%d:%d' % (tb_, hh_)])
        for cg in range((NC_FF + 3) // 4):
            cbase = cg * 4; ncg = min(4, NC_FF - cbase)
            if cg == 4:
                load_w(wd, 'wd', w_ffn_down[:, 0:512], NC_FF, 512)
            wb = wblk if cg % 2 == 0 else wblk2; wkk = 'wblk' if cg % 2 == 0 else 'wblk2'
            load_w(wb[:, :, 0:ncg * 128], wkk, w_ffn_up[:, cbase * 128:(cbase + ncg) * 128], 8, ncg * 128)
            load_w(wb[:, :, 512:512 + ncg * 128], wkk, w_ffn_up[:, DFF + cbase * 128:DFF + (cbase + ncg) * 128], 8, ncg * 128)
            for ci in range(ncg):
                c = cbase + ci
                for tb in range(4):
                    k = it5 % 2; it5 += 1
                    Gk = Gs[k]; gkk = 'Gs%d' % k; ak_ = 'acc%d' % k; sk_ = 'sl%d' % k; hk = 'hidb%d' % k
                    cols = slice(tb * 512, (tb + 1) * 512)
                    ak = [aK(4 * tb + j) for j in range(4)]
                    bg = 2 * k; bu_ = bg + 1
                    if tb == 0:
                        mset(Gk[:, 0:2], 0.0, [gkk])
                    for kc in range(8):
                        mm(pb[bg], wb[:, kc, ci * 128:(ci + 1) * 128], aT[:, kc, cols], kc == 0, kc == 7, [wkk] + ak, [pk(bg)])
                    for kc in range(8):
                        mm(pb[bu_], wb[:, kc, 512 + ci * 128:512 + (ci + 1) * 128], aT[:, kc, cols], kc == 0, kc == 7, [wkk] + ak, [pk(bu_)])
                    act(Gk[:, 2:514], pb[bg], AF.Copy, [pk(bg)], [gkk])
                    if tb < 3:
                        cp(Gs[1 - k][:, 0:2], Gk[:, 512:514], [gkk], ['Gs%d' % (1 - k)])
                    ts(acc[k], Gk[:, 2:514], cw[:, 2, c:c + 1], cb[:, c:c + 1], ALU.mult, ALU.add, [gkk, 'cw', 'cb'], [ak_])
                    stt(acc[k], Gk[:, 1:513], cw[:, 1, c:c + 1], acc[k], ALU.mult, ALU.add, [gkk, 'cw', ak_], [ak_])
                    stt(acc[k], Gk[:, 0:512], cw[:, 0, c:c + 1], acc[k], ALU.mult, ALU.add, [gkk, 'cw', ak_], [ak_])
                    act(sl[k], acc[k], AF.Silu, [ak_], [sk_])
                    p5_fin()
                    p5_pend.append((k, c, tb, bu_))

        p5_fin()
        wdA = wblk.rearrange("p a (b n) -> p (a b) n", n=512)
        wdB = wblk2.rearrange("p a (b n) -> p (a b) n", n=512)
        load_w(wdA, 'wblk', w_ffn_down[0:16 * 128, 512:1024], 16, 512)
        load_w(wdB[:, 0:6, :], 'wblk2', w_ffn_down[16 * 128:22 * 128, 512:1024], 6, 512)
        for hf in range(2):

            def p5_L(i):
                s = i % 2; hs = (i // 2) % 2
                if i % 2 == 0:
                    dma(hidt[hs], hid_d[b, i // 2], ['hid:%d:%d' % (i // 4, (i // 2) % 2)], ['hidt%d' % hs])
                dma(xt[s][:, 0:512], h2_d[b, i * 128:(i + 1) * 128, hf * 512:(hf + 1) * 512], ['h2d:%d' % i], ['xt%d' % s])
            p5_L(0)
            for i in range(NT):
                tcols = slice(i * 128, (i + 1) * 128)
                s = i % 2; hs = (i // 2) % 2
                if i + 1 < NT: p5_L(i + 1)
                for c in range(NC_FF):
                    if hf == 0: wsl = wd[:, c, :]; wk_ = 'wd'
                    elif c < 16: wsl = wdA[:, c, :]; wk_ = 'wblk'
                    else: wsl = wdB[:, c - 16, :]; wk_ = 'wblk2'
                    mm(pb[s], hidt[hs][:, c, (i % 2) * 128:(i % 2 + 1) * 128], wsl, c == 0, c == NC_FF - 1, ['hidt%d' % hs, wk_], [pk(s)])
                tt(xt[s][:, 512:1024], pb[s], xt[s][:, 0:512], ALU.add, [pk(s), 'xt%d' % s], ['ot%d' % s])
                dma(out[b, tcols, hf * 512:(hf + 1) * 512], xt[s][:, 512:1024], ['ot%d' % s], ['out:%d:%d:%d' % (b, i, hf)], q='aq')
    P.emit(es)
    return nc, es


_PARAMS = ["norm_mix", "w_in", "fox_q_norm", "fox_k_norm", "fox_f_bias", "s5_a_re", "s5_a_im", "s5_log_dt", "s5_b_re", "s5_b_im",
           "s5_c_re", "s5_c_im", "s5_d", "s5_w_glu", "s5_b_glu", "out_norm_fox", "out_norm_s5", "w_out", "norm_cross", "norm_mem",
           "w_xq", "w_xkv", "xq_norm", "xk_norm", "w_xo", "norm_ffn", "w_ffn_up", "ffn_conv_w", "ffn_conv_b", "w_ffn_down"]


def kernel(**inputs):
    nc, es = build()
    with es:
        params = {k: np.ascontiguousarray(np.asarray(inputs[k], dtype=np.float32)[0]) for k in _PARAMS}
        x = np.asarray(inputs["x"], dtype=np.float32); mem = np.asarray(inputs["mem"], dtype=np.float32)
        in_maps = []
        for c in range(8):
            m = dict(params)
            m["x"] = np.ascontiguousarray(x[NB * c:NB * (c + 1)])
            m["mem"] = np.ascontiguousarray(mem[NB * c:NB * (c + 1)])
            in_maps.append(m)
        res = run_bass_kernel_spmd(nc, in_maps, core_ids=list(range(8)))
    return np.concatenate([r["out"] for r in res.results], axis=0).astype(np.float32)
```

```python
import math
from contextlib import ExitStack
import numpy as np
import concourse.bass as bass
import concourse.mybir as mybir
from concourse.bass_utils import run_bass_kernel_spmd

F32 = mybir.dt.float32
BF16 = mybir.dt.bfloat16
AF = mybir.ActivationFunctionType
ALU = mybir.AluOpType
AX = mybir.AxisListType

D = 1024; L = 2048; NT = 16; NB = 2; NMEM = 256; DFF = 2816; NC_FF = 22
EPS = 1e-6
ENGS = ['sp', 'pe', 'act', 'dve', 'pool']
NDS = 16
DEBUG = None


class Prog:
    def __init__(self, nc):
        self.nc = nc; self.ops = []; self.lastw = {}; self.rd = {}
        self.bar = set(); self.last_eng = {}; self.ndma = 0; self.last_slot = {}; self.gq_since = []

    def barrier(self):
        self.bar = set(self.last_eng.values()) | set(self.last_slot.values()) | set(self.gq_since)
        self.gq_since = []

    def add(self, eng, fn, r=(), w=()):
        i = len(self.ops); deps = set(self.bar)
        for k in list(r) + list(w):
            if k in self.lastw: deps.add(self.lastw[k])
        for k in w:
            rdk = self.rd.get(k)
            if rdk:
                deps.update(rdk[0].values()); deps.update(rdk[1])
        self.ops.append(dict(eng=eng, fn=fn, deps=deps, sig=False))
        for k in w:
            self.lastw[k] = i; self.rd[k] = ({}, [])
        for k in r:
            rdk = self.rd.setdefault(k, ({}, []))
            if eng in ('sp', 'gq', 'aq'): rdk[1].append(i)
            else: rdk[0][eng] = i
        if eng in ('sp', 'aq'):
            self.last_slot[self.ndma % NDS] = i; self.ndma += 1
        elif eng == 'gq':
            self.gq_since.append(i)
        else:
            self.last_eng[eng] = i
        return i

    def emit(self, es):
        nc = self.nc; ops = self.ops
        for op in ops:
            for d in op['deps']:
                if ops[d]['eng'] == 'pe' and op['eng'] == 'pe': continue
                ops[d]['sig'] = True
        cnt = {e: 0 for e in ENGS}; di = 0; qi = 0
        for op in ops:
            e = op['eng']
            if e in ('sp', 'aq'):
                op['dsem'] = di % NDS; op['dval'] = 16 * (di // NDS + 1); di += 1
            elif e == 'gq':
                op['dsem'] = NDS + qi; op['dval'] = 16; qi += 1
            elif op['sig']:
                cnt[e] += 1; op['ord'] = cnt[e]
        esem = {e: es.enter_context(nc.semaphore("s_" + e)) for e in ENGS if e != 'sp'}
        dsem = [es.enter_context(nc.semaphore("d_%d" % i)) for i in range(NDS + qi)]
        dfinal = [0] * (NDS + qi)
        for op in ops:
            if op['eng'] in ('sp', 'gq', 'aq'): dfinal[op['dsem']] = op['dval']

        def run(e, eng):
            waited = {}

            def wait(key, sem, val):
                if waited.get(key, 0) >= val: return
                eng.wait_ge(sem, val); waited[key] = val
            for op in ops:
                oe = op['eng']
                if {'gq': 'pool', 'aq': 'act'}.get(oe, oe) != e: continue
                for d in sorted(op['deps']):
                    dop = ops[d]
                    if dop['eng'] == 'pe' and oe == 'pe': continue
                    if dop['eng'] in ('sp', 'gq', 'aq'): wait(('d', dop['dsem']), dsem[dop['dsem']], dop['dval'])
                    else: wait(dop['eng'], esem[dop['eng']], dop['ord'])
                if oe in ('sp', 'gq', 'aq'):
                    if op['dval'] > 16: wait(('d', op['dsem']), dsem[op['dsem']], op['dval'] - 16)
                    op['fn']().then_inc(dsem[op['dsem']], 16)
                else:
                    ins = op['fn']()
                    if op['sig']: ins.then_inc(esem[e], 1)
            if e == 'sp':
                for i in range(len(dfinal)):
                    if dfinal[i]: wait(('d', i), dsem[i], dfinal[i])
        block = es.enter_context(nc.Block())

        @block.sync
        def _(eng): run('sp', eng)

        @block.tensor
        def _(eng): run('pe', eng)

        @block.scalar
        def _(eng): run('act', eng)

        @block.vector
        def _(eng): run('dve', eng)

        @block.gpsimd
        def _(eng): run('pool', eng)
        print("ops:", len(ops), {e: sum(1 for o in ops if o['eng'] == e) for e in ENGS + ['gq']}, "signals:", cnt)


def build():
    nc = bass.Bass("TRN2", target_bir_lowering=False)
    es = ExitStack()
    P = Prog(nc)

    def din(name, shape): return nc.dram_tensor(name, shape, F32, kind="ExternalInput").ap()
    x = din("x", [NB, L, D]); mem = din("mem", [NB, NMEM, D])
    norm_mix = din("norm_mix", [D]); w_in = din("w_in", [D, 2056])
    fox_q_norm = din("fox_q_norm", [64]); fox_k_norm = din("fox_k_norm", [64]); fox_f_bias = din("fox_f_bias", [8])
    s5_a_re = din("s5_a_re", [32, 64]); s5_a_im = din("s5_a_im", [32, 64]); s5_log_dt = din("s5_log_dt", [32])
    s5_b_re = din("s5_b_re", [32, 64, 16]); s5_b_im = din("s5_b_im", [32, 64, 16])
    s5_c_re = din("s5_c_re", [32, 16, 64]); s5_c_im = din("s5_c_im", [32, 16, 64]); s5_d = din("s5_d", [32, 16])
    s5_w_glu = din("s5_w_glu", [512, 512]); s5_b_glu = din("s5_b_glu", [512])
    out_norm_fox = din("out_norm_fox", [512]); out_norm_s5 = din("out_norm_s5", [512]); w_out = din("w_out", [D, D])
    norm_cross = din("norm_cross", [D]); norm_mem = din("norm_mem", [D]); w_xq = din("w_xq", [D, D])
    w_xkv = din("w_xkv", [D, 2 * D]); xq_norm = din("xq_norm", [256]); xk_norm = din("xk_norm", [256])
    w_xo = din("w_xo", [D, D]); norm_ffn = din("norm_ffn", [D]); w_ffn_up = din("w_ffn_up", [D, 2 * DFF])
    ffn_conv_w = din("ffn_conv_w", [3, DFF]); ffn_conv_b = din("ffn_conv_b", [DFF]); w_ffn_down = din("w_ffn_down", [DFF, D])
    out = nc.dram_tensor("out", [NB, L, D], F32, kind="ExternalOutput").ap()
    h1_d = nc.dram_tensor("h1_d", [NB, L, D], F32, kind="Internal").ap()
    h2_d = nc.dram_tensor("h2_d", [NB, L, D], F32, kind="Internal").ap()
    hid_d = nc.dram_tensor("hid_d", [NB, 8, 128, NC_FF, 256], BF16, kind="Internal").ap()
    mix_d = nc.dram_tensor("mix_d", [NB, NT, 128, 8, 128], BF16, kind="Internal").ap()

    def sb(name, shape, dt=F32): return es.enter_context(nc.sbuf_tensor(name, shape, dt))[:]

    AW = 15580
    arena = es.enter_context(nc.sbuf_tensor("arena", [128, AW], F32))
    aoff = [0]

    def areset():
        aoff[0] = 0; P.barrier()

    def al(shape, dt=F32):
        n = 1
        for v in shape[1:]: n *= v
        nw = (n + 1) // 2 if dt == BF16 else n
        nw = (nw + 7) // 8 * 8
        assert aoff[0] + nw <= AW, ("arena overflow", aoff[0], nw)
        v = arena[:, aoff[0]:aoff[0] + nw]; aoff[0] += nw
        if dt == BF16: v = v.bitcast(BF16)
        v = v[0:shape[0], 0:n]
        if len(shape) > 2:
            names = " ".join("a%d" % i for i in range(len(shape) - 1))
            v = v.rearrange("p (%s) -> p %s" % (names, names), **{"a%d" % i: shape[i + 1] for i in range(len(shape) - 1)})
        return v

    def dma(o, i, r, w, q='sp'):
        e = nc.sync if q == 'sp' else nc.scalar
        P.add(q, lambda: e.dma_start(out=o, in_=i, allow_slow_non_contiguous=True), r, w)

    def mm(o, lhsT, rhs, start, stop, r, w, skip=False, tp=None):
        if tp is None:
            P.add('pe', lambda: nc.tensor.matmul(o, lhsT=lhsT, rhs=rhs, start=start, stop=stop, skip_group_check=skip), r, w)
        else:
            P.add('pe', lambda: nc.tensor.matmul(o, lhsT=lhsT, rhs=rhs, start=start, stop=stop, skip_group_check=skip, tile_position=tp), r, w)

    def tr(o, i, ident, r, w):
        P.add('pe', lambda: nc.tensor.transpose(o, i, ident), r, w)

    def act(o, i, func, r, w, scale=None, bias=None, accum=None):
        kw = {}
        if scale is not None: kw['scale'] = scale
        if bias is not None: kw['bias'] = bias
        if accum is not None: kw['accum_out'] = accum
        P.add('act', lambda: nc.scalar.activation(out=o, in_=i, func=func, **kw), r, w)

    DEF = ['dve']

    def tt(o, a, b, op, r, w, eng=None):
        eng = eng or DEF[0]
        e = nc.vector if eng == 'dve' else nc.gpsimd
        P.add(eng, lambda: e.tensor_tensor(out=o, in0=a, in1=b, op=op), r, w)

    def ts(o, a, s1, s2, op0, op1, r, w, eng=None):
        eng = eng or DEF[0]
        e = nc.vector if eng == 'dve' else nc.gpsimd
        if op1 is None:
            P.add(eng, lambda: e.tensor_scalar(out=o, in0=a, scalar1=s1, scalar2=None, op0=op0), r, w)
        else:
            P.add(eng, lambda: e.tensor_scalar(out=o, in0=a, scalar1=s1, scalar2=s2, op0=op0, op1=op1), r, w)

    def stt(o, a, s, b, op0, op1, r, w):
        P.add('dve', lambda: nc.vector.scalar_tensor_tensor(out=o, in0=a, scalar=s, in1=b, op0=op0, op1=op1), r, w)

    def cp(o, i, r, w, eng=None):
        eng = eng or DEF[0]
        e = nc.vector if eng == 'dve' else nc.gpsimd
        P.add(eng, lambda: e.tensor_copy(out=o, in_=i), r, w)

    def recip(o, i, r, w):
        P.add('dve', lambda: nc.vector.reciprocal(out=o, in_=i), r, w)

    def red(o, i, r, w):
        P.add('dve', lambda: nc.vector.tensor_reduce(out=o, in_=i, axis=AX.X, op=ALU.add), r, w)

    def mset(o, v, w, eng=None):
        eng = eng or DEF[0]
        e = nc.vector if eng == 'dve' else nc.gpsimd
        P.add(eng, lambda: e.memset(o, v), (), w)

    def rsqrt_chain(o, ss, n, keyr, keyw, tmp):
        ts(tmp, ss, 1.0 / n, EPS, ALU.mult, ALU.add, keyr, ['rs_tmp'])
        act(tmp, tmp, AF.Ln, ['rs_tmp'], ['rs_tmp'])
        act(o, tmp, AF.Exp, ['rs_tmp'], keyw, scale=-0.5)

    ident_f = sb("ident_f", [128, 128]); ident_b = sb("ident_b", [128, 128], BF16)
    ones_f = al([128, 128]); ones_b = sb("ones_b", [128, 128], BF16)
    tri_f = al([128, 128]); tri_b = sb("tri_b", [128, 128], BF16)
    mset(ones_f, 1.0, ['ones_f'], 'pool')
    cp(ones_b, ones_f, ['ones_f'], ['ones_b'], 'pool')
    P.add('pool', lambda: nc.gpsimd.affine_select(out=ident_f, in_=ones_f, pattern=[[1, 128]], compare_op=ALU.is_equal,
                                                  fill=0.0, base=0, channel_multiplier=-1), ['ones_f'], ['ident_f'])
    P.add('pool', lambda: nc.gpsimd.affine_select(out=tri_f, in_=ones_f, pattern=[[1, 128]], compare_op=ALU.is_ge,
                                                  fill=0.0, base=0, channel_multiplier=-1), ['ones_f'], ['tri_f'])
    cp(ident_b, ident_f, ['ident_f'], ['ident_b'], 'pool')
    cp(tri_b, tri_f, ['tri_f'], ['tri_b'], 'pool')

    PS = es.enter_context(nc.psum_tensor("PS", [128, 4096], F32))[:]
    pb = [PS[:, 512 * i:512 * (i + 1)] for i in range(8)]
    def pk(i): return 'pb%d' % i

    def colload(name, src, n):
        t = sb(name, [128, n])
        dma(t, src.rearrange("(c p) -> p c", p=128), (), [name])
        return t

    def bcload(name, src, n):
        t = sb(name, [128, n])
        dma(t, src.partition_broadcast(128), (), [name])
        return t
    g_mix = colload("g_mix", norm_mix, 8); g_cross = colload("g_cross", norm_cross, 8)
    g_mem = colload("g_mem", norm_mem, 8); g_ffn = colload("g_ffn", norm_ffn, 8)
    g_s5 = colload("g_s5", out_norm_s5, 4); b_glu = colload("b_glu", s5_b_glu, 4)
    cb = colload("cb", ffn_conv_b, NC_FF)
    cw = sb("cw", [128, 3, NC_FF])
    for k in range(3):
        dma(cw[:, k, :], ffn_conv_w[k].rearrange("(c p) -> p c", p=128), (), ['cw'])
    gq_bc = bcload("gq_bc", fox_q_norm, 64); gk_bc = bcload("gk_bc", fox_k_norm, 64)
    fb_bc = bcload("fb_bc", fox_f_bias, 8); gfox_bc = bcload("gfox_bc", out_norm_fox, 512)
    gxq_bc = bcload("gxq_bc", xq_norm, 256); gxk_bc = bcload("gxk_bc", xk_norm, 256)

    def load_w(dst, dkey, src, kc, ncols, gain=None):
        for k0 in range(0, kc, 22):
            kn = min(22, kc - k0)
            d_ = dst[:, k0:k0 + kn, :]; s_ = src[k0 * 128:(k0 + kn) * 128, :].rearrange("(c p) n -> p c n", p=128)
            P.add('gq', lambda d_=d_, s_=s_: nc.gpsimd.dma_start(out=d_, in_=s_), (), [dkey])

    aT = sb("aT", [128, 8, L], BF16)
    xsc = sb("xsc", [128, D]); xsc2p = sb("xsc2p", [128, D])
    xt = [sb("xt%d" % i, [128, D]) for i in range(2)]
    sm = sb("sm", [128, 64])
    wblk = sb("wblk", [128, 8, 1024], BF16)
    wblk2 = sb("wblk2", [128, 8, 1024], BF16)
    wglu = sb("wglu", [128, 4, 512], BF16)

    def aK(i): return 'aT:%d' % i

    stat_bufs = [xsc, xsc2p]

    def stats_chain(src_tile, skey, par=0):
        xs_ = stat_bufs[par]; c0_ = 0 if par == 0 else 16
        kss = 'ss%d' % par; kr = 'rstd%d' % par; kx = 'xsc%d' % par
        mset(sm[:, c0_:c0_ + 1], 0.0, [kss])
        act(xs_, src_tile, AF.Square, [skey], [kx, kss], accum=sm[:, c0_:c0_ + 1])
        rsqrt_chain(sm[:, c0_ + 2:c0_ + 3], sm[:, c0_:c0_ + 1], D, [kss], [kr], sm[:, c0_ + 1:c0_ + 2])
        act(xs_, src_tile, AF.Copy, [skey, kr], [kx], scale=sm[:, c0_ + 2:c0_ + 3])

    def stats_T(dstT, dkey, gcol, gkey, par=0):
        xs_ = stat_bufs[par]; kx = 'xsc%d' % par
        for hf in range(2):
            bank = 6 + hf
            for j in range(4):
                kc = hf * 4 + j
                tr(pb[bank][:, j * 128:(j + 1) * 128], xs_[:, kc * 128:(kc + 1) * 128], ident_f, [kx, 'ident_f'], [pk(bank)])
            tt(dstT[:, 4 * hf:4 * hf + 4, :], pb[bank].rearrange("p (c t) -> p c t", c=4),
               gcol[:, 4 * hf:4 * hf + 4].unsqueeze(2).to_broadcast([128, 4, 128]), ALU.mult, [pk(bank), gkey], [dkey])

    def tile_stats_T(src_tile, skey, dstT, dkey, gcol, gkey, par=0):
        stats_chain(src_tile, skey, par)
        stats_T(dstT, dkey, gcol, gkey, par)

    p1_pending = []

    def p1_tile(b, i):
        s = i % 2
        dma(xt[s], x[b, i * 128:(i + 1) * 128, :], (), ['xt%d' % s])
        stats_chain(xt[s], 'xt%d' % s, par=s)
        p1_flush()
        p1_pending.append(i)

    def p1_flush():
        while p1_pending:
            j = p1_pending.pop()
            stats_T(aT[:, :, j * 128:(j + 1) * 128], aK(j), g_mix, 'g_mix', par=j % 2)

    def p1_tiles(b):
        for i in range(NT):
            p1_tile(b, i)
        p1_flush()
    DEF[0] = 'pool'
    T2 = 128
    Mlag = sb("Mlag", [128, 4, 8, 128], BF16)
    Ast32 = sb("Ast32", [128, 4, 8, 2, 128], BF16)
    Qst32 = sb("Qst32", [128, 16, 8, 2, 32], BF16)
    E8r = sb("E8r", [128, 16, T2 + 1]); E8i = sb("E8i", [128, 16, T2 + 1])
    r8 = sb("r8", [128, 16]); dcol = sb("dcol", [128, 4]); carry = sb("carry", [128, 16, 2])
    dma(dcol, s5_d.rearrange("(o g) c -> (g c) o", o=4), (), ['dcol'])
    load_w(wglu, 'wglu', s5_w_glu, 4, 512)

    are = al([128, 16]); aim = al([128, 16]); ldt = al([128, 16])
    for h in range(2):
        dma(are[h * 64:(h + 1) * 64, :], s5_a_re.rearrange("(q h) p -> h p q", h=2)[h], (), ['are'])
        dma(aim[h * 64:(h + 1) * 64, :], s5_a_im.rearrange("(q h) p -> h p q", h=2)[h], (), ['aim'])
        dma(ldt[h * 64:(h + 1) * 64, :], s5_log_dt.rearrange("(q h) -> h q", h=2)[h].partition_broadcast(64), (), ['ldt'])
    bre = al([128, 16, 16]); bim = al([128, 16, 16])
    for h in range(2):
        dma(bre[h * 64:(h + 1) * 64], s5_b_re.rearrange("(q h) p c -> h p q c", h=2)[h], (), ['bre'])
        dma(bim[h * 64:(h + 1) * 64], s5_b_im.rearrange("(q h) p c -> h p q c", h=2)[h], (), ['bim'])
    craw = [al([128, 4, 2, 64]) for k in range(2)]
    for k, src in enumerate((s5_c_re, s5_c_im)):
        for dup in range(2):
            dma(craw[k][:, :, dup, :], src.rearrange("(o g) c p -> (g c) o p", o=4), (), ['craw%d' % k])
    cre = al([128, 16, 16]); cim = al([128, 16, 16])
    for k, dst in enumerate((cre, cim)):
        for o in range(4):
            tr(pb[0][:, 0:128], craw[k][:, o].rearrange("p a b -> p (a b)"), ident_f, ['craw%d' % k, 'ident_f'], [pk(0)])
            v = pb[0][:, 0:128].rearrange("p (q h c) -> p q h c", q=4, h=2)
            for h in range(2):
                cp(dst[h * 64:(h + 1) * 64, o * 4:(o + 1) * 4, :], v[h * 64:(h + 1) * 64, :, h, :], [pk(0)], ['cre' if k == 0 else 'cim'], 'dve')
    S = al([128, 24, 16])
    def pl(i): return S[:, i, :]
    K5 = ['s5s']
    dt_ = pl(0); rho = pl(1); th = pl(2); mag = pl(3); cs = pl(4); sn = pl(5); lbr = pl(6); lbi = pl(7)
    den = pl(8); nr = pl(9); cfr = pl(10); cfi = pl(11); t1 = pl(12); t2 = pl(13); inv8 = pl(14)
    act(dt_, ldt, AF.Exp, ['ldt'], K5)
    tt(rho, are, dt_, ALU.mult, K5 + ['are'], K5)
    tt(th, aim, dt_, ALU.mult, K5 + ['aim'], K5)
    act(mag, rho, AF.Exp, K5, K5)
    act(r8, rho, AF.Exp, K5, ['r8'], scale=8.0)
    recip(inv8, r8, ['r8'], K5)
    MAGIC = 12582912.0
    TWO_PI = 2.0 * math.pi

    def sincos(o_s, o_c, ang, tmp1, tmp2, keys):
        for (o, off) in ((o_s, 0.0), (o_c, math.pi / 2)):
            ts(tmp1, ang, off, 1.0 / TWO_PI, ALU.add, ALU.mult, keys, keys)
            ts(tmp2, tmp1, MAGIC, None, ALU.add, None, keys, keys)
            ts(tmp2, tmp2, MAGIC, None, ALU.subtract, None, keys, keys)
            tt(tmp1, tmp1, tmp2, ALU.subtract, keys, keys)
            ts(tmp1, tmp1, TWO_PI, None, ALU.mult, None, keys, keys)
            ts(tmp1, tmp1, math.pi, -math.pi, ALU.min, ALU.max, keys, keys)
            act(o, tmp1, AF.Sin, keys, keys)
    sincos(sn, cs, th, t1, t2, K5)
    tt(lbr, mag, cs, ALU.mult, K5, K5); tt(lbi, mag, sn, ALU.mult, K5, K5)
    tt(den, are, are, ALU.mult, ['are'] + K5, K5); tt(t1, aim, aim, ALU.mult, ['aim'] + K5, K5)
    tt(den, den, t1, ALU.add, K5, K5); recip(den, den, K5, K5)
    ts(nr, lbr, -1.0, None, ALU.add, None, K5, K5)
    tt(t1, nr, are, ALU.mult, K5, K5); tt(t2, lbi, aim, ALU.mult, K5, K5); tt(cfr, t1, t2, ALU.add, K5, K5)
    tt(cfr, cfr, den, ALU.mult, K5, K5)
    tt(t1, lbi, are, ALU.mult, K5, K5); tt(t2, nr, aim, ALU.mult, K5, K5); tt(cfi, t1, t2, ALU.subtract, K5, K5)
    tt(cfi, cfi, den, ALU.mult, K5, K5)
    bbr = al([128, 16, 16]); bbi = al([128, 16, 16]); tb1 = al([128, 16, 16]); nbbi = al([128, 16, 16])
    KB = ['bb']
    cfr_b = cfr.unsqueeze(2).to_broadcast([128, 16, 16]); cfi_b = cfi.unsqueeze(2).to_broadcast([128, 16, 16])
    tt(bbr, bre, cfr_b, ALU.mult, K5 + ['bre'], KB); tt(tb1, bim, cfi_b, ALU.mult, K5 + ['bim'], KB)
    tt(bbr, bbr, tb1, ALU.subtract, KB, KB)
    tt(bbi, bim, cfr_b, ALU.mult, K5 + ['bim'], KB); tt(tb1, bre, cfi_b, ALU.mult, K5 + ['bre'], KB)
    tt(bbi, bbi, tb1, ALU.add, KB, KB)
    ts(nbbi, bbi, -1.0, None, ALU.mult, None, KB, ['nbbi'])

    def cdouble(Tr, Ti, nmax, tA, tB, key, tkey):
        n = 1
        while n < nmax:
            m = min(n, nmax - n)
            ar = Tr[:, :, 1:1 + m]; ai = Ti[:, :, 1:1 + m]
            br_ = Tr[:, :, n:n + 1].to_broadcast([128, 16, m]); bi_ = Ti[:, :, n:n + 1].to_broadcast([128, 16, m])
            tt(tA[:, :, 0:m], ar, br_, ALU.mult, [key], [tkey]); tt(tB[:, :, 0:m], ai, bi_, ALU.mult, [key], [tkey])
            tt(Tr[:, :, n + 1:n + 1 + m], tA[:, :, 0:m], tB[:, :, 0:m], ALU.subtract, [tkey], [key])
            tt(tA[:, :, 0:m], ar, bi_, ALU.mult, [key], [tkey]); tt(tB[:, :, 0:m], ai, br_, ALU.mult, [key], [tkey])
            tt(Ti[:, :, n + 1:n + 1 + m], tA[:, :, 0:m], tB[:, :, 0:m], ALU.add, [tkey], [key])
            n += m
    Et1 = al([128, 16, 64]); Et2 = al([128, 16, 64])
    Lr = al([128, 16, 9]); Li = al([128, 16, 9])
    mset(Lr[:, :, 0:1], 1.0, ['L']); mset(Li[:, :, 0:1], 0.0, ['L'])
    cp(Lr[:, :, 1:2], lbr.unsqueeze(2), K5, ['L']); cp(Li[:, :, 1:2], lbi.unsqueeze(2), K5, ['L'])
    cdouble(Lr, Li, 8, Et1, Et2, 'L', 'Et')
    KE = ['E8']
    mset(E8r[:, :, 0:1], 1.0, KE); mset(E8i[:, :, 0:1], 0.0, KE)
    tt(E8r[:, :, 1:2], Lr[:, :, 8:9], inv8.unsqueeze(2), ALU.mult, ['L'] + K5, KE)
    tt(E8i[:, :, 1:2], Li[:, :, 8:9], inv8.unsqueeze(2), ALU.mult, ['L'] + K5, KE)
    cdouble(E8r, E8i, T2, Et1, Et2, 'E8', 'Et')
    Gi = al([8, 128]); bdmask = al([128, 128]); mtmp = al([128, 128])
    mset(Gi, 1.0, ['Gi'], 'pool')
    P.add('pool', lambda: nc.gpsimd.affine_select(out=Gi, in_=Gi, pattern=[[1, 128]], compare_op=ALU.is_ge, fill=0.0, base=0,
                                                  channel_multiplier=-16), ['Gi'], ['Gi'])
    P.add('pool', lambda: nc.gpsimd.affine_select(out=Gi, in_=Gi, pattern=[[-1, 128]], compare_op=ALU.is_ge, fill=0.0, base=15,
                                                  channel_multiplier=16), ['Gi'], ['Gi'])
    mm(pb[1][:, 0:128], Gi, Gi, True, True, ['Gi'], [pk(1)])
    cp(bdmask, pb[1][:, 0:128], [pk(1)], ['bdmask'], 'dve')
    SOB = al([128, 4, 2, 128]); SOC = al([128, 4, 2, 128])

    def fill_SO(dst, dkey, srcs, keys):
        v = dst.rearrange("p o r (q h c) -> p o r q h c", q=4, h=2)
        for ri, src in enumerate(srcs):
            sv = src.rearrange("p (o q) c -> p o q c", q=4)
            for h in range(2):
                cp(v[h * 64:(h + 1) * 64, :, ri, :, h, :], sv[h * 64:(h + 1) * 64], keys, [dkey])
    mset(SOB, 0.0, ['SOB']); mset(SOC, 0.0, ['SOC']); mset(Qst32, 0.0, ['Qst32'])
    fill_SO(SOB, 'SOB', (bbr, nbbi), KB + ['nbbi'])
    DEF[0] = 'dve'
    SOA = al([128, 4, 2, 128]); LBr = al([128, 16, 16]); LBi = al([128, 16, 16]); LBt = al([128, 16, 16])
    mset(SOA, 0.0, ['SOA'])
    for s_ in range(8):
        k_ = 7 - s_
        lr_b = Lr[:, :, k_:k_ + 1].to_broadcast([128, 16, 16]); li_b = Li[:, :, k_:k_ + 1].to_broadcast([128, 16, 16])
        KC = ['LB']
        tt(LBr, bbr, lr_b, ALU.mult, KB + ['L'], KC); tt(LBt, bbi, li_b, ALU.mult, KB + ['L'], KC); tt(LBr, LBr, LBt, ALU.subtract, KC, KC)
        tt(LBi, bbi, lr_b, ALU.mult, KB + ['L'], KC); tt(LBt, bbr, li_b, ALU.mult, KB + ['L'], KC); tt(LBi, LBi, LBt, ALU.add, KC, KC)
        fill_SO(SOA, 'SOA', (LBr, LBi), KC)
        for o in range(4):
            for ri in range(2):
                tr(pb[1][:, 0:128], SOA[:, o, ri, :], ident_f, ['SOA', 'ident_f'], [pk(1)])
                cp(Ast32[:, o, s_, ri, :], pb[1][:, 0:128], [pk(1)], ['Ast32'], 'dve')

    DEF[0] = 'pool'
    CLr = al([128, 16, 16]); CLi = al([128, 16, 16]); CLt = al([128, 16, 16])
    for tau in range(9):
        if tau < 8:
            DEF[0] = 'dve'
            p1_tile(0, 2 * tau); p1_tile(0, 2 * tau + 1)
            if tau == 7: p1_flush()
            DEF[0] = 'pool'
        lr_b = Lr[:, :, tau:tau + 1].to_broadcast([128, 16, 16]); li_b = Li[:, :, tau:tau + 1].to_broadcast([128, 16, 16])
        KC = ['CL']
        tt(CLr, cre, lr_b, ALU.mult, ['cre', 'L'], KC); tt(CLt, cim, li_b, ALU.mult, ['cim', 'L'], KC); tt(CLr, CLr, CLt, ALU.subtract, KC, KC)
        tt(CLi, cre, li_b, ALU.mult, ['cre', 'L'], KC); tt(CLt, cim, lr_b, ALU.mult, ['cim', 'L'], KC); tt(CLi, CLi, CLt, ALU.add, KC, KC)
        if tau >= 1:
            for h in range(2):
                cp(Qst32[h * 64:(h + 1) * 64, :, tau - 1, 0, 16 * h:16 * h + 16], CLr[h * 64:(h + 1) * 64], KC, ['Qst32'])
                ts(Qst32[h * 64:(h + 1) * 64, :, tau - 1, 1, 16 * h:16 * h + 16], CLi[h * 64:(h + 1) * 64], -1.0, None, ALU.mult, None, KC, ['Qst32'])
        if tau <= 7:
            fill_SO(SOC, 'SOC', (CLr, CLi), KC)
            for o in range(4):
                bk = (0, 2, 3, 4, 5)[(tau * 4 + o) % 5]
                for ri in range(2):
                    mm(pb[bk][:, 0:128], SOB[:, o, ri, :], SOC[:, o, ri, :], ri == 0, ri == 1, ['SOB', 'SOC'], [pk(bk)])
                if tau == 0:
                    tt(mtmp, pb[bk][:, 0:128], bdmask, ALU.mult, [pk(bk), 'bdmask'], ['mtmp'], 'dve')
                    stt(Mlag[:, o, tau, :], ident_f, dcol[:, o:o + 1], mtmp, ALU.mult, ALU.add, ['ident_f', 'dcol', 'mtmp'], ['Mlag'])
                else:
                    tt(Mlag[:, o, tau, :], pb[bk][:, 0:128], bdmask, ALU.mult, [pk(bk), 'bdmask'], ['Mlag'], 'dve')
    DEF[0] = 'dve'
    HG = 2
    gcount = [0]; itc = [0]
    for b in range(NB):
        areset()
        if b > 0:
            p1_tiles(b)

        uT = al([128, 4, L], BF16); gyT = al([128, 4, L], BF16)
        y2T = al([128, 4, 512], BF16); sqT = al([128, 4, 512], BF16)
        Xbufs = [al([128, 4, 2, 257], BF16) for _ in range(2)]
        vre = al([128, 4, T2]); vim = al([128, 4, T2]); wre = al([128, 4, T2]); wim = al([128, 4, T2])
        c1 = al([128, 4, 1]); c2 = al([128, 4, 1]); c3 = al([128, 4, 1]); c4 = al([128, 4, 1])
        pA = sqT[:, 0, :].rearrange("p (q j) -> p q j", q=4); pB = sqT[:, 1, :].rearrange("p (q j) -> p q j", q=4)
        ytmp = al([128, 512]); ytmp2 = al([128, 512])
        for o in range(4):
            wb = wblk if o % 2 == 0 else wblk2; wkk = 'wblk' if o % 2 == 0 else 'wblk2'
            load_w(wb[:, :, 0:128], wkk, w_in[:, 1544 + o * 128:1544 + (o + 1) * 128], 8, 128)
            for tb in range(4):
                bank = tb % 2
                for kc in range(8):
                    mm(pb[bank], wb[:, kc, 0:128], aT[:, kc, tb * 512:(tb + 1) * 512], kc == 0, kc == 7,
                       [wkk] + [aK(4 * tb + j) for j in range(4)], [pk(bank)])
                act(uT[:, o, tb * 512:(tb + 1) * 512], pb[bank], AF.Copy, [pk(bank)], ['uT:%d' % o])

        load_w(wblk[:, :, 0:HG * 64], 'wblk', w_in[:, 0:HG * 64], 8, HG * 64)
        load_w(wblk[:, :, HG * 64:2 * HG * 64], 'wblk', w_in[:, 512:512 + HG * 64], 8, HG * 64)
        load_w(wblk[:, :, 2 * HG * 64:3 * HG * 64], 'wblk', w_in[:, 1024:1024 + HG * 64], 8, HG * 64)
        load_w(wblk[:, :, 768:776], 'wblk', w_in[:, 1536:1544], 8, 8)
        mset(carry, 0.0, ['carry'])
        Vv = PS[:, 1024:3072].rearrange("p (q x) -> p q x", q=4)[:, :, 0:2 * T2].rearrange("p q (r j) -> p q r j", r=2)
        VK = [pk(2), pk(3), pk(4), pk(5)]
        def s5_state(o, jh):
            uo = uT[:, o, :].rearrange("p (j s) -> p j s", s=8)
            uk = ['uT:%d' % o]
            Xbuf = Xbufs[o % 2]; xk_ = 'Xbuf%d' % (o % 2)
            if jh == 0:
                mset(Xbuf[:, :, :, 0:1], 0.0, [xk_])
            j0 = jh * T2
            mset(Vv, 0.0, VK)
            for s_ in range(8):
                for ri in range(2):
                    for q in range(4):
                        mm(Vv[:, q, ri, :], Ast32[32 * q:32 * q + 32, o, s_, ri, :], uo[32 * q:32 * q + 32, j0:j0 + T2, s_], False, s_ == 7,
                           ['Ast32'] + uk, VK, skip=True, tp=(32 * q, 0))
            return Xbuf, xk_, j0

        def s5_rot(o, jh, Xbuf, xk_, j0):
            er = E8r[:, 4 * o:4 * o + 4, 0:T2]; ei = E8i[:, 4 * o:4 * o + 4, 0:T2]
            Vr = Vv[:, :, 0, :]; Vi = Vv[:, :, 1, :]
            KS = ['s5w']
            tt(vre, Vr, er, ALU.mult, VK + ['E8'], ['vre']); tt(vim, Vi, er, ALU.mult, VK + ['E8'], ['vim'])
            tt(wre, Vi, ei, ALU.mult, VK + ['E8'], ['wre']); tt(wim, Vr, ei, ALU.mult, VK + ['E8'], ['wim'])
            tt(vre, vre, wre, ALU.add, ['vre', 'wre'], ['vre']); tt(vim, vim, wim, ALU.subtract, ['vim', 'wim'], ['vim'])
            for q in range(4):
                pr = 4 * o + q
                rb = r8[:, pr:pr + 1].to_broadcast([128, T2])
                for (w_, v_, ci_, wk_, vk_) in ((wre, vre, 0, 'wre', 'vre'), (wim, vim, 1, 'wim', 'vim')):
                    P.add('dve', lambda rb=rb, pr=pr, w_=w_, v_=v_, ci_=ci_, q=q: nc.vector.tensor_tensor_scan(
                        out=w_[:, q, :], data0=rb, data1=v_[:, q, :], initial=carry[:, pr, ci_:ci_ + 1], op0=ALU.mult, op1=ALU.add),
                        [vk_, 'carry', 'r8'], [wk_])
            xo_r = Xbuf[:, :, 0, 1 + j0:1 + j0 + T2]; xo_i = Xbuf[:, :, 1, 1 + j0:1 + j0 + T2]
            wl_r = wre[:, :, T2 - 1:T2]; wl_i = wim[:, :, T2 - 1:T2]
            eTr = E8r[:, 4 * o:4 * o + 4, T2:T2 + 1]; eTi = E8i[:, 4 * o:4 * o + 4, T2:T2 + 1]
            tt(c1, wl_r, eTr, ALU.mult, ['wre', 'E8'], ['c1']); tt(c2, wl_i, eTi, ALU.mult, ['wim', 'E8'], ['c2'])
            tt(c3, wl_i, eTr, ALU.mult, ['wim', 'E8'], ['c3']); tt(c4, wl_r, eTi, ALU.mult, ['wre', 'E8'], ['c4'])
            tt(carry[:, 4 * o:4 * o + 4, 0:1], c1, c2, ALU.subtract, ['c1', 'c2'], ['carry'])
            tt(carry[:, 4 * o:4 * o + 4, 1:2], c3, c4, ALU.add, ['c3', 'c4'], ['carry'])
            tt(pA, wre, er, ALU.mult, ['wre', 'E8'], ['sqT:0'], 'pool'); tt(pB, wim, ei, ALU.mult, ['wim', 'E8'], ['sqT:1'], 'pool')
            tt(xo_r, pA, pB, ALU.subtract, ['sqT:0', 'sqT:1'], [xk_], 'pool')
            tt(pA, wre, ei, ALU.mult, ['wre', 'E8'], ['sqT:0'], 'pool'); tt(pB, wim, er, ALU.mult, ['wim', 'E8'], ['sqT:1'], 'pool')
            tt(xo_i, pA, pB, ALU.add, ['sqT:0', 'sqT:1'], [xk_], 'pool')

        def s5_Y(o, t):
            uo = uT[:, o, :].rearrange("p (j s) -> p j s", s=8)
            uk = ['uT:%d' % o]
            Xbuf = Xbufs[o % 2]; xk_ = 'Xbuf%d' % (o % 2)
            gyo = gyT[:, o, :].rearrange("p (j s) -> p j s", s=8)
            yb = t % 2
            for s_ in range(t + 1):
                mm(pb[yb][:, 0:256], Mlag[:, o, t - s_, :], uo[:, :, s_], s_ == 0, False, ['Mlag'] + uk, [pk(yb)])
            for q in range(4):
                for ri in range(2):
                    mm(pb[yb][32 * q:32 * q + 32, 0:256], Qst32[:, 4 * o + q, t, ri, :], Xbuf[:, q, ri, 0:256], False, ri == 1,
                       ['Qst32', xk_], [pk(yb)], tp=(0, 32 * q))
            yf = ytmp[:, (t % 2) * 256:(t % 2) * 256 + 256]; yq = ytmp2[:, (t % 2) * 256:(t % 2) * 256 + 256]
            ky = 'yf%d' % (t % 2); kq = 'yq%d' % (t % 2)
            act(yf, pb[yb][:, 0:256], AF.Copy, [pk(yb)], [ky])
            tt(yq, yf, yf, ALU.mult, [ky], [kq])
            ts(yq, yq, 0.044715, 1.0, ALU.mult, ALU.add, [kq], [kq])
            tt(yq, yq, yf, ALU.mult, [ky, kq], [kq])
            act(yq, yq, AF.Sigmoid, [kq], [kq], scale=1.5957691216)
            tt(gyo[:, :, t], yf, yq, ALU.mult, [ky, kq], ['gyT:%d' % o], 'pool')
        for o in range(5):
            for jh in range(2):
                if o < 4:
                    st = s5_state(o, jh)
                if o >= 1:
                    for t in range(4 * jh, 4 * jh + 4):
                        s5_Y(o - 1, t)
                if o < 4:
                    s5_rot(o, jh, *st)

        for tb in range(4):
            cols = slice(tb * 512, (tb + 1) * 512)
            gk = ['gyT:%d' % j for j in range(4)]
            for nch in range(4):
                bank = nch % 2
                for kc in range(4):
                    mm(pb[bank], wglu[:, kc, nch * 128:(nch + 1) * 128], gyT[:, kc, cols], kc == 0, kc == 3, ['wglu'] + gk, [pk(bank)])
                sg_ = (ytmp, vre.rearrange("p q j -> p (q j)"))[nch % 2]; sgk = ('ytmp', 'vre')[nch % 2]
                act(sg_, pb[bank], AF.Sigmoid, [pk(bank), 'b_glu'], [sgk], bias=b_glu[:, nch:nch + 1])
                tt(y2T[:, nch, :], gyT[:, nch, cols], sg_, ALU.mult, gk + [sgk], ['y2T:%d' % nch])
                tt(sqT[:, nch, :], y2T[:, nch, :], y2T[:, nch, :], ALU.mult, ['y2T:%d' % nch], ['sqT:%d' % nch])
            for nch in range(4):
                mm(pb[5], ones_b, sqT[:, nch, :], nch == 0, nch == 3, ['ones_b', 'sqT:%d' % nch], [pk(5)])
            ts(ytmp2, pb[5], 1.0 / 512, EPS, ALU.mult, ALU.add, [pk(5)], ['ytmp2'])
            act(ytmp2, ytmp2, AF.Sqrt, ['ytmp2'], ['ytmp2'])
            recip(ytmp2, ytmp2, ['ytmp2'], ['ytmp2'])
            for nch in range(4):
                stt(sqT[:, nch, :], y2T[:, nch, :], g_s5[:, nch:nch + 1], ytmp2, ALU.mult, ALU.mult, ['y2T:%d' % nch, 'g_s5', 'ytmp2', 'sqT:%d' % nch], ['sqT:%d' % nch])
                dma(mix_d[b, 4 * tb:4 * tb + 4, :, 4 + nch, :].rearrange("j p t -> p j t"), sqT[:, nch, :].rearrange("p (j t) -> p j t", j=4),
                    ['sqT:%d' % nch], ['mixd:%d' % tb])

        areset()
        QT = al([68, HG, L], BF16); KT = al([68, HG, L], BF16)
        Vp = al([128, NT, HG, 65], BF16)
        fox = al([128, NT, 512], BF16)
        caug = al([128, NT, 8, 2], BF16); ncaug = al([128, NT, 8, 2], BF16)
        lfs = al([128, NT, 3, 8], BF16)
        TG = 4
        qk_sb = al([128, TG, 2 * HG, 64]); qk_sq = al([128, TG, 2 * HG, 64])
        aug = [al([128, TG, 2 * HG, 68], BF16) for _ in range(2)]
        gqk = al([128, 2 * HG, 64])
        fzA = al([128, NT, 8]); fzB = al([128, NT, 8]); fzC = al([128, NT, 8]); ones_f16 = al([128, NT])
        rsq = al([128, TG, 2 * HG]); rsq2 = al([128, TG, 2 * HG])
        mset(ones_f16, 1.0, ['ones_f16'])
        PT = [al([128, 512], BF16) for _ in range(3)]
        foxn = [al([128, 512], BF16) for _ in range(2)]; foxT = [al([128, 4, 128], BF16) for _ in range(2)]
        fss = al([128, NT]); frs = al([128, NT])
        for a in range(2):
            mset(aug[a], 1.0, ['aug%d' % a])
        for hh in range(HG):
            cp(gqk[:, hh, :], gq_bc, ['gq_bc'], ['gqk']); cp(gqk[:, HG + hh, :], gk_bc, ['gk_bc'], ['gqk'])
        pbt2 = PS[:, 2048:3072].bitcast(BF16)
        for hg in range(8 // HG):
            h0 = hg * HG
            wq = wblk[:, :, 0:HG * 64]; wk = wblk[:, :, HG * 64:2 * HG * 64]; wv = wblk[:, :, 2 * HG * 64:3 * HG * 64]; wf = wblk[:, :, 768:776]
            if hg > 0:
                load_w(wq, 'wblk', w_in[:, h0 * 64:(h0 + HG) * 64], 8, HG * 64)
                load_w(wk, 'wblk', w_in[:, 512 + h0 * 64:512 + (h0 + HG) * 64], 8, HG * 64)
                load_w(wv, 'wblk', w_in[:, 1024 + h0 * 64:1024 + (h0 + HG) * 64], 8, HG * 64)
            if hg == 0:
                load_w(wblk2, 'wblk2', w_out, 8, 1024)
            mset(Vp, 1.0, ['Vp:%d' % i for i in range(NT)])
            if hg == 0:
                ALLK = [aK(i) for i in range(NT)]
                for i in range(NT):
                    for kc in range(8):
                        mm(pb[7][:, i * 8:(i + 1) * 8], aT[:, kc, i * 128:(i + 1) * 128], wf[:, kc, :], kc == 0, kc == 7, ['wblk', aK(i)], [pk(7)])
                tt(fzA, pb[7][:, 0:NT * 8].rearrange("p (t h) -> p t h", t=NT), fb_bc.unsqueeze(1).to_broadcast([128, NT, 8]), ALU.add,
                   [pk(7), 'fb_bc'], ['fzA'])
                act(fzA, fzA, AF.Exp, ['fzA'], ['fzA'], scale=-1.0)
                act(fzA, fzA, AF.Ln, ['fzA'], ['fzA'], bias=1.0)
                cp(lfs[:, :, 0, :], fzA, ['fzA'], ['lfs'])
                tt(fzB, fzA, lfs[:, :, 0, :], ALU.subtract, ['fzA', 'lfs'], ['fzB'])
                cp(lfs[:, :, 1, :], fzB, ['fzB'], ['lfs'])
                tt(fzC, fzB, lfs[:, :, 1, :], ALU.subtract, ['fzB', 'lfs'], ['fzC'])
                cp(lfs[:, :, 2, :], fzC, ['fzC'], ['lfs'])
                mm(pb[6][:, 0:NT * 24], ones_b, lfs.rearrange("p t k h -> p (t k h)"), True, True, ['ones_b', 'lfs'], [pk(6)])
                red(fzB, pb[6][:, 0:NT * 24].rearrange("p (t k h) -> p t h k", t=NT, k=3), [pk(6)], ['fzB'])
                for h in range(8):
                    P.add('dve', lambda h=h: nc.vector.tensor_tensor_scan(out=fzC[:, :, h], data0=ones_f16, data1=fzB[:, :, h], initial=0.0,
                                                                          op0=ALU.mult, op1=ALU.add), ['fzB', 'ones_f16'], ['fzC'])
                tt(fzC, fzC, fzB, ALU.subtract, ['fzC', 'fzB'], ['fzC'])
                for part in range(3):
                    mm(pb[1][:, 0:NT * 8], tri_b, lfs[:, :, part, :], part == 0, part == 2, ['tri_b', 'lfs'], [pk(1)])
                tt(fzA, pb[1][:, 0:NT * 8].rearrange("p (t h) -> p t h", t=NT), fzC, ALU.add, [pk(1), 'fzC'], ['fzA'])
                ts(fzA, fzA, -8.0, None, ALU.mult, None, ['fzA'], ['fzA'])
                CK = ['caug:%d' % i for i in range(NT)]; NCK = ['ncaug:%d' % i for i in range(NT)]
                cp(caug[:, :, :, 0], fzA, ['fzA'], CK)
                tt(caug[:, :, :, 1], fzA, caug[:, :, :, 0], ALU.subtract, ['fzA'] + CK, CK)
                ts(ncaug, caug, -1.0, None, ALU.mult, None, CK, NCK)
            W_ = HG * 64

            PBK = (0, 1, 6, 7)

            def proj_mm(gi):
                for tl, i in enumerate(range(gi * TG, (gi + 1) * TG)):
                    tcols = slice(i * 128, (i + 1) * 128)
                    for kc in range(8):
                        mm(pb[PBK[tl]][:, 0:3 * W_], aT[:, kc, tcols], wblk[:, kc, 0:3 * W_], kc == 0, kc == 7, ['wblk', aK(i)], [pk(PBK[tl])])
            proj_mm(0)
            for gi in range(NT // TG):
                tiles = range(gi * TG, (gi + 1) * TG)
                for half, (b0, b1) in enumerate(((0, 1), (6, 7))):
                    pv_ = PS[:, b0 * 512:(b1 + 1) * 512].rearrange("p (t c) -> p t c", t=2)
                    tsl = slice(2 * half, 2 * half + 2)
                    act(qk_sb[:, tsl, 0:HG, :], pv_[:, :, 0:W_].rearrange("p t (h d) -> p t h d", h=HG), AF.Copy, [pk(b0), pk(b1)], ['qk_sb'])
                    act(qk_sb[:, tsl, HG:2 * HG, :], pv_[:, :, W_:2 * W_].rearrange("p t (h d) -> p t h d", h=HG), AF.Copy, [pk(b0), pk(b1)], ['qk_sb'])
                    act(Vp[:, gi * TG + 2 * half:gi * TG + 2 * half + 2, :, 0:64], pv_[:, :, 2 * W_:3 * W_].rearrange("p t (h d) -> p t h d", h=HG), AF.Copy,
                        [pk(b0), pk(b1)], ['Vp:%d' % i for i in tiles])
                if gi + 1 < NT // TG:
                    proj_mm(gi + 1)
                tt(qk_sq, qk_sb, qk_sb, ALU.mult, ['qk_sb'], ['qk_sq'])
                red(rsq, qk_sq, ['qk_sq'], ['rsq'])
                rsqrt_chain(rsq2, rsq, 64, ['rsq'], ['rsq2'], rsq)
                tt(qk_sq, qk_sb, rsq2.unsqueeze(3).to_broadcast([128, TG, 2 * HG, 64]), ALU.mult, ['qk_sb', 'rsq2'], ['qk_sq'])
                ag = aug[gi % 2]; agk = 'aug%d' % (gi % 2)
                tt(ag[:, :, :, 0:64], qk_sq, gqk.unsqueeze(1).to_broadcast([128, TG, 2 * HG, 64]), ALU.mult, ['qk_sq', 'gqk'], [agk])
                cp(ag[:, :, 0:HG, 64:66], caug[:, gi * TG:(gi + 1) * TG, h0:h0 + HG, :], ['caug:%d' % i for i in tiles], [agk])
                cp(ag[:, :, HG:2 * HG, 66:68], ncaug[:, gi * TG:(gi + 1) * TG, h0:h0 + HG, :], ['ncaug:%d' % i for i in tiles], [agk])
                for tl in range(TG):
                    for j in range(2 * HG):
                        blk = tl * 2 * HG + j
                        tr(pbt2[0:68, blk * 128:(blk + 1) * 128], ag[:, tl, j, :], ident_b, [agk, 'ident_b'], [pk(4), pk(5)])
                pv = pbt2[0:68, :].rearrange("p (t j x) -> p j t x", t=TG, j=2 * HG)
                gcols = slice(gi * TG * 128, (gi + 1) * TG * 128)
                cp(QT[:, :, gcols].rearrange("p h (t x) -> p h t x", t=TG), pv[:, 0:HG], [pk(4), pk(5)], ['QT:%d' % i for i in tiles])
                cp(KT[:, :, gcols].rearrange("p h (t x) -> p h t x", t=TG), pv[:, HG:2 * HG], [pk(4), pk(5)], ['KT:%d' % i for i in tiles])
            for hh in range(HG):
                h = h0 + hh
                for qg in range(4):
                    ob = 4 + (gcount[0] % 2); gcount[0] += 1
                    mset(pb[ob], 0.0, [pk(ob)])
                    its = []
                    for kt in range(4 * qg + 4):
                        q0 = max(kt, 4 * qg); N = (4 * qg + 4 - q0) * 128
                        its.append((kt, q0, N, (2, 3, 6)[itc[0] % 3], PT[itc[0] % 3], 'PT%d' % (itc[0] % 3))); itc[0] += 1

                    def qk(itm):
                        kt, q0, N, sbk, ptt, ptk = itm
                        mm(pb[sbk][:, 0:N], KT[:, hh, kt * 128:(kt + 1) * 128], QT[:, hh, q0 * 128:(4 * qg + 4) * 128], True, True,
                           ['KT:%d' % kt] + ['QT:%d' % j for j in range(q0, 4 * qg + 4)], [pk(sbk)])
                    qk(its[0])
                    if len(its) > 1: qk(its[1])
                    for n_, itm in enumerate(its):
                        kt, q0, N, sbk, ptt, ptk = itm
                        if n_ + 2 < len(its): qk(its[n_ + 2])
                        act(ptt[:, 0:N], pb[sbk][:, 0:N], AF.Exp, [pk(sbk)], [ptk], scale=0.125)
                        if kt >= 4 * qg:
                            tt(ptt[:, 0:128], ptt[:, 0:128], tri_b, ALU.mult, [ptk, 'tri_b'], [ptk])
                        for qb in range(q0, 4 * qg + 4):
                            j = qb - 4 * qg
                            mm(pb[ob][:, j * 128:j * 128 + 65], ptt[:, (qb - q0) * 128:(qb - q0 + 1) * 128], Vp[:, kt, hh, :],
                               False, kt == qb, [ptk, 'Vp:%d' % kt], [pk(ob)], skip=True)
                    ov = pb[ob].rearrange("p (j c) -> p j c", j=4)
                    recip(sm[:, 12:16], ov[:, :, 64], [pk(ob)], ['sml'])
                    tt(fox[:, 4 * qg:4 * qg + 4, h * 64:(h + 1) * 64], ov[:, :, 0:64], sm[:, 12:16].unsqueeze(2).to_broadcast([128, 4, 64]),
                       ALU.mult, [pk(ob), 'sml'], ['fox:%d' % qg])
        for i in range(NT):
            fk = 'fox:%d' % (i // 4)
            jn = foxn[i % 2]; jk = 'foxn%d' % (i % 2)
            tt(jn, fox[:, i, :], fox[:, i, :], ALU.mult, [fk], [jk])
            red(fss[:, i:i + 1], jn, [jk], ['fss'])
        rsqrt_chain(frs, fss, 512, ['fss'], ['frs'], fss)
        def fox_norm(i):
            stt(foxn[i % 2], fox[:, i, :], frs[:, i:i + 1], gfox_bc, ALU.mult, ALU.mult, ['fox:%d' % (i // 4), 'frs', 'gfox_bc'], ['foxn%d' % (i % 2)])
        fox_norm(0)
        for i in range(NT):
            fk = 'fox:%d' % (i // 4); p = i % 2
            if i + 1 < NT: fox_norm(i + 1)
            pbt = pb[4 + p].bitcast(BF16)
            for j in range(4):
                tr(pbt[:, j * 128:(j + 1) * 128], foxn[p][:, j * 128:(j + 1) * 128], ident_b, ['foxn%d' % p, 'ident_b'], [pk(4 + p)])
            cp(foxT[p], pbt[:, 0:512].rearrange("p (c t) -> p c t", c=4), [pk(4 + p)], ['foxT%d' % p])
            dma(mix_d[b, i, :, 0:4, :], foxT[p], ['foxT%d' % p], ['mixf:%d' % i])

        areset()
        mixt = [al([128, 8, 128], BF16) for _ in range(2)]
        h1t = [al([128, D]) for _ in range(2)]
        load_w(wblk, 'wblk', w_xq, 8, 1024)

        def p3_L(i):
            tcols = slice(i * 128, (i + 1) * 128)
            s = i % 2
            dma(mixt[s], mix_d[b, i], ['mixf:%d' % i, 'mixd:%d' % (i // 4)], ['mixt%d' % s])
            dma(xt[s], x[b, tcols, :], (), ['xt%d' % s])

        def p3_A(i):
            tcols = slice(i * 128, (i + 1) * 128)
            s = i % 2; mt_ = mixt[s]; mk = 'mixt%d' % s; hk_ = 'h1t%d' % s
            for hf in range(2):
                for kc in range(8):
                    mm(pb[hf], mt_[:, kc, :], wblk2[:, kc, hf * 512:(hf + 1) * 512], kc == 0, kc == 7, [mk, 'wblk2'], [pk(hf)])
                tt(h1t[s][:, hf * 512:(hf + 1) * 512], pb[hf], xt[s][:, hf * 512:(hf + 1) * 512], ALU.add, [pk(hf), 'xt%d' % s], [hk_])
            dma(h1_d[b, tcols, :], h1t[s], [hk_], ['h1d:%d' % i])

        p3_L(0)
        for step in range(NT + 1):
            if step + 1 < NT: p3_L(step + 1)
            if step < NT:
                p3_A(step)
                stats_chain(h1t[step % 2], 'h1t%d' % (step % 2), par=step % 2)
            if step >= 1:
                j = step - 1
                stats_T(aT[:, :, j * 128:(j + 1) * 128], aK(j), g_cross, 'g_cross', par=j % 2)

        areset()
        xkT = al([128, 4, 2, NMEM], BF16)
        xvp = al([128, 2, 4, 257], BF16)
        a_save = aoff[0]
        memT = al([128, 8, NMEM], BF16)
        xk_sb = al([128, 2, 256]); xk_sq = al([128, 2, 256]); xkn = al([128, 2, 256], BF16)
        mset(xvp, 1.0, ['xvp'])
        for mt in range(2):
            s = mt % 2
            dma(xt[s], mem[b, mt * 128:(mt + 1) * 128, :], (), ['xt%d' % s])
            tile_stats_T(xt[s], 'xt%d' % s, memT[:, :, mt * 128:(mt + 1) * 128], 'memT', g_mem, 'g_mem', par=s)
        for cbk in range(4):
            wkv = wblk2[:, :, (cbk % 2) * 512:(cbk % 2 + 1) * 512]; wkvk = 'wblk2h%d' % (cbk % 2)
            P.add('gq', lambda wkv=wkv, cbk=cbk: nc.gpsimd.dma_start(out=wkv, in_=w_xkv[:, cbk * 512:(cbk + 1) * 512].rearrange("(c p) n -> p c n", p=128)),
                  (), [wkvk] + (['wblk2'] if cbk < 2 else []))
            for mt in range(2):
                bank = mt
                for kc in range(8):
                    mm(pb[bank], memT[:, kc, mt * 128:(mt + 1) * 128], wkv[:, kc, :], kc == 0, kc == 7, ['memT', wkvk], [pk(bank)])
                if cbk < 2:
                    act(xk_sb, pb[bank].rearrange("p (h d) -> p h d", h=2), AF.Copy, [pk(bank)], ['xk_sb'])
                    tt(xk_sq, xk_sb, xk_sb, ALU.mult, ['xk_sb'], ['xk_sq'])
                    red(sm[:, 48:50], xk_sq, ['xk_sq'], ['smq'])
                    rsqrt_chain(sm[:, 56:58], sm[:, 48:50], 256, ['smq'], ['smr'], sm[:, 4:6])
                    tt(xk_sq, xk_sb, sm[:, 56:58].unsqueeze(2).to_broadcast([128, 2, 256]), ALU.mult, ['xk_sb', 'smr'], ['xk_sq'])
                    tt(xkn, xk_sq, gxk_bc.unsqueeze(1).to_broadcast([128, 2, 256]), ALU.mult, ['xk_sq', 'gxk_bc'], ['xkn'])
                    pbt = pb[5].bitcast(BF16)
                    for j in range(4):
                        tr(pbt[:, j * 128:(j + 1) * 128], xkn[:, j // 2, (j % 2) * 128:(j % 2 + 1) * 128], ident_b, ['xkn', 'ident_b'], [pk(5)])
                    cp(xkT[:, 2 * cbk:2 * cbk + 2, :, mt * 128:(mt + 1) * 128],
                       pbt[:, 0:512].rearrange("p (h c m) -> p h c m", h=2, c=2), [pk(5)], ['xkT'])
                else:
                    hv = 2 * (cbk - 2)
                    act(xvp[:, mt, hv:hv + 2, 0:256], pb[bank].rearrange("p (h d) -> p h d", h=2), AF.Copy, [pk(bank)], ['xvp'])
        P.add('gq', lambda: nc.gpsimd.dma_start(out=wblk2, in_=w_xo.rearrange("(c p) n -> p c n", p=128)), (), ['wblk2', 'wblk2h0', 'wblk2h1'])
        P.barrier(); aoff[0] = a_save
        xq_sb = [al([128, 4, 256]) for _ in range(2)]; xq_sq = [al([128, 4, 256]) for _ in range(2)]
        xqn = [al([128, 4, 256], BF16) for _ in range(2)]; xqT = [al([128, 8, 128], BF16) for _ in range(2)]
        PTx = [al([128, 1024], BF16) for _ in range(2)]; xo_sb = [al([128, 1024], BF16) for _ in range(2)]
        xoT = [al([128, 8, 128], BF16) for _ in range(2)]
        h1t = [al([128, D]) for _ in range(2)]
        pbt = pb[5].bitcast(BF16)

        def p4_A1(i):
            tcols = slice(i * 128, (i + 1) * 128)
            p = i % 2
            dma(xt[p], h1_d[b, tcols, :], ['h1d:%d' % i], ['xt%d' % p])
            for hf in range(2):
                for kc in range(8):
                    mm(pb[hf], aT[:, kc, tcols], wblk[:, kc, hf * 512:(hf + 1) * 512], kc == 0, kc == 7, [aK(i), 'wblk'], [pk(hf)])

        def p4_A1n(i):
            p = i % 2
            c0_ = 24 + 8 * p
            mset(sm[:, c0_:c0_ + 4], 0.0, ['smq%d' % p])
            for h in range(4):
                src = pb[h // 2][:, (h % 2) * 256:(h % 2 + 1) * 256]
                act(xq_sq[p][:, h, :], src, AF.Square, [pk(h // 2)], ['xq_sq%d' % p, 'smq%d' % p], accum=sm[:, c0_ + h:c0_ + h + 1])
            rsqrt_chain(sm[:, c0_ + 4:c0_ + 8], sm[:, c0_:c0_ + 4], 256, ['smq%d' % p], ['smr%d' % p], sm[:, c0_:c0_ + 4])
            for h in range(4):
                src = pb[h // 2][:, (h % 2) * 256:(h % 2 + 1) * 256]
                stt(xqn[p][:, h, :], src, sm[:, c0_ + 4 + h:c0_ + 5 + h], gxq_bc, ALU.mult, ALU.mult, [pk(h // 2), 'smr%d' % p, 'gxq_bc'], ['xqn%d' % p])

        def p4_A2(i):
            p = i % 2
            for j in range(8):
                tr(pbt[:, j * 128:(j + 1) * 128], xqn[p][:, j // 2, (j % 2) * 128:(j % 2 + 1) * 128], ident_b, ['xqn%d' % p, 'ident_b'], [pk(5)])
            cp(xqT[p], pbt.rearrange("p (j t) -> p j t", j=8), [pk(5)], ['xqT%d' % p])

        def p4_B1(i):
            p = i % 2
            for h in range(4):
                for mt in range(2):
                    bank = 2 + h // 2; c0_ = ((h % 2) * 2 + mt) * 128
                    for dc in range(2):
                        mm(pb[bank][:, c0_:c0_ + 128], xkT[:, h, dc, mt * 128:(mt + 1) * 128], xqT[p][:, h * 2 + dc, :], dc == 0, dc == 1,
                           ['xkT', 'xqT%d' % p], [pk(bank)])
            for hb in range(2):
                act(PTx[p][:, hb * 512:(hb + 1) * 512], pb[2 + hb], AF.Exp, [pk(2 + hb)], ['PTx%d' % p], scale=1.0 / 16)

        def p4_B2(i):
            p = i % 2
            for h in range(4):
                bank = 4 if h % 2 == 0 else 7
                for mt in range(2):
                    mm(pb[bank][:, 0:257], PTx[p][:, (h * 2 + mt) * 128:(h * 2 + mt + 1) * 128], xvp[:, mt, h, :], mt == 0, mt == 1,
                       ['PTx%d' % p, 'xvp'], [pk(bank)])
                c_ = 40 + 4 * p + h
                recip(sm[:, c_:c_ + 1], pb[bank][:, 256:257], [pk(bank)], ['sml%d' % c_])
                ts(xo_sb[p][:, h * 256:(h + 1) * 256], pb[bank][:, 0:256], sm[:, c_:c_ + 1], None, ALU.mult, None, [pk(bank), 'sml%d' % c_], ['xo_sb%d' % p])
            for j in range(8):
                tr(pbt[:, j * 128:(j + 1) * 128], xo_sb[p][:, j * 128:(j + 1) * 128], ident_b, ['xo_sb%d' % p, 'ident_b'], [pk(5)])
            cp(xoT[p], pbt.rearrange("p (j t) -> p j t", j=8), [pk(5)], ['xoT%d' % p])

        def p4_C1(i):
            tcols = slice(i * 128, (i + 1) * 128)
            p = i % 2
            for hf in range(2):
                for kc in range(8):
                    mm(pb[hf], xoT[p][:, kc, :], wblk2[:, kc, hf * 512:(hf + 1) * 512], kc == 0, kc == 7, ['xoT%d' % p, 'wblk2'], [pk(hf)])
                tt(h1t[p][:, hf * 512:(hf + 1) * 512], pb[hf], xt[p][:, hf * 512:(hf + 1) * 512], ALU.add, [pk(hf), 'xt%d' % p], ['h1t%d' % p])
            dma(h2_d[b, tcols, :], h1t[p], ['h1t%d' % p], ['h2d:%d' % i])
        for step in range(NT + 2):
            ic = step - 2; ib = step - 1; ia = step
            if 0 <= ib < NT: p4_B1(ib)
            if 0 <= ic < NT: p4_C1(ic)
            if 0 <= ib < NT: p4_B2(ib)
            if 0 <= ic < NT: stats_chain(h1t[ic % 2], 'h1t%d' % (ic % 2), par=ic % 2)
            if 0 <= ia < NT: p4_A1(ia)
            if 0 <= ia < NT: p4_A1n(ia)
            if 0 <= ic < NT: stats_T(aT[:, :, ic * 128:(ic + 1) * 128], aK(ic), g_ffn, 'g_ffn', par=ic % 2)
            if 0 <= ia < NT: p4_A2(ia)

        load_w(wblk[:, :, 0:512], 'wblk', w_ffn_up[:, 0:512], 8, 512)
        load_w(wblk[:, :, 512:1024], 'wblk', w_ffn_up[:, DFF:DFF + 512], 8, 512)
        areset()
        Gs = [al([128, 514]) for _ in range(2)]; acc = [al([128, 512]) for _ in range(2)]; sl = [al([128, 512]) for _ in range(2)]
        hidb = [al([128, 512], BF16) for _ in range(2)]
        wd = al([128, NC_FF, 512], BF16)
        hidt = [al([128, NC_FF, 256], BF16) for _ in range(2)]
        it5 = 0
        p5_pend = []

        def p5_fin():
            while p5_pend:
                k_, c_, tb_, bu__ = p5_pend.pop()
                tt(hidb[k_], sl[k_], pb[bu__], ALU.mult, ['sl%d' % k_, pk(bu__)], ['hidb%d' % k_])
                for hh_ in range(2):
                    dma(hid_d[b, 2 * tb_ + hh_, :, c_, :], hidb[k_][:, hh_ * 256:(hh_ + 1) * 256], ['hidb%d' % k_], ['hid:%d:%d' % (tb_, hh_)])
        for cg in range((NC_FF + 3) // 4):
            cbase = cg * 4; ncg = min(4, NC_FF - cbase)
            if cg == 4:
                load_w(wd, 'wd', w_ffn_down[:, 0:512], NC_FF, 512)
            wb = wblk if cg % 2 == 0 else wblk2; wkk = 'wblk' if cg % 2 == 0 else 'wblk2'
            if cg > 0:
                load_w(wb[:, :, 0:ncg * 128], wkk, w_ffn_up[:, cbase * 128:(cbase + ncg) * 128], 8, ncg * 128)
                load_w(wb[:, :, 512:512 + ncg * 128], wkk, w_ffn_up[:, DFF + cbase * 128:DFF + (cbase + ncg) * 128], 8, ncg * 128)
            for ci in range(ncg):
                c = cbase + ci
                for tb in range(4):
                    k = it5 % 2; it5 += 1
                    Gk = Gs[k]; gkk = 'Gs%d' % k; ak_ = 'acc%d' % k; sk_ = 'sl%d' % k; hk = 'hidb%d' % k
                    cols = slice(tb * 512, (tb + 1) * 512)
                    ak = [aK(4 * tb + j) for j in range(4)]
                    bg = 2 * k; bu_ = bg + 1
                    if tb == 0:
                        mset(Gk[:, 0:2], 0.0, [gkk])
                    for kc in range(8):
                        mm(pb[bg], wb[:, kc, ci * 128:(ci + 1) * 128], aT[:, kc, cols], kc == 0, kc == 7, [wkk] + ak, [pk(bg)])
                    for kc in range(8):
                        mm(pb[bu_], wb[:, kc, 512 + ci * 128:512 + (ci + 1) * 128], aT[:, kc, cols], kc == 0, kc == 7, [wkk] + ak, [pk(bu_)])
                    act(Gk[:, 2:514], pb[bg], AF.Copy, [pk(bg)], [gkk])
                    if tb < 3:
                        cp(Gs[1 - k][:, 0:2], Gk[:, 512:514], [gkk], ['Gs%d' % (1 - k)])
                    ts(acc[k], Gk[:, 2:514], cw[:, 2, c:c + 1], cb[:, c:c + 1], ALU.mult, ALU.add, [gkk, 'cw', 'cb'], [ak_])
                    stt(acc[k], Gk[:, 1:513], cw[:, 1, c:c + 1], acc[k], ALU.mult, ALU.add, [gkk, 'cw', ak_], [ak_])
                    stt(acc[k], Gk[:, 0:512], cw[:, 0, c:c + 1], acc[k], ALU.mult, ALU.add, [gkk, 'cw', ak_], [ak_])
                    act(sl[k], acc[k], AF.Silu, [ak_], [sk_])
                    p5_fin()
                    p5_pend.append((k, c, tb, bu_))

        p5_fin()
        wdA = wblk.rearrange("p a (b n) -> p (a b) n", n=512)
        wdB = wblk2.rearrange("p a (b n) -> p (a b) n", n=512)
        load_w(wdA, 'wblk', w_ffn_down[0:16 * 128, 512:1024], 16, 512)
        load_w(wdB[:, 0:6, :], 'wblk2', w_ffn_down[16 * 128:22 * 128, 512:1024], 6, 512)
        for hf in range(2):

            def p5_L(i):
                s = i % 2; hs = (i // 2) % 2
                if i % 2 == 0:
                    dma(hidt[hs], hid_d[b, i // 2], ['hid:%d:%d' % (i // 4, (i // 2) % 2)], ['hidt%d' % hs])
                dma(xt[s][:, 0:512], h2_d[b, i * 128:(i + 1) * 128, hf * 512:(hf + 1) * 512], ['h2d:%d' % i], ['xt%d' % s])
            p5_L(0)
            for i in range(NT):
                tcols = slice(i * 128, (i + 1) * 128)
                s = i % 2; hs = (i // 2) % 2
                if i + 1 < NT: p5_L(i + 1)
                for c in range(NC_FF):
                    if hf == 0: wsl = wd[:, c, :]; wk_ = 'wd'
                    elif c < 16: wsl = wdA[:, c, :]; wk_ = 'wblk'
                    else: wsl = wdB[:, c - 16, :]; wk_ = 'wblk2'
                    mm(pb[s], hidt[hs][:, c, (i % 2) * 128:(i % 2 + 1) * 128], wsl, c == 0, c == NC_FF - 1, ['hidt%d' % hs, wk_], [pk(s)])
                tt(xt[s][:, 512:1024], pb[s], xt[s][:, 0:512], ALU.add, [pk(s), 'xt%d' % s], ['ot%d' % s])
                dma(out[b, tcols, hf * 512:(hf + 1) * 512], xt[s][:, 512:1024], ['ot%d' % s], ['out:%d:%d:%d' % (b, i, hf)], q='aq')
    P.emit(es)
    return nc, es


_PARAMS = ["norm_mix", "w_in", "fox_q_norm", "fox_k_norm", "fox_f_bias", "s5_a_re", "s5_a_im", "s5_log_dt", "s5_b_re", "s5_b_im",
           "s5_c_re", "s5_c_im", "s5_d", "s5_w_glu", "s5_b_glu", "out_norm_fox", "out_norm_s5", "w_out", "norm_cross", "norm_mem",
           "w_xq", "w_xkv", "xq_norm", "xk_norm", "w_xo", "norm_ffn", "w_ffn_up", "ffn_conv_w", "ffn_conv_b", "w_ffn_down"]


def kernel(**inputs):
    nc, es = build()
    with es:
        params = {k: np.ascontiguousarray(np.asarray(inputs[k], dtype=np.float32)[0]) for k in _PARAMS}
        x = np.asarray(inputs["x"], dtype=np.float32); mem = np.asarray(inputs["mem"], dtype=np.float32)
        in_maps = []
        for c in range(8):
            m = dict(params)
            m["x"] = np.ascontiguousarray(x[NB * c:NB * (c + 1)])
            m["mem"] = np.ascontiguousarray(mem[NB * c:NB * (c + 1)])
            in_maps.append(m)
        res = run_bass_kernel_spmd(nc, in_maps, core_ids=list(range(8)))
    return np.concatenate([r["out"] for r in res.results], axis=0).astype(np.float32)
```
